# Optimizing a Trainium2 kernel written in Bass

```python
import jax
import jax.numpy as jnp
from jax import lax
import numpy as np

D_MODEL = 1024
BATCH = 4
SEQ = 8192
DEPTH = 2

CTX_LEN = 256
GRID_W = 64
MIX_WIDTH = D_MODEL
ATTN_HEAD_DIM = 64
ATTN_HEADS = MIX_WIDTH // (2 * ATTN_HEAD_DIM)
ATTN_KV_HEADS = ATTN_HEADS // 4
ATTN_GROUP = ATTN_HEADS // ATTN_KV_HEADS
ATTN_WIDTH = ATTN_HEADS * ATTN_HEAD_DIM
KV_WIDTH = ATTN_KV_HEADS * ATTN_HEAD_DIM
WINDOW = 128
BLOCK = 128
ROPE_BASE = 10000.0
RET_HEADS = 4
RET_WIDTH = MIX_WIDTH - ATTN_WIDTH
RET_V_DIM = RET_WIDTH // RET_HEADS
RET_QK_DIM = RET_V_DIM // 2
RET_QK_WIDTH = RET_HEADS * RET_QK_DIM
CHUNK = 128
IN_WIDTH = ATTN_WIDTH + 2 * KV_WIDTH + 2 * RET_QK_WIDTH + 2 * RET_WIDTH
D_FF = 256 * ((8 * D_MODEL // 3 + 255) // 256)
FFN_RESIDUAL = 0.5
N_MOD = 9
RMS_EPS = 1e-6
GN_EPS = 1e-5
NEG_INF = -1e30

kernel_name = 'hybrid_swa_retention_macaron_dit'


def rmsnorm(x, g):
    xf = x.astype(jnp.float32)
    y = xf * lax.rsqrt(jnp.mean(xf * xf, axis=-1, keepdims=True) + RMS_EPS)
    return (y * g.astype(jnp.float32)).astype(x.dtype)


def rotate(x, ang):
    f = ang.shape[-1]
    cs = jnp.cos(ang)[:, None, :].astype(x.dtype)
    sn = jnp.sin(ang)[:, None, :].astype(x.dtype)
    x1, x2 = x[..., :f], x[..., f:]
    return jnp.concatenate([x1 * cs - x2 * sn, x2 * cs + x1 * sn], axis=-1)


def axial_rope_angles(n_tokens):
    rows = n_tokens // GRID_W
    row = jnp.repeat(jnp.arange(rows, dtype=jnp.float32), GRID_W)
    col = jnp.tile(jnp.arange(GRID_W, dtype=jnp.float32), rows)
    nf = ATTN_HEAD_DIM // 4
    inv = ROPE_BASE ** (-jnp.arange(nf, dtype=jnp.float32) / nf)
    return row[:, None] * inv, col[:, None] * inv


def apply_axial(x, ang_row, ang_col):
    half = ATTN_HEAD_DIM // 2
    return jnp.concatenate([rotate(x[..., :half], ang_row), rotate(x[..., half:], ang_col)], axis=-1)


def retention_angles(n_tokens):
    t = jnp.arange(n_tokens, dtype=jnp.float32)
    inv = ROPE_BASE ** (-jnp.linspace(0.0, 1.0, RET_QK_DIM // 2, dtype=jnp.float32))
    return t[:, None] * inv


def swiglu_sublayer(h, shift, scale, gate, g_pre, g_post, wi, wo):
    u = rmsnorm(h, g_pre) * (1 + scale) + shift
    a, b = jnp.split(u @ wi, 2, axis=-1)
    y = (jax.nn.silu(a) * b) @ wo
    return h + FFN_RESIDUAL * gate * rmsnorm(y, g_post)


def project(u, w):
    b, t = u.shape[:2]
    sizes = [ATTN_WIDTH, KV_WIDTH, KV_WIDTH, RET_QK_WIDTH, RET_QK_WIDTH, RET_WIDTH, RET_WIDTH]
    qa, ka, va, qr, kr, vr, gr = jnp.split(u @ w, [int(o) for o in np.cumsum(sizes)[:-1]], axis=-1)
    qa = qa.reshape(b, t, ATTN_HEADS, ATTN_HEAD_DIM)
    ka = ka.reshape(b, t, ATTN_KV_HEADS, ATTN_HEAD_DIM)
    va = va.reshape(b, t, ATTN_KV_HEADS, ATTN_HEAD_DIM)
    qr = qr.reshape(b, t, RET_HEADS, RET_QK_DIM)
    kr = kr.reshape(b, t, RET_HEADS, RET_QK_DIM) * (RET_QK_DIM ** -0.5)
    vr = vr.reshape(b, t, RET_HEADS, RET_V_DIM)
    return qa, ka, va, qr, kr, vr, gr


def window_attention(q, k, v, k_ctx, v_ctx, sink):
    b, s = q.shape[:2]
    nb = s // BLOCK
    qb = q.reshape(b, nb, BLOCK, ATTN_KV_HEADS, ATTN_GROUP, ATTN_HEAD_DIM)

    def band(t):
        tp = jnp.pad(t, ((0, 0), (BLOCK, BLOCK), (0, 0), (0, 0)))
        tp = tp.reshape(b, nb + 2, BLOCK, ATTN_KV_HEADS, ATTN_HEAD_DIM)
        return jnp.concatenate([tp[:, :-2], tp[:, 1:-1], tp[:, 2:]], axis=2)

    kb, vb = band(k), band(v)
    qpos = jnp.arange(nb)[:, None] * BLOCK + jnp.arange(BLOCK)[None, :]
    kpos = (jnp.arange(nb)[:, None] - 1) * BLOCK + jnp.arange(3 * BLOCK)[None, :]
    rel = kpos[:, None, :] - qpos[:, :, None]
    valid = (jnp.abs(rel) <= WINDOW) & (kpos[:, None, :] >= 0) & (kpos[:, None, :] < s)
    scale = ATTN_HEAD_DIM ** -0.5
    s_loc = jnp.einsum('bnqkgd,bnskd->bnkgqs', qb, kb).astype(jnp.float32) * scale
    s_loc = jnp.where(valid[None, :, None, None], s_loc, NEG_INF)
    s_ctx = jnp.einsum('bnqkgd,blkd->bnkgql', qb, k_ctx).astype(jnp.float32) * scale
    sk = sink.astype(jnp.float32).reshape(ATTN_KV_HEADS, ATTN_GROUP)[None, None, :, :, None, None]
    m = jnp.maximum(jnp.maximum(s_loc.max(-1, keepdims=True), s_ctx.max(-1, keepdims=True)), sk)
    e_loc = jnp.exp(s_loc - m)
    e_ctx = jnp.exp(s_ctx - m)
    denom = e_loc.sum(-1, keepdims=True) + e_ctx.sum(-1, keepdims=True) + jnp.exp(sk - m)
    out = (jnp.einsum('bnkgqs,bnskd->bnqkgd', (e_loc / denom).astype(v.dtype), vb)
           + jnp.einsum('bnkgql,blkd->bnqkgd', (e_ctx / denom).astype(v.dtype), v_ctx))
    return out.reshape(b, s, ATTN_WIDTH)


def context_attention(q, k, v, sink):
    b, l = q.shape[:2]
    qg = q.reshape(b, l, ATTN_KV_HEADS, ATTN_GROUP, ATTN_HEAD_DIM)
    sc = jnp.einsum('blkgd,bmkd->bkglm', qg, k).astype(jnp.float32) * (ATTN_HEAD_DIM ** -0.5)
    sk = sink.astype(jnp.float32).reshape(ATTN_KV_HEADS, ATTN_GROUP)[None, :, :, None, None]
    m = jnp.maximum(sc.max(-1, keepdims=True), sk)
    e = jnp.exp(sc - m)
    p = e / (e.sum(-1, keepdims=True) + jnp.exp(sk - m))
    out = jnp.einsum('bkglm,bmkd->blkgd', p.astype(v.dtype), v)
    return out.reshape(b, l, ATTN_WIDTH)


def decay_tables(log_g, strict):
    idx = jnp.arange(CHUNK, dtype=jnp.float32)
    diff = idx[:, None] - idx[None, :]
    mask = diff > 0 if strict else diff >= 0
    intra = jnp.where(mask[None], jnp.exp(jnp.maximum(diff, 0.0)[None] * log_g[:, None, None]), 0.0)
    xi = jnp.exp((idx + 1.0)[None, :] * log_g[:, None])
    zeta = jnp.exp((CHUNK - 1.0 - idx)[None, :] * log_g[:, None])
    return intra, xi, zeta, jnp.exp(CHUNK * log_g)


def retention_dir(q, k, v, log_g, s0, strict):
    b, t = q.shape[:2]
    n = t // CHUNK
    qc = q.reshape(b, n, CHUNK, RET_HEADS, RET_QK_DIM)
    kc = k.reshape(b, n, CHUNK, RET_HEADS, RET_QK_DIM)
    vc = v.reshape(b, n, CHUNK, RET_HEADS, RET_V_DIM)
    intra, xi, zeta, chunk_decay = decay_tables(log_g, strict)
    scores = jnp.einsum('bnqhd,bnshd->bnhqs', qc, kc) * intra
    out_inner = jnp.einsum('bnhqs,bnshe->bnqhe', scores, vc)
    kv = jnp.einsum('bnshd,hs,bnshe->nbhde', kc, zeta, vc)

    def step(state, kv_i):
        return chunk_decay[None, :, None, None] * state + kv_i, state

    _, s_prev = lax.scan(step, s0, kv)
    out_cross = jnp.einsum('bnqhd,hq,nbhde->bnqhe', qc, xi, s_prev)
    return (out_inner + out_cross).reshape(b, t, RET_HEADS, RET_V_DIM)


def bidirectional_retention(q, k, v, log_f, log_b, s_f, s_b):
    flip = lambda a: a[:, ::-1]
    out_f = retention_dir(q, k, v, log_f, s_f, False)
    out_b = flip(retention_dir(flip(q), flip(k), flip(v), log_b, s_b, True))
    return out_f + out_b


def context_state(k, v, log_g, reverse):
    l = k.shape[1]
    idx = jnp.arange(l, dtype=jnp.float32)
    w = jnp.exp((idx if reverse else (l - 1.0 - idx))[None, :] * log_g[:, None])
    return jnp.einsum('blhd,hl,blhe->bhde', k, w, v)


def retention_output(y, g, gn_gain):
    b, t = y.shape[:2]
    yf = y.astype(jnp.float32)
    mu = yf.mean(-1, keepdims=True)
    var = ((yf - mu) ** 2).mean(-1, keepdims=True)
    yn = (yf - mu) * lax.rsqrt(var + GN_EPS) * gn_gain.astype(jnp.float32).reshape(RET_HEADS, RET_V_DIM)
    return (jax.nn.silu(g.astype(jnp.float32)) * yn.reshape(b, t, RET_WIDTH)).astype(g.dtype)


def setup_inputs(seed: int = 0) -> dict:
    key = jax.random.key(seed)
    ks = jax.random.split(key, 18)
    nrm = jax.random.normal
    f32 = jnp.float32
    base_logit = jnp.log(2.0 ** (5.0 + jnp.arange(RET_HEADS, dtype=f32)) - 1.0)
    return {
        'x': nrm(ks[0], (BATCH, SEQ, D_MODEL), f32),
        'c': nrm(ks[1], (BATCH, D_MODEL), f32),
        'ctx': nrm(ks[2], (BATCH, CTX_LEN, D_MODEL), f32),
        'c_ctx': nrm(ks[3], (D_MODEL,), f32),
        'ada_w': nrm(ks[4], (DEPTH, D_MODEL, N_MOD * D_MODEL), f32) * (0.5 * D_MODEL ** -0.5),
        'ada_b': nrm(ks[5], (DEPTH, N_MOD * D_MODEL), f32) * 0.01,
        'norm_pre': 1.0 + 0.05 * nrm(ks[6], (DEPTH, 3, D_MODEL), f32),
        'norm_post': 1.0 + 0.05 * nrm(ks[7], (DEPTH, 3, D_MODEL), f32),
        'ffn1_wi': nrm(ks[8], (DEPTH, D_MODEL, 2 * D_FF), f32) * D_MODEL ** -0.5,
        'ffn1_wo': nrm(ks[9], (DEPTH, D_FF, D_MODEL), f32) * D_FF ** -0.5,
        'ffn2_wi': nrm(ks[10], (DEPTH, D_MODEL, 2 * D_FF), f32) * D_MODEL ** -0.5,
        'ffn2_wo': nrm(ks[11], (DEPTH, D_FF, D_MODEL), f32) * D_FF ** -0.5,
        'w_in': nrm(ks[12], (DEPTH, D_MODEL, IN_WIDTH), f32) * D_MODEL ** -0.5,
        'w_out': nrm(ks[13], (DEPTH, MIX_WIDTH, D_MODEL), f32) * MIX_WIDTH ** -0.5,
        'attn_sink': 0.5 * nrm(ks[14], (DEPTH, ATTN_HEADS), f32),
        'ret_decay_fwd': base_logit + 0.05 * nrm(ks[15], (DEPTH, RET_HEADS), f32),
        'ret_decay_bwd': base_logit + 0.05 * nrm(ks[16], (DEPTH, RET_HEADS), f32),
        'ret_gn': 1.0 + 0.05 * nrm(ks[17], (DEPTH, RET_WIDTH), f32),
    }


def reference(x, c, ctx, c_ctx, ada_w, ada_b, norm_pre, norm_post, ffn1_wi, ffn1_wo, ffn2_wi, ffn2_wo,
              w_in, w_out, attn_sink, ret_decay_fwd, ret_decay_bwd, ret_gn):
    b, s = x.shape[:2]
    ang_row, ang_col = axial_rope_angles(s)
    ang_ret = retention_angles(s)
    h, hc = x, ctx
    for l in range(DEPTH):
        last = l == DEPTH - 1
        mod = (jax.nn.silu(c) @ ada_w[l] + ada_b[l]).reshape(b, N_MOD, 1, D_MODEL)
        mod_c = (jax.nn.silu(c_ctx) @ ada_w[l] + ada_b[l]).reshape(N_MOD, 1, D_MODEL)

        h = swiglu_sublayer(h, mod[:, 0], mod[:, 1], mod[:, 2], norm_pre[l, 0], norm_post[l, 0], ffn1_wi[l], ffn1_wo[l])
        hc = swiglu_sublayer(hc, mod_c[0], mod_c[1], mod_c[2], norm_pre[l, 0], norm_post[l, 0], ffn1_wi[l], ffn1_wo[l])

        u = rmsnorm(h, norm_pre[l, 1]) * (1 + mod[:, 4]) + mod[:, 3]
        uc = rmsnorm(hc, norm_pre[l, 1]) * (1 + mod_c[4]) + mod_c[3]
        qa, ka, va, qr, kr, vr, gr = project(u, w_in[l])
        qac, kac, vac, qrc, krc, vrc, grc = project(uc, w_in[l])
        log_f = jax.nn.log_sigmoid(ret_decay_fwd[l].astype(jnp.float32))
        log_b = jax.nn.log_sigmoid(ret_decay_bwd[l].astype(jnp.float32))
        s_f = context_state(krc, vrc, log_f, reverse=False)
        s_b = context_state(krc, vrc, log_b, reverse=True)

        attn = window_attention(apply_axial(qa, ang_row, ang_col), apply_axial(ka, ang_row, ang_col), va,
                                kac, vac, attn_sink[l])
        ret = retention_output(bidirectional_retention(rotate(qr, ang_ret), rotate(kr, ang_ret), vr,
                                                       log_f, log_b, s_f, s_b), gr, ret_gn[l])
        y = jnp.concatenate([attn, ret], axis=-1) @ w_out[l]
        h = h + mod[:, 5] * rmsnorm(y, norm_post[l, 1])

        if not last:
            zero_state = jnp.zeros((hc.shape[0], RET_HEADS, RET_QK_DIM, RET_V_DIM), jnp.float32)
            attn_c = context_attention(qac, kac, vac, attn_sink[l])
            ret_c = retention_output(bidirectional_retention(qrc, krc, vrc, log_f, log_b, zero_state, zero_state),
                                     grc, ret_gn[l])
            yc = jnp.concatenate([attn_c, ret_c], axis=-1) @ w_out[l]
            hc = hc + mod_c[5] * rmsnorm(yc, norm_post[l, 1])
            hc = swiglu_sublayer(hc, mod_c[6], mod_c[7], mod_c[8], norm_pre[l, 2], norm_post[l, 2], ffn2_wi[l], ffn2_wo[l])

        h = swiglu_sublayer(h, mod[:, 6], mod[:, 7], mod[:, 8], norm_pre[l, 2], norm_post[l, 2], ffn2_wi[l], ffn2_wo[l])
    return h
```

```python
import numpy as np
import concourse.bass as bass
import concourse.mybir as mybir
from concourse.bass_utils import run_bass_kernel_spmd

F32 = mybir.dt.float32
BF16 = mybir.dt.bfloat16
AF = mybir.ActivationFunctionType
ALU = mybir.AluOpType
AX = mybir.AxisListType

D = 1024
DFF = 2816
NCORES = 8


class Buf:
    __slots__ = ("name", "w", "rs", "dsem", "dcount")

    def __init__(self, name):
        self.name = name
        self.w = None
        self.rs = []
        self.dsem = None
        self.dcount = 0


class Op:
    __slots__ = ("eng", "seq", "fn", "waits", "sig", "val", "dma", "snap", "key", "pos")


ENGS = ("pe", "act", "dve", "pool", "sp")


class Prog:
    def __init__(self, nc):
        self.nc = nc
        self.ops = {e: [] for e in ENGS}
        self.known = {e: {} for e in ENGS}
        self.snapver = {e: None for e in ENGS}
        self.dma_bufs = []
        self.nbuf = 0

    def buf(self, name=None):
        self.nbuf += 1
        return Buf(name or f"b{self.nbuf}")

    def bufs(self, n, name="b"):
        return [self.buf(f"{name}{i}") for i in range(n)]

    def _record(self, eng, fn, reads, writes, dma_buf=None):
        op = Op()
        op.eng = eng
        op.fn = fn
        op.sig = False
        op.val = None
        op.dma = dma_buf
        op.waits = []
        known = self.known[eng]
        deps = []
        for b in reads:
            if b.w is not None:
                deps.append((b.w, "raw"))
        for b in writes:
            if b.w is not None:
                deps.append((b.w, "waw"))
            for r in b.rs:
                deps.append((r, "war"))
        changed = False
        for d, kind in deps:
            if d.dma is None and d.eng == eng:
                if eng in ("pe", "sp") or kind == "war":
                    continue
            if known.get(d.key, -1) >= d.pos:
                continue
            op.waits.append(d)
            d.sig = True
            known[d.key] = d.pos
            for k, v in d.snap.items():
                if known.get(k, -1) < v:
                    known[k] = v
            changed = True
        if changed or self.snapver[eng] is None:
            self.snapver[eng] = dict(known)
        op.snap = self.snapver[eng]
        lst = self.ops[eng]
        op.seq = len(lst)
        lst.append(op)
        if dma_buf is not None:
            if dma_buf.dsem is None:
                dma_buf.dsem = ("dma", len(self.dma_bufs))
                self.dma_bufs.append(dma_buf)
            dma_buf.dcount += 1
            op.key = dma_buf.dsem
            op.pos = dma_buf.dcount
        else:
            op.key = eng
            op.pos = op.seq
        for b in reads:
            b.rs.append(op)
        for b in writes:
            b.w = op
            b.rs = []
        return op

    def op(self, eng, fn, reads=(), writes=()):
        return self._record(eng, fn, list(reads), list(writes))

    def dma(self, queue, fn, sb, reads=(), writes=()):
        return self._record(queue, fn, list(reads), list(writes), dma_buf=sb)

    def finish(self, bufs):
        self._record("sp", None, list(bufs), [])

    def emit(self):
        nc = self.nc
        for e in ENGS:
            c = 0
            for op in self.ops[e]:
                if op.dma is None and op.sig:
                    c += 1
                    op.val = c
        esem = {e: nc.alloc_semaphore(name=f"s_{e}") for e in ENGS}
        dsem = [nc.alloc_semaphore(name=f"d_{i}") for i in range(len(self.dma_bufs))]

        def run(e, eng):
            for op in self.ops[e]:
                for d in op.waits:
                    if d.dma is not None:
                        eng.wait_ge(dsem[d.key[1]], 16 * d.pos)
                    else:
                        eng.wait_ge(esem[d.eng], d.val)
                if op.fn is None:
                    continue
                ins = op.fn(eng)
                if op.dma is not None:
                    ins.then_inc(dsem[op.key[1]], 16)
                elif op.sig:
                    ins.then_inc(esem[e], 1)

        with nc.Block() as block:
            @block.tensor
            def _(eng):
                run("pe", eng)

            @block.scalar
            def _(eng):
                run("act", eng)

            @block.vector
            def _(eng):
                run("dve", eng)

            @block.gpsimd
            def _(eng):
                run("pool", eng)

            @block.sync
            def _(eng):
                run("sp", eng)


class Ctx:
    def __init__(self):
        self.nc = bass.Bass("TRN2", target_bir_lowering=False)
        self.P = Prog(self.nc)
        self.n = 0
        self.outs = []

    def sb(self, shape, dt, name=None):
        self.n += 1
        return self.nc.alloc_sbuf_tensor(f"sb_{name or self.n}", list(shape), dt).ap()

    def ps(self, shape, dt, name=None):
        self.n += 1
        return self.nc.alloc_psum_tensor(f"ps_{name or self.n}", list(shape), dt).ap()

    def din(self, name, shape, dt=F32):
        return self.nc.dram_tensor(name, list(shape), dt, kind="ExternalInput").ap()

    def dout(self, name, shape, dt=F32):
        return self.nc.dram_tensor(name, list(shape), dt, kind="ExternalOutput").ap()

    def identity(self):
        P = self.P
        identf = self.sb([128, 128], F32)
        ident = self.sb([128, 128], BF16)
        b = P.buf("ident")
        P.op("pool", lambda e: e.memset(identf[:, :], 0.0), writes=[b])
        P.op("pool", lambda e: e.affine_select(out=identf[:, :], in_=identf[:, :], pattern=[[-1, 128]],
                                                compare_op=ALU.not_equal, fill=1.0, base=0,
                                                channel_multiplier=1), reads=[b], writes=[b])
        P.op("dve", lambda e: e.tensor_copy(out=ident[:, :], in_=identf[:, :]), reads=[b], writes=[b])
        return ident, b

    def consts(self):
        P = self.P
        c = self.sb([128, 2], F32)
        b = P.buf("consts")
        P.op("pool", lambda e: e.memset(c[:, 0:1], -0.5), writes=[b])
        return c, b


def rstd_from_ss(cx, ss_ap, out_ap, bss, bout, cst, bcst, eps, tmp_ap):
    P = cx.P
    P.op("dve", lambda e: e.tensor_scalar_add(out=tmp_ap, in0=ss_ap, scalar1=eps), reads=[bss], writes=[bout])
    n = ss_ap.shape[1]
    P.op("pool", lambda e: e.tensor_tensor(out=out_ap, in0=tmp_ap, in1=cst[:, 0:1].to_broadcast([128, n]), op=ALU.pow),
         reads=[bout, bcst], writes=[bout])


def build_ffn(tile_sets):
    cx = Ctx()
    nc, P = cx.nc, cx.P
    nt = len(tile_sets)
    T = nt * 128
    nset = max(tile_sets) + 1
    x = cx.din("x", [T, D])
    wi = cx.din("wi", [D, 2 * DFF])
    wo = cx.din("wo", [DFF, D])
    abT = cx.din("abT", [nset, 128, 8, 2])
    Gd = cx.din("G", [nset, D])
    y = cx.dout("y", [T, D])

    wib = cx.sb([128, 8, 2 * DFF], BF16, "wib")
    wob = cx.sb([128, 22, D], BF16, "wob")
    ab = cx.sb([128, nset, 8, 2], F32, "ab")
    G = cx.sb([128, nset, D], F32, "G")
    hb = [cx.sb([128, D], F32, f"hb{i}") for i in range(4)]
    xn = [cx.sb([128, D], BF16, f"xn{i}") for i in range(2)]
    junk = cx.sb([128, D], BF16, "junk")
    t1 = cx.sb([128, D], F32, "t1")
    uT = [cx.sb([128, 8, 256], BF16, f"uT{i}") for i in range(2)]
    gT = cx.sb([128, 22, 256], BF16, "gT")
    sil = [cx.sb([128, 256], F32, f"sil{i}") for i in range(2)]
    st = [cx.sb([128, 8], F32, f"st{i}") for i in range(4)]
    pT = cx.ps([128, 1024], BF16, "pT")
    pab = [cx.ps([128, 512], F32, f"pab{i}") for i in range(4)]
    pY = [cx.ps([128, 512], F32, f"pY{i}") for i in range(3)]

    Bwib, Bwob, Bab, BG = P.buf("wib"), P.buf("wob"), P.buf("ab"), P.buf("G")
    Bhb, Bxn, Bt1, Bsil, Bst = P.bufs(4, "hb"), P.bufs(2, "xn"), P.buf("t1"), P.bufs(2, "sil"), P.bufs(4, "st")
    Bjunk, BuT, BgT, BpT = P.buf("junk"), P.bufs(2, "uT"), P.buf("gT"), P.buf("pT")
    Bpab, BpY = P.bufs(4, "pab"), P.bufs(3, "pY")

    ident, Bid = cx.identity()
    cst, Bc = cx.consts()

    for s in range(nset):
        P.dma("sp", lambda e, s=s: e.dma_start(out=ab[:, s, :, :], in_=abT[s]), Bab, writes=[Bab])
        P.dma("sp", lambda e, s=s: e.dma_start(out=G[:, s, :], in_=Gd[s:s + 1, :].partition_broadcast(128)), BG, writes=[BG])
    for k in range(8):
        P.dma("pool", lambda e, k=k: e.dma_start(out=wib[:, k, :], in_=wi[k * 128:(k + 1) * 128, :]), Bwib, writes=[Bwib])
    wo_v = wo.rearrange("(k p) n -> p k n", p=128)
    for k0 in range(0, 22, 6):
        k1 = min(22, k0 + 6)
        P.dma("pool", lambda e, k0=k0, k1=k1: e.dma_start(out=wob[:, k0:k1, :], in_=wo_v[:, k0:k1, :]), Bwob, writes=[Bwob])

    blocks = [list(range(b0, min(nt, b0 + 2))) for b0 in range(0, nt, 2)]
    Bouts = []
    yctr = [0]

    def phase1(bi):
        tiles = blocks[bi]
        par = bi % 2
        for j, t in enumerate(tiles):
            s = tile_sets[t]
            h, Bh = hb[par * 2 + j], Bhb[par * 2 + j]
            P.dma("sp", lambda e, h=h, t=t: e.dma_start(out=h[:, :], in_=x[t * 128:(t + 1) * 128, :]), Bh, writes=[Bh])
            P.op("act", lambda e, h=h, j=j: e.activation(out=junk[:, :], in_=h[:, :], func=AF.Square, scale=1.0 / 32,
                                                        accum_out=st[j][:, 0:1]), reads=[Bh], writes=[Bjunk, Bst[j]])
            rstd_from_ss(cx, st[j][:, 0:1], st[j][:, 2:3], Bst[j], Bst[j], cst, Bc, 1e-6, st[j][:, 1:2])
            P.op("dve", lambda e, h=h, j=j: e.tensor_scalar(out=xn[j][:, :], in0=h[:, :], scalar1=st[j][:, 2:3], scalar2=None,
                                                           op0=ALU.mult), reads=[Bh, Bst[j]], writes=[Bxn[j]])
            for k in range(8):
                P.op("pe", lambda e, j=j, k=k: e.transpose(out=pT[:, k * 128:(k + 1) * 128], in_=xn[j][:, k * 128:(k + 1) * 128],
                                                          identity=ident[:, :]), reads=[Bxn[j], Bid], writes=[BpT])
            for k in range(8):
                if k % 2 == 0:
                    P.op("act", lambda e, j=j, k=k, s=s: e.activation(out=uT[par][:, k, j * 128:(j + 1) * 128], in_=pT[:, k * 128:(k + 1) * 128],
                                                                     func=AF.Identity, scale=ab[:, s, k, 0:1], bias=ab[:, s, k, 1:2]),
                         reads=[BpT, Bab], writes=[BuT[par]])
                else:
                    P.op("dve", lambda e, j=j, k=k, s=s: e.tensor_scalar(out=uT[par][:, k, j * 128:(j + 1) * 128], in0=pT[:, k * 128:(k + 1) * 128],
                                                                        scalar1=ab[:, s, k, 0:1], scalar2=ab[:, s, k, 1:2],
                                                                        op0=ALU.mult, op1=ALU.add), reads=[BpT, Bab], writes=[BuT[par]])

    def phase2(bi):
        tiles = blocks[bi]
        par = bi % 2
        N = len(tiles) * 128
        u = uT[par]
        for m in range(22):
            pa, pb = pab[(m % 2) * 2], pab[(m % 2) * 2 + 1]
            Ba, Bb = Bpab[(m % 2) * 2], Bpab[(m % 2) * 2 + 1]
            for k in range(8):
                P.op("pe", lambda e, k=k, m=m, pa=pa: e.matmul(pa[:, 0:N], lhsT=wib[:, k, m * 128:(m + 1) * 128], rhs=u[:, k, 0:N],
                                                             start=(k == 0), stop=(k == 7)), reads=[Bwib, BuT[par]], writes=[Ba])
            for k in range(8):
                P.op("pe", lambda e, k=k, m=m, pb=pb: e.matmul(pb[:, 0:N], lhsT=wib[:, k, DFF + m * 128:DFF + (m + 1) * 128], rhs=u[:, k, 0:N],
                                                             start=(k == 0), stop=(k == 7)), reads=[Bwib, BuT[par]], writes=[Bb])
            sl, Bs = sil[m % 2], Bsil[m % 2]
            P.op("act", lambda e, pa=pa, sl=sl: e.activation(out=sl[:, 0:N], in_=pa[:, 0:N], func=AF.Silu), reads=[Ba], writes=[Bs])
            P.op("dve", lambda e, pb=pb, sl=sl, m=m: e.tensor_tensor(out=gT[:, m, 0:N], in0=sl[:, 0:N], in1=pb[:, 0:N], op=ALU.mult),
                 reads=[Bs, Bb], writes=[BgT])

    def phase3(bi):
        tiles = blocks[bi]
        par = bi % 2
        for j, t in enumerate(tiles):
            s = tile_sets[t]
            h, Bh = hb[par * 2 + j], Bhb[par * 2 + j]
            s3, Bs3 = st[2 + j], Bst[2 + j]
            pys = []
            for half in range(2):
                py, Bp = pY[yctr[0] % 3], BpY[yctr[0] % 3]
                yctr[0] += 1
                pys.append((py, Bp))
                for k in range(22):
                    P.op("pe", lambda e, k=k, j=j, half=half, py=py: e.matmul(py[:, :], lhsT=gT[:, k, j * 128:(j + 1) * 128],
                                                                            rhs=wob[:, k, half * 512:(half + 1) * 512],
                                                                            start=(k == 0), stop=(k == 21)), reads=[BgT, Bwob], writes=[Bp])
                P.op("act", lambda e, py=py, s3=s3, half=half: e.activation(out=junk[:, 0:512], in_=py[:, :], func=AF.Square, scale=1.0 / 32,
                                                                          accum_out=s3[:, 4 + half:5 + half]),
                     reads=[Bp], writes=[Bjunk, Bs3])
            P.op("dve", lambda e, s3=s3: e.tensor_tensor(out=s3[:, 6:7], in0=s3[:, 4:5], in1=s3[:, 5:6], op=ALU.add),
                 reads=[Bs3], writes=[Bs3])
            rstd_from_ss(cx, s3[:, 6:7], s3[:, 7:8], Bs3, Bs3, cst, Bc, 1e-6, s3[:, 3:4])
            for half in range(2):
                py, Bp = pys[half]
                P.op("dve", lambda e, py=py, s3=s3, half=half, s=s: e.scalar_tensor_tensor(
                    out=t1[:, half * 512:(half + 1) * 512], in0=py[:, :], scalar=s3[:, 7:8],
                    in1=G[:, s, half * 512:(half + 1) * 512], op0=ALU.mult, op1=ALU.mult), reads=[Bp, Bs3, BG], writes=[Bt1])
            P.op("pool", lambda e, h=h: e.tensor_tensor(out=h[:, :], in0=h[:, :], in1=t1[:, :], op=ALU.add),
                 reads=[Bh, Bt1], writes=[Bh])
            Bo = P.buf()
            Bouts.append(Bo)
            P.dma("sp", lambda e, h=h, t=t: e.dma_start(out=y[t * 128:(t + 1) * 128, :], in_=h[:, :]), Bh, reads=[Bh], writes=[Bo])

    nb = len(blocks)
    phase1(0)
    for bi in range(nb):
        phase2(bi)
        if bi + 1 < nb:
            phase1(bi + 1)
        phase3(bi)
    P.finish(Bouts)
    P.emit()
    return nc


def build_mod():
    cx = Ctx()
    nc, P = cx.nc, cx.P
    cT = cx.din("cT", [128, 8, 2])
    aw = cx.din("ada_w", [D, 9 * D])
    abias = cx.din("ada_b", [1, 9 * D])
    gpre = cx.din("gpre", [1, 3 * D])
    gpost = cx.din("gpost", [1, 3 * D])
    modv = cx.dout("modv", [2, 9, D])
    GW = 1536
    cs = cx.sb([128, 8, 2], F32, "cs")
    sc = cx.sb([128, 8, 2], BF16, "sc")
    wch = [cx.sb([128, 8, GW], BF16, f"wch{i}") for i in range(2)]
    mod = cx.sb([2, 9 * D], F32, "mod")
    bia = cx.sb([2, 9 * D], F32, "bia")
    gp = cx.sb([2, 3 * D], F32, "gp")
    gq = cx.sb([2, 3 * D], F32, "gq")
    outv = cx.sb([2, 9, D], F32, "outv")
    pm = [cx.ps([128, 512], F32, f"pm{i}") for i in range(2)]
    Bcs, Bsc, Bmod, Bbia, Bgp, Bgq, Bout = [P.buf() for _ in range(7)]
    Bw, Bpm = P.bufs(2, "w"), P.bufs(2, "pm")
    P.dma("sp", lambda e: e.dma_start(out=cs[:, :, :], in_=cT[:, :, :]), Bcs, writes=[Bcs])
    P.dma("sp", lambda e: e.dma_start(out=bia[:, :], in_=abias.partition_broadcast(2)), Bbia, writes=[Bbia])
    P.dma("sp", lambda e: e.dma_start(out=gp[:, :], in_=gpre.partition_broadcast(2)), Bgp, writes=[Bgp])
    P.dma("sp", lambda e: e.dma_start(out=gq[:, :], in_=gpost.partition_broadcast(2)), Bgq, writes=[Bgq])
    P.op("act", lambda e: e.activation(out=sc[:, :, :], in_=cs[:, :, :], func=AF.Silu), reads=[Bcs], writes=[Bsc])
    ci = 0
    for g in range(9 * D // GW):
        w, Bwg = wch[g % 2], Bw[g % 2]
        for k in range(8):
            P.dma("pool", lambda e, w=w, k=k, g=g: e.dma_start(out=w[:, k, :], in_=aw[k * 128:(k + 1) * 128, g * GW:(g + 1) * GW]),
                  Bwg, writes=[Bwg])
        for n in range(GW // 512):
            p, Bp = pm[ci % 2], Bpm[ci % 2]
            ci += 1
            for k in range(8):
                P.op("pe", lambda e, w=w, k=k, n=n, p=p: e.matmul(p[0:2, :], lhsT=sc[:, k, :], rhs=w[:, k, n * 512:(n + 1) * 512],
                                                                start=(k == 0), stop=(k == 7)), reads=[Bsc, Bwg], writes=[Bp])
            c0 = g * GW + n * 512
            P.op("dve", lambda e, p=p, c0=c0: e.tensor_tensor(out=mod[:, c0:c0 + 512], in0=p[0:2, :], in1=bia[:, c0:c0 + 512], op=ALU.add),
                 reads=[Bp, Bbia], writes=[Bmod])
    for s in range(3):
        coef = 1.0 if s == 1 else 0.5
        sh, scl, gt = mod[:, (3 * s) * D:(3 * s + 1) * D], mod[:, (3 * s + 1) * D:(3 * s + 2) * D], mod[:, (3 * s + 2) * D:(3 * s + 3) * D]
        P.op("dve", lambda e, s=s, scl=scl: e.scalar_tensor_tensor(out=outv[:, 3 * s, :], in0=scl, scalar=1.0, in1=gp[:, s * D:(s + 1) * D],
                                                                  op0=ALU.add, op1=ALU.mult), reads=[Bmod, Bgp], writes=[Bout])
        P.op("dve", lambda e, s=s, sh=sh: e.tensor_copy(out=outv[:, 3 * s + 1, :], in_=sh), reads=[Bmod], writes=[Bout])
        P.op("dve", lambda e, s=s, gt=gt, coef=coef: e.scalar_tensor_tensor(out=outv[:, 3 * s + 2, :], in0=gt, scalar=coef,
                                                                          in1=gq[:, s * D:(s + 1) * D], op0=ALU.mult, op1=ALU.mult),
             reads=[Bmod, Bgq], writes=[Bout])
    Bo = P.buf()
    P.dma("sp", lambda e: e.dma_start(out=modv[:, :, :], in_=outv[:, :, :]), Bout, reads=[Bout], writes=[Bo])
    P.finish([Bo])
    P.emit()
    return nc


NFM = 9
NCX = 18 * 128 + 1152
PAIR_TAB = [0, 0, 0, 0, 0, 1, 1, 2, 2]


def build_s2(tile_sets):
    cx = Ctx()
    nc, P = cx.nc, cx.P
    nt = len(tile_sets)
    T = nt * 128
    nset = max(tile_sets) + 1
    x = cx.din("x", [T, D])
    wext = cx.din("wext", [D, NCX])
    abT = cx.din("abT", [nset, 128, 8, 2])
    tab = cx.din("tab", [6, 128, T])
    gn = cx.din("gn", [1, 512])
    FM = cx.dout("FM", [NFM, 128, T], BF16)
    VA = cx.dout("VA", [T, 128], BF16)
    VR = cx.dout("VR", [T, 512], BF16)
    GG = cx.dout("GG", [T, 512], F32)

    wb = cx.sb([128, 8, NCX], BF16, "wb")
    ab = cx.sb([128, nset, 8, 2], F32, "ab")
    gnb = cx.sb([128, 512], F32, "gnb")
    hb = [cx.sb([128, D], F32, f"hb{i}") for i in range(4)]
    xn = [cx.sb([128, D], BF16, f"xn{i}") for i in range(2)]
    junk = cx.sb([128, D], BF16, "junk")
    uT = [cx.sb([128, 8, 256], BF16, f"uT{i}") for i in range(2)]
    tb = [cx.sb([128, 6, 256], F32, f"tb{i}") for i in range(2)]
    r1 = [cx.sb([128, 256], F32, f"r1{i}") for i in range(2)]
    r2 = [cx.sb([128, 256], F32, f"r2{i}") for i in range(2)]
    fmo = [cx.sb([128, 256], BF16, f"fmo{i}") for i in range(3)]
    vao = [cx.sb([128, 128], BF16, f"vao{i}") for i in range(2)]
    vro = [cx.sb([128, 512], BF16, f"vro{i}") for i in range(2)]
    gs = [cx.sb([128, 512], F32, f"gs{i}") for i in range(2)]
    ggo = [cx.sb([128, 512], F32, f"ggo{i}") for i in range(2)]
    st = [cx.sb([128, 8], F32, f"st{i}") for i in range(2)]
    pT = cx.ps([128, 1024], BF16, "pT")
    pxp = [cx.ps([128, 512], F32, f"pxp{i}") for i in range(4)]
    ptm = [cx.ps([128, 512], F32, f"ptm{i}") for i in range(3)]

    Bwb, Bab, Bgn = P.buf(), P.buf(), P.buf()
    Bhb, Bxn, BuT, Btb = P.bufs(4), P.bufs(2), P.bufs(2), P.bufs(2)
    Br1, Br2, Bfmo, Bvao, Bvro, Bgs, Bggo, Bst = P.bufs(2), P.bufs(2), P.bufs(3), P.bufs(2), P.bufs(2), P.bufs(2), P.bufs(2), P.bufs(2)
    Bjunk, BpT = P.buf(), P.buf()
    Bpxp, Bptm = P.bufs(4), P.bufs(3)
    ident, Bid = cx.identity()
    cst, Bc = cx.consts()
    Bouts = []

    for s in range(nset):
        P.dma("sp", lambda e, s=s: e.dma_start(out=ab[:, s, :, :], in_=abT[s]), Bab, writes=[Bab])
    P.dma("sp", lambda e: e.dma_start(out=gnb[:, :], in_=gn.partition_broadcast(128)), Bgn, writes=[Bgn])
    for k in range(8):
        P.dma("pool", lambda e, k=k: e.dma_start(out=wb[:, k, :], in_=wext[k * 128:(k + 1) * 128, :]), Bwb, writes=[Bwb])

    blocks = [list(range(b0, min(nt, b0 + 2))) for b0 in range(0, nt, 2)]
    ctr = {"fm": 0, "tm": 0, "o": 0}

    def phase1(bi):
        tiles = blocks[bi]
        par = bi % 2
        for j, t in enumerate(tiles):
            s = tile_sets[t]
            h, Bh = hb[par * 2 + j], Bhb[par * 2 + j]
            P.dma("sp", lambda e, h=h, t=t: e.dma_start(out=h[:, :], in_=x[t * 128:(t + 1) * 128, :]), Bh, writes=[Bh])
            P.op("act", lambda e, h=h, j=j: e.activation(out=junk[:, :], in_=h[:, :], func=AF.Square, scale=1.0 / 32,
                                                        accum_out=st[j][:, 0:1]), reads=[Bh], writes=[Bjunk, Bst[j]])
            rstd_from_ss(cx, st[j][:, 0:1], st[j][:, 2:3], Bst[j], Bst[j], cst, Bc, 1e-6, st[j][:, 1:2])
            P.op("dve", lambda e, h=h, j=j: e.tensor_scalar(out=xn[j][:, :], in0=h[:, :], scalar1=st[j][:, 2:3], scalar2=None,
                                                           op0=ALU.mult), reads=[Bh, Bst[j]], writes=[Bxn[j]])
            for k in range(8):
                P.op("pe", lambda e, j=j, k=k: e.transpose(out=pT[:, k * 128:(k + 1) * 128], in_=xn[j][:, k * 128:(k + 1) * 128],
                                                          identity=ident[:, :]), reads=[Bxn[j], Bid], writes=[BpT])
            for k in range(8):
                if k % 2 == 0:
                    P.op("act", lambda e, j=j, k=k, s=s: e.activation(out=uT[par][:, k, j * 128:(j + 1) * 128], in_=pT[:, k * 128:(k + 1) * 128],
                                                                     func=AF.Identity, scale=ab[:, s, k, 0:1], bias=ab[:, s, k, 1:2]),
                         reads=[BpT, Bab], writes=[BuT[par]])
                else:
                    P.op("dve", lambda e, j=j, k=k, s=s: e.tensor_scalar(out=uT[par][:, k, j * 128:(j + 1) * 128], in0=pT[:, k * 128:(k + 1) * 128],
                                                                        scalar1=ab[:, s, k, 0:1], scalar2=ab[:, s, k, 1:2],
                                                                        op0=ALU.mult, op1=ALU.add), reads=[BpT, Bab], writes=[BuT[par]])
        t0 = tiles[0] * 128
        N = len(tiles) * 128
        P.dma("sp", lambda e: e.dma_start(out=tb[par][:, :, 0:N], in_=tab[:, :, t0:t0 + N].rearrange("s p t -> p s t")),
              Btb[par], writes=[Btb[par]])

    def phase2(bi):
        tiles = blocks[bi]
        par = bi % 2
        N = len(tiles) * 128
        t0 = tiles[0] * 128
        u = uT[par]
        for i in range(NFM):
            q = ctr["fm"] % 2
            ctr["fm"] += 1
            px, pp = pxp[q * 2], pxp[q * 2 + 1]
            Bx, Bp = Bpxp[q * 2], Bpxp[q * 2 + 1]
            for k in range(8):
                P.op("pe", lambda e, k=k, i=i, px=px: e.matmul(px[:, 0:N], lhsT=wb[:, k, (2 * i) * 128:(2 * i + 1) * 128], rhs=u[:, k, 0:N],
                                                             start=(k == 0), stop=(k == 7)), reads=[Bwb, BuT[par]], writes=[Bx])
            for k in range(8):
                P.op("pe", lambda e, k=k, i=i, pp=pp: e.matmul(pp[:, 0:N], lhsT=wb[:, k, (2 * i + 1) * 128:(2 * i + 2) * 128], rhs=u[:, k, 0:N],
                                                             start=(k == 0), stop=(k == 7)), reads=[Bwb, BuT[par]], writes=[Bp])
            tp = PAIR_TAB[i]
            P.op("dve", lambda e, px=px, q=q, tp=tp: e.tensor_tensor(out=r1[q][:, 0:N], in0=px[:, 0:N], in1=tb[par][:, 2 * tp, 0:N], op=ALU.mult),
                 reads=[Bx, Btb[par]], writes=[Br1[q]])
            P.op("dve", lambda e, pp=pp, q=q, tp=tp: e.tensor_tensor(out=r2[q][:, 0:N], in0=pp[:, 0:N], in1=tb[par][:, 2 * tp + 1, 0:N], op=ALU.mult),
                 reads=[Bp, Btb[par]], writes=[Br2[q]])
            o = ctr["o"] % 3
            ctr["o"] += 1
            P.op("pool", lambda e, q=q, o=o: e.tensor_tensor(out=fmo[o][:, 0:N], in0=r1[q][:, 0:N], in1=r2[q][:, 0:N], op=ALU.add),
                 reads=[Br1[q], Br2[q]], writes=[Bfmo[o]])
            Bo = P.buf()
            Bouts.append(Bo)
            P.dma("sp", lambda e, o=o, i=i: e.dma_start(out=FM[i, :, t0:t0 + N], in_=fmo[o][:, 0:N]), Bfmo[o], reads=[Bfmo[o]], writes=[Bo])
        c0 = 18 * 128
        for j, t in enumerate(tiles):
            def tm(cols, ncol):
                q = ctr["tm"] % 3
                ctr["tm"] += 1
                p, Bp_ = ptm[q], Bptm[q]
                for k in range(8):
                    P.op("pe", lambda e, k=k, p=p, j=j: e.matmul(p[:, 0:ncol], lhsT=u[:, k, j * 128:(j + 1) * 128], rhs=wb[:, k, cols:cols + ncol],
                                                          start=(k == 0), stop=(k == 7)), reads=[Bwb, BuT[par]], writes=[Bp_])
                return p, Bp_
            r0 = t * 128
            p, Bp_ = tm(c0, 128)
            P.op("act", lambda e, p=p, j=j: e.activation(out=vao[j][:, :], in_=p[:, 0:128], func=AF.Copy), reads=[Bp_], writes=[Bvao[j]])
            Bo = P.buf(); Bouts.append(Bo)
            P.dma("sp", lambda e, j=j, r0=r0: e.dma_start(out=VA[r0:r0 + 128, :], in_=vao[j][:, :]), Bvao[j], reads=[Bvao[j]], writes=[Bo])
            p, Bp_ = tm(c0 + 128, 512)
            P.op("act", lambda e, p=p, j=j: e.activation(out=vro[j][:, :], in_=p[:, :], func=AF.Copy), reads=[Bp_], writes=[Bvro[j]])
            Bo = P.buf(); Bouts.append(Bo)
            P.dma("sp", lambda e, j=j, r0=r0: e.dma_start(out=VR[r0:r0 + 128, :], in_=vro[j][:, :]), Bvro[j], reads=[Bvro[j]], writes=[Bo])
            p, Bp_ = tm(c0 + 640, 512)
            P.op("act", lambda e, p=p, j=j: e.activation(out=gs[j][:, :], in_=p[:, :], func=AF.Silu), reads=[Bp_], writes=[Bgs[j]])
            P.op("dve", lambda e, j=j: e.tensor_tensor(out=ggo[j][:, :], in0=gs[j][:, :], in1=gnb[:, :], op=ALU.mult),
                 reads=[Bgs[j], Bgn], writes=[Bggo[j]])
            Bo = P.buf(); Bouts.append(Bo)
            P.dma("sp", lambda e, j=j, r0=r0: e.dma_start(out=GG[r0:r0 + 128, :], in_=ggo[j][:, :]), Bggo[j], reads=[Bggo[j]], writes=[Bo])

    nb = len(blocks)
    phase1(0)
    for bi in range(nb):
        if bi + 1 < nb:
            phase1(bi + 1)
        phase2(bi)
    P.finish(Bouts)
    P.emit()
    return nc


def build_s3(S, stop=9):
    cx = Ctx()
    nc, P = cx.nc, cx.P
    nq = S // 128
    NCH = nq + 2
    T2 = NCH * 128
    qa_d = cx.din("qa", [2, 128, T2], BF16)
    ka_d = cx.din("ka", [2, 128, T2], BF16)
    va_d = cx.din("va", [T2, 64], BF16)
    qd_d = cx.din("qd", [2, 128, T2], BF16)
    kt_d = cx.din("kt", [2, 64, T2], BF16)
    ktok_d = cx.din("ktok", [T2, 2, 64], BF16)
    vr_d = cx.din("vr", [T2, 2, 128], BF16)
    gg_d = cx.din("gg", [T2, 2, 128], F32)
    dec3_d = cx.din("dec3", [128, 3, 2])
    sink_d = cx.din("sinkb", [128, 4])
    cf_d = cx.din("cf", [128, 5 * 128 + 512 + 2 + 4])
    tri_d = cx.din("tri", [128, 2, 512], BF16)
    mix = cx.dout("mix", [T2, 512], BF16)

    arena = cx.sb([128, 4 * T2], BF16, "arena")
    qa = arena[:, 0:2 * T2].rearrange("p (c t) -> p c t", c=2)
    ka = arena[:, 2 * T2:4 * T2].rearrange("p (c t) -> p c t", c=2)
    vaug = cx.sb([128, NCH, 65], BF16, "vaug")
    Pt = [cx.sb([128, 5, 512], BF16, f"Pt{i}") for i in range(2)]
    qd = arena[:, 3 * T2:4 * T2]
    kt = cx.sb([64, T2], BF16, "kt")
    ktok = cx.sb([128, NCH, 64], BF16, "ktok")
    vr = cx.sb([128, NCH, 128], BF16, "vr")
    KV = arena[:, 0:2 * T2].bitcast(F32).rearrange("p (n d) -> p n d", d=128)
    STb = arena[:, 2 * T2:3 * T2].rearrange("p (n d) -> p n d", d=128)
    cf = cx.sb([128, 5 * 128 + 512 + 2 + 4], F32, "cf")
    tri = cx.sb([128, 2, 512], BF16, "tri")
    dec3 = cx.sb([128, 6], F32, "dec3")
    L3 = cx.sb([128, 6], F32, "L3")
    ltmp = cx.sb([128, 6], F32, "ltmp")
    sinkb = cx.sb([128, 4], F32, "sinkb")
    ES = cx.sb([128, 4], F32, "ES")
    XI4 = cx.sb([128, 512], F32, "XI4")
    MT = cx.sb([128, 128], F32, "MT")
    E2 = cx.sb([128, 128], F32, "E2")
    ZETA = cx.sb([128, 2], F32, "ZETA")
    DEC = cx.sb([128, 1], F32, "DEC")
    WC = cx.sb([128, 2, 2], F32, "WC")
    den = [cx.sb([128, 8], F32, f"den{i}") for i in range(2)]
    mixa = [cx.sb([128, 256], BF16, f"mixa{i}") for i in range(2)]
    KZg = [cx.sb([128, 4, 128], BF16, f"KZg{i}") for i in range(2)]
    KZc = cx.sb([128, 2, 128], BF16, "KZc")
    QXg = [cx.sb([128, 512], BF16, f"QXg{i}") for i in range(2)]
    Wt = [cx.sb([128, 128], BF16, f"Wt{i}") for i in range(2)]
    R = [cx.sb([128, 128], F32, f"R{i}") for i in range(2)]
    gst = [cx.sb([128, 24], F32, f"gst{i}") for i in range(2)]
    ggt = [cx.sb([128, 4, 128], F32, f"ggt{i}") for i in range(2)]
    tn = [cx.sb([128, 4, 128], F32, f"tn{i}") for i in range(2)]
    mixr = [cx.sb([128, 4, 128], BF16, f"mixr{i}") for i in range(2)]
    junk = cx.sb([128, 128], BF16, "junk")
    pS = [cx.ps([128, 512], F32, f"pS{i}") for i in range(7)]
    pO = cx.ps([128, 512], F32, "pO")

    Bqa, Bka, Bva, Bqd, Bkt, Bktok, Bvr, BKV, BSTb, Bcf, Btri, Bdec, BL3, Bsink, BES = [P.buf() for _ in range(15)]
    BKV = Bqa
    BSTb = Bka
    BXI, BMT, BE2, BZ, BDEC, BWC, BKZc, Bjunk, BpO = [P.buf() for _ in range(9)]
    BPt, Bden, Bmixa, BKZg, BQXg, BWt, BR, Bgst, Bggt, Btn, Bmixr = [P.bufs(2) for _ in range(11)]
    BpS = P.bufs(7)
    cst, Bc = cx.consts()
    Bouts = []

    o_dp, o_dn, o_mf, o_mb, o_xi, o_ze, o_wc = 0, 128, 256, 384, 640, 1152, 1154

    P.dma("sp", lambda e: e.dma_start(out=cf[:, :], in_=cf_d[:, :]), Bcf, writes=[Bcf])
    P.dma("sp", lambda e: e.dma_start(out=tri[:, :, :], in_=tri_d[:, :, :]), Btri, writes=[Btri])
    P.dma("sp", lambda e: e.dma_start(out=dec3[:, :], in_=dec3_d.rearrange("p a b -> p (a b)")), Bdec, writes=[Bdec])
    P.dma("sp", lambda e: e.dma_start(out=sinkb[:, :], in_=sink_d[:, :]), Bsink, writes=[Bsink])
    P.dma("sp", lambda e: e.dma_start(out=ka[:, 0, :], in_=ka_d[0]), Bka, writes=[Bka])
    P.dma("sp", lambda e: e.dma_start(out=ka[:, 1, :], in_=ka_d[1]), Bqd, writes=[Bqd])
    for c in range(2):
        P.dma("sp", lambda e, c=c: e.dma_start(out=qa[:, c, :], in_=qa_d[c]), Bqa, writes=[Bqa])
    P.op("pool", lambda e: e.memset(vaug[:, :, 64:65], 1.0), writes=[Bva])
    P.dma("sp", lambda e: e.dma_start(out=vaug[:, :, 0:64], in_=va_d.rearrange("(n p) d -> p n d", p=128)), Bva, writes=[Bva])

    P.op("act", lambda e: e.activation(out=ltmp[:, :], in_=dec3[:, :], func=AF.Exp, scale=-1.0), reads=[Bdec], writes=[BL3])
    P.op("dve", lambda e: e.tensor_scalar_add(out=ltmp[:, :], in0=ltmp[:, :], scalar1=1.0), reads=[BL3], writes=[BL3])
    P.op("act", lambda e: e.activation(out=L3[:, :], in_=ltmp[:, :], func=AF.Ln), reads=[BL3], writes=[BL3])
    P.op("dve", lambda e: e.tensor_scalar_mul(out=L3[:, :], in0=L3[:, :], scalar1=-1.0), reads=[BL3], writes=[BL3])
    P.op("act", lambda e: e.activation(out=ES[:, :], in_=sinkb[:, :], func=AF.Exp), reads=[Bsink], writes=[BES])

    if stop <= 1:
        Bo = P.buf(); Bouts.append(Bo)
        P.dma("sp", lambda e: e.dma_start(out=mix[0:128, 0:512], in_=tri[:, 0, :]), Btri, reads=[BES, BL3, Bva, Bqa, Bka, Bqd, Btri, Bcf], writes=[Bo])
        P.finish(Bouts); P.emit(); return nc
    actr = [0]

    def attn_tile(qo, keys, out_row):
        i0 = actr[0]
        actr[0] += 1
        pt, Bpt = Pt[i0 % 2], BPt[i0 % 2]
        nk = len(keys)
        for i, (ko, ci, mk) in enumerate(keys):
            ps, Bps = pS[i], BpS[i]
            for a in range(4):
                c, r = a // 2, a % 2
                P.op("pe", lambda e, ps=ps, a=a, c=c, r=r, ko=ko: e.matmul(
                    ps[:, a * 128:(a + 1) * 128], lhsT=ka[:, r, ko:ko + 128],
                    rhs=qa[:, c, qo:qo + 128], start=True, stop=True), reads=[Bka, Bqd, Bqa], writes=[Bps])
            P.op("act", lambda e, ps=ps, i=i, pt=pt: e.activation(out=pt[:, i, :], in_=ps[:, :], func=AF.Exp, scale=0.125),
                 reads=[Bps], writes=[Bpt])
            if mk is not None:
                P.op("dve", lambda e, i=i, pt=pt, mk=mk: e.tensor_tensor(out=pt[:, i, :], in0=pt[:, i, :], in1=tri[:, mk, :], op=ALU.mult),
                     reads=[Bpt, Btri], writes=[Bpt])
        for a in range(4):
            for i, (ko, ci, mk) in enumerate(keys):
                P.op("pe", lambda e, a=a, i=i, ci=ci, pt=pt: e.matmul(pO[:, a * 128:a * 128 + 65], lhsT=pt[:, i, a * 128:(a + 1) * 128],
                                                                    rhs=vaug[:, ci, :], start=(i == 0), stop=(i == nk - 1)),
                     reads=[Bpt, Bva], writes=[BpO])
        dn, Bdn = den[i0 % 2], Bden[i0 % 2]
        mo, Bmo = mixa[i0 % 2], Bmixa[i0 % 2]
        pov = pO[:, :].rearrange("p (a e) -> p a e", e=128)
        P.op("dve", lambda e, dn=dn: e.tensor_tensor(out=dn[:, 0:4], in0=pov[:, :, 64], in1=ES[:, :], op=ALU.add),
             reads=[BpO, BES], writes=[Bdn])
        P.op("dve", lambda e, dn=dn: e.reciprocal(out=dn[:, 4:8], in_=dn[:, 0:4]), reads=[Bdn], writes=[Bdn])
        for a in range(4):
            P.op("act", lambda e, a=a, dn=dn, mo=mo: e.activation(out=mo[:, a * 64:(a + 1) * 64], in_=pO[:, a * 128:a * 128 + 64],
                                                                func=AF.Identity, scale=dn[:, 4 + a:5 + a]), reads=[BpO, Bdn], writes=[Bmo])
        Bo = P.buf(); Bouts.append(Bo)
        P.dma("sp", lambda e, mo=mo: e.dma_start(out=mix[out_row:out_row + 128, 0:256], in_=mo[:, :]), Bmo, reads=[Bmo], writes=[Bo])

    ctxk = [(S + c * 128, nq + c, None) for c in range(2)]
    for n in range(nq):
        keys = []
        if n > 0:
            keys.append(((n - 1) * 128, n - 1, 0))
        keys.append((n * 128, n, None))
        if n < nq - 1:
            keys.append(((n + 1) * 128, n + 1, 1))
        attn_tile(n * 128, keys + ctxk, n * 128)
    for c in range(2):
        attn_tile(S + c * 128, ctxk, S + c * 128)

    if stop <= 2:
        P.finish(Bouts); P.emit(); return nc
    kvq, scq, oq = [0], [0], [0]
    for r in range(2):
        P.dma("sp", lambda e, r=r: e.dma_start(out=qd[:, :], in_=qd_d[r]), Bqd, writes=[Bqd])
        P.dma("sp", lambda e, r=r: e.dma_start(out=kt[:, :], in_=kt_d[r]), Bkt, writes=[Bkt])
        P.dma("sp", lambda e, r=r: e.dma_start(out=ktok[:, :, :], in_=ktok_d[:, r, :].rearrange("(n p) d -> p n d", p=128)), Bktok, writes=[Bktok])
        P.dma("sp", lambda e, r=r: e.dma_start(out=vr[:, :, :], in_=vr_d[:, r, :].rearrange("(n p) d -> p n d", p=128)), Bvr, writes=[Bvr])
        P.op("act", lambda e, r=r: e.activation(out=XI4[:, :], in_=cf[:, o_xi:o_xi + 512], func=AF.Exp, scale=L3[:, r:r + 1]), reads=[Bcf, BL3], writes=[BXI])
        P.op("act", lambda e, r=r: e.activation(out=MT[:, :], in_=cf[:, o_dp:o_dp + 128], func=AF.Exp, scale=L3[:, 2 + r:3 + r]), reads=[Bcf, BL3], writes=[BMT])
        P.op("act", lambda e, r=r: e.activation(out=E2[:, :], in_=cf[:, o_dn:o_dn + 128], func=AF.Exp, scale=L3[:, 4 + r:5 + r]), reads=[Bcf, BL3], writes=[BE2])
        P.op("dve", lambda e: e.tensor_tensor(out=MT[:, :], in0=MT[:, :], in1=cf[:, o_mf:o_mf + 128], op=ALU.mult), reads=[BMT, Bcf], writes=[BMT])
        P.op("dve", lambda e: e.tensor_tensor(out=E2[:, :], in0=E2[:, :], in1=cf[:, o_mb:o_mb + 128], op=ALU.mult), reads=[BE2, Bcf], writes=[BE2])
        P.op("dve", lambda e: e.tensor_tensor(out=MT[:, :], in0=MT[:, :], in1=E2[:, :], op=ALU.add), reads=[BMT, BE2], writes=[BMT])
        P.op("act", lambda e, r=r: e.activation(out=ZETA[:, 0:1], in_=cf[:, o_ze:o_ze + 1], func=AF.Exp, scale=L3[:, 2 + r:3 + r]), reads=[Bcf, BL3], writes=[BZ])
        P.op("act", lambda e, r=r: e.activation(out=ZETA[:, 1:2], in_=cf[:, o_ze + 1:o_ze + 2], func=AF.Exp, scale=L3[:, 4 + r:5 + r]), reads=[Bcf, BL3], writes=[BZ])
        P.op("act", lambda e, r=r: e.activation(out=DEC[:, :], in_=L3[:, r:r + 1], func=AF.Exp, scale=128.0), reads=[BL3], writes=[BDEC])
        wcv = cf[:, o_wc:o_wc + 4].rearrange("p (c d) -> p c d", d=2)
        P.op("act", lambda e, r=r: e.activation(out=WC[:, :, 0], in_=wcv[:, :, 0], func=AF.Exp, scale=L3[:, 2 + r:3 + r]), reads=[Bcf, BL3], writes=[BWC])
        P.op("act", lambda e, r=r: e.activation(out=WC[:, :, 1], in_=wcv[:, :, 1], func=AF.Exp, scale=L3[:, 4 + r:5 + r]), reads=[Bcf, BL3], writes=[BWC])
        for c in range(2):
            for d_ in range(2):
                P.op("dve", lambda e, c=c, d_=d_: e.tensor_scalar(out=KZc[:, c, d_ * 64:(d_ + 1) * 64], in0=ktok[:, nq + c, :], scalar1=WC[:, c, d_:d_ + 1],
                                                                 scalar2=None, op0=ALU.mult), reads=[Bktok, BWC], writes=[BKZc])
        for c in range(2):
            P.op("pe", lambda e, c=c: e.matmul(pO[:, 0:128], lhsT=KZc[:, c, :], rhs=vr[:, nq + c, :], start=(c == 0), stop=(c == 1)),
                 reads=[BKZc, Bvr], writes=[BpO])
        groups = [list(range(g0, min(g0 + 4, nq))) for g0 in range(0, nq, 4)] + [[nq, nq + 1]]
        for grp in groups:
            g0, ng = grp[0], len(grp)
            kz, Bkz = KZg[kvq[0] % 2], BKZg[kvq[0] % 2]
            pk, Bpk = pS[4 + kvq[0] % 2], BpS[4 + kvq[0] % 2]
            kvq[0] += 1
            for d_ in range(2):
                P.op("dve", lambda e, kz=kz, g0=g0, ng=ng, d_=d_: e.tensor_scalar(out=kz[:, 0:ng, d_ * 64:(d_ + 1) * 64], in0=ktok[:, g0:g0 + ng, :],
                                                                               scalar1=ZETA[:, d_:d_ + 1], scalar2=None, op0=ALU.mult),
                     reads=[Bktok, BZ], writes=[Bkz])
            for i, n in enumerate(grp):
                P.op("pe", lambda e, kz=kz, i=i, n=n, pk=pk: e.matmul(pk[:, i * 128:(i + 1) * 128], lhsT=kz[:, i, :], rhs=vr[:, n, :],
                                                                    start=True, stop=True), reads=[Bkz, Bvr], writes=[Bpk])
            P.op("act", lambda e, pk=pk, g0=g0, ng=ng: e.activation(out=KV[:, g0:g0 + ng, :], in_=pk[:, 0:ng * 128].rearrange("p (n d) -> p n d", d=128),
                                                                  func=AF.Copy), reads=[Bpk], writes=[BKV])
        P.op("dve", lambda e: e.tensor_copy(out=R[0][:, :], in_=pO[:, 0:128]), reads=[BpO], writes=[BR[0]])
        P.op("dve", lambda e: e.tensor_copy(out=R[1][:, :], in_=pO[:, 0:128]), reads=[BpO], writes=[BR[1]])
        f, b = slice(0, 64), slice(64, 128)
        P.op("act", lambda e: e.activation(out=STb[f, 0, :], in_=R[0][f, :], func=AF.Copy), reads=[BR[0]], writes=[BSTb])
        P.op("act", lambda e: e.activation(out=STb[b, nq - 1, :], in_=R[1][b, :], func=AF.Copy), reads=[BR[1]], writes=[BSTb])
        for n in range(nq - 1):
            P.op("dve", lambda e, n=n: e.scalar_tensor_tensor(out=R[0][f, :], in0=R[0][f, :], scalar=DEC[f, 0:1], in1=KV[f, n, :],
                                                             op0=ALU.mult, op1=ALU.add), reads=[BR[0], BDEC, BKV], writes=[BR[0]])
            P.op("act", lambda e, n=n: e.activation(out=STb[f, n + 1, :], in_=R[0][f, :], func=AF.Copy), reads=[BR[0]], writes=[BSTb])
            m = nq - 1 - n
            P.op("dve", lambda e, m=m: e.scalar_tensor_tensor(out=R[1][b, :], in0=R[1][b, :], scalar=DEC[b, 0:1], in1=KV[b, m, :],
                                                             op0=ALU.mult, op1=ALU.add), reads=[BR[1], BDEC, BKV], writes=[BR[1]])
            P.op("act", lambda e, m=m: e.activation(out=STb[b, m - 1, :], in_=R[1][b, :], func=AF.Copy), reads=[BR[1]], writes=[BSTb])
        P.op("pool", lambda e: e.memset(STb[f, nq, :], 0.0), writes=[BSTb])
        P.op("pool", lambda e: e.memset(STb[b, nq + 1, :], 0.0), writes=[BSTb])
        P.op("act", lambda e: e.activation(out=STb[f, nq + 1, :], in_=KV[f, nq, :], func=AF.Copy), reads=[BKV], writes=[BSTb])
        P.op("act", lambda e: e.activation(out=STb[b, nq, :], in_=KV[b, nq + 1, :], func=AF.Copy), reads=[BKV], writes=[BSTb])
        for grp in groups:
            g0, ng = grp[0], len(grp)
            q = oq[0] % 2
            oq[0] += 1
            po, Bpo = pS[2 + q], BpS[2 + q]
            qx, Bqx = QXg[q], BQXg[q]
            P.op("dve", lambda e, qx=qx, g0=g0, ng=ng: e.tensor_tensor(out=qx[:, 0:ng * 128], in0=qd[:, g0 * 128:(g0 + ng) * 128],
                                                                      in1=XI4[:, 0:ng * 128], op=ALU.mult), reads=[Bqd, BXI], writes=[Bqx])
            for i, n in enumerate(grp):
                s_ = scq[0] % 2
                scq[0] += 1
                psc, Bpsc = pS[s_], BpS[s_]
                P.op("pe", lambda e, n=n, psc=psc: e.matmul(psc[:, 0:128], lhsT=kt[0:64, n * 128:(n + 1) * 128], rhs=qd[0:64, n * 128:(n + 1) * 128],
                                                          start=True, stop=True), reads=[Bkt, Bqd], writes=[Bpsc])
                P.op("dve", lambda e, psc=psc, s_=s_: e.tensor_tensor(out=Wt[s_][:, :], in0=psc[:, 0:128], in1=MT[:, :], op=ALU.mult),
                     reads=[Bpsc, BMT], writes=[BWt[s_]])
                P.op("pe", lambda e, i=i, n=n, s_=s_, po=po: e.matmul(po[:, i * 128:(i + 1) * 128], lhsT=Wt[s_][:, :], rhs=vr[:, n, :],
                                                                    start=True, stop=False), reads=[BWt[s_], Bvr], writes=[Bpo])
                P.op("pe", lambda e, i=i, n=n, qx=qx, po=po: e.matmul(po[:, i * 128:(i + 1) * 128], lhsT=qx[:, i * 128:(i + 1) * 128], rhs=STb[:, n, :],
                                                                    start=False, stop=True), reads=[Bqx, BSTb], writes=[Bpo])
            gs_, Bgs_ = gst[q], Bgst[q]
            for i in range(ng):
                P.op("act", lambda e, i=i, po=po, gs_=gs_: e.activation(out=junk[:, :], in_=po[:, i * 128:(i + 1) * 128], func=AF.Identity,
                                                                      accum_out=gs_[:, i:i + 1]), reads=[Bpo], writes=[Bjunk, Bgs_])
                P.op("act", lambda e, i=i, po=po, gs_=gs_: e.activation(out=junk[:, :], in_=po[:, i * 128:(i + 1) * 128], func=AF.Square,
                                                                      accum_out=gs_[:, 4 + i:5 + i]), reads=[Bpo], writes=[Bjunk, Bgs_])
            P.op("dve", lambda e, gs_=gs_, ng=ng: e.tensor_scalar_mul(out=gs_[:, 8:8 + ng], in0=gs_[:, 0:ng], scalar1=1.0 / 128), reads=[Bgs_], writes=[Bgs_])
            P.op("dve", lambda e, gs_=gs_, ng=ng: e.tensor_tensor(out=gs_[:, 12:12 + ng], in0=gs_[:, 8:8 + ng], in1=gs_[:, 8:8 + ng], op=ALU.mult),
                 reads=[Bgs_], writes=[Bgs_])
            P.op("dve", lambda e, gs_=gs_, ng=ng: e.scalar_tensor_tensor(out=gs_[:, 16:16 + ng], in0=gs_[:, 4:4 + ng], scalar=1.0 / 128,
                                                                        in1=gs_[:, 12:12 + ng], op0=ALU.mult, op1=ALU.subtract),
                 reads=[Bgs_], writes=[Bgs_])
            rstd_from_ss(cx, gs_[:, 16:16 + ng], gs_[:, 20:20 + ng], Bgs_, Bgs_, cst, Bc, 1e-5, gs_[:, 12:12 + ng])
            gt, Bgt = ggt[q], Bggt[q]
            P.dma("sp", lambda e, gt=gt, g0=g0, ng=ng, r=r: e.dma_start(out=gt[:, 0:ng, :], in_=gg_d[g0 * 128:(g0 + ng) * 128, r, :].rearrange("(n p) d -> p n d", p=128)),
                  Bgt, writes=[Bgt])
            for i in range(ng):
                P.op("dve", lambda e, i=i, po=po, gs_=gs_, q=q: e.tensor_scalar(out=tn[q][:, i, :], in0=po[:, i * 128:(i + 1) * 128],
                                                                              scalar1=gs_[:, 8 + i:9 + i], scalar2=gs_[:, 20 + i:21 + i],
                                                                              op0=ALU.subtract, op1=ALU.mult), reads=[Bpo, Bgs_], writes=[Btn[q]])
            P.op("pool", lambda e, q=q, gt=gt, ng=ng: e.tensor_tensor(out=mixr[q][:, 0:ng, :], in0=tn[q][:, 0:ng, :], in1=gt[:, 0:ng, :], op=ALU.mult),
                 reads=[Btn[q], Bgt], writes=[Bmixr[q]])
            Bo = P.buf(); Bouts.append(Bo)
            P.dma("sp", lambda e, q=q, g0=g0, ng=ng, r=r: e.dma_start(
                out=mix[g0 * 128:(g0 + ng) * 128, 256 + r * 128:256 + (r + 1) * 128].rearrange("(n p) d -> p n d", p=128),
                in_=mixr[q][:, 0:ng, :]), Bmixr[q], reads=[Bmixr[q]], writes=[Bo])
    P.finish(Bouts)
    P.emit()
    return nc


def build_out(tile_sets):
    cx = Ctx()
    nc, P = cx.nc, cx.P
    nt = len(tile_sets)
    T = nt * 128
    nset = max(tile_sets) + 1
    x = cx.din("x", [T, D])
    mixT = cx.din("mixT", [D, T], BF16)
    wo = cx.din("wo", [D, D])
    Gd = cx.din("G", [nset, D])
    y = cx.dout("y", [T, D])
    wob = cx.sb([128, 8, D], BF16, "wob")
    G = cx.sb([128, nset, D], F32, "G")
    hb = [cx.sb([128, D], F32, f"hb{i}") for i in range(3)]
    mt = [cx.sb([128, 8, 128], BF16, f"mt{i}") for i in range(3)]
    t1 = [cx.sb([128, D], F32, f"t1{i}") for i in range(2)]
    junk = cx.sb([128, 512], BF16, "junk")
    st = [cx.sb([128, 8], F32, f"st{i}") for i in range(2)]
    pY = [cx.ps([128, 512], F32, f"pY{i}") for i in range(4)]
    Bwob, BG, Bjunk = P.buf(), P.buf(), P.buf()
    Bhb, Bmt, Bt1, Bst, BpY = P.bufs(3), P.bufs(3), P.bufs(2), P.bufs(2), P.bufs(4)
    cst, Bc = cx.consts()
    Bouts = []
    for s in range(nset):
        P.dma("sp", lambda e, s=s: e.dma_start(out=G[:, s, :], in_=Gd[s:s + 1, :].partition_broadcast(128)), BG, writes=[BG])
    P.dma("pool", lambda e: e.dma_start(out=wob[:, :, :], in_=wo.rearrange("(k p) n -> p k n", p=128)), Bwob, writes=[Bwob])
    mv = mixT.rearrange("(k p) t -> p k t", p=128)

    def load(t):
        h, Bh = hb[t % 3], Bhb[t % 3]
        P.dma("sp", lambda e, h=h, t=t: e.dma_start(out=h[:, :], in_=x[t * 128:(t + 1) * 128, :]), Bh, writes=[Bh])
        P.dma("sp", lambda e, t=t: e.dma_start(out=mt[t % 3][:, :, :], in_=mv[:, :, t * 128:(t + 1) * 128]), Bmt[t % 3], writes=[Bmt[t % 3]])

    load(0)
    if nt > 1:
        load(1)
    for t in range(nt):
        if t + 2 < nt:
            load(t + 2)
        s = tile_sets[t]
        h, Bh = hb[t % 3], Bhb[t % 3]
        m, Bm = mt[t % 3], Bmt[t % 3]
        s3, Bs3 = st[t % 2], Bst[t % 2]
        tt, Btt = t1[t % 2], Bt1[t % 2]
        pys = []
        for half in range(2):
            py, Bp = pY[(2 * t + half) % 4], BpY[(2 * t + half) % 4]
            pys.append((py, Bp))
            for k in range(8):
                P.op("pe", lambda e, k=k, half=half, py=py, m=m: e.matmul(py[:, :], lhsT=m[:, k, :], rhs=wob[:, k, half * 512:(half + 1) * 512],
                                                                        start=(k == 0), stop=(k == 7)), reads=[Bm, Bwob], writes=[Bp])
            P.op("act", lambda e, py=py, s3=s3, half=half: e.activation(out=junk[:, :], in_=py[:, :], func=AF.Square, scale=1.0 / 32,
                                                                      accum_out=s3[:, 4 + half:5 + half]), reads=[Bp], writes=[Bjunk, Bs3])
        P.op("dve", lambda e, s3=s3: e.tensor_tensor(out=s3[:, 6:7], in0=s3[:, 4:5], in1=s3[:, 5:6], op=ALU.add), reads=[Bs3], writes=[Bs3])
        rstd_from_ss(cx, s3[:, 6:7], s3[:, 7:8], Bs3, Bs3, cst, Bc, 1e-6, s3[:, 3:4])
        for half in range(2):
            py, Bp = pys[half]
            P.op("dve", lambda e, py=py, s3=s3, half=half, s=s, tt=tt: e.scalar_tensor_tensor(
                out=tt[:, half * 512:(half + 1) * 512], in0=py[:, :], scalar=s3[:, 7:8],
                in1=G[:, s, half * 512:(half + 1) * 512], op0=ALU.mult, op1=ALU.mult), reads=[Bp, Bs3, BG], writes=[Btt])
        P.op("pool", lambda e, h=h, tt=tt: e.tensor_tensor(out=h[:, :], in0=h[:, :], in1=tt[:, :], op=ALU.add), reads=[Bh, Btt], writes=[Bh])
        Bo = P.buf(); Bouts.append(Bo)
        P.dma("sp", lambda e, h=h, t=t: e.dma_start(out=y[t * 128:(t + 1) * 128, :], in_=h[:, :]), Bh, reads=[Bh], writes=[Bo])
    P.finish(Bouts)
    P.emit()
    return nc


_PROGS = {}
DBG = {}


def _prog(key, fn):
    if key not in _PROGS:
        _PROGS[key] = fn()
    return _PROGS[key]


def _run(nc, in_maps):
    res = run_bass_kernel_spmd(nc, in_maps, core_ids=list(range(len(in_maps))))
    return res.results


def _wext_index():
    permA = np.array([d + 16 if (d % 32) < 16 else d - 16 for d in range(64)])
    permR = np.array([d + 32 if d < 32 else d - 32 for d in range(64)])
    idx = []

    def pair(base, perm):
        main = base + np.arange(128)
        part = base + np.concatenate([perm, 64 + perm])
        idx.append(main)
        idx.append(part)
    for c in range(4):
        pair(c * 128, permA)
    pair(512, permA)
    for c in range(2):
        pair(768 + c * 128, permR)
    for c in range(2):
        pair(1024 + c * 128, permR)
    idx.append(np.arange(640, 768))
    idx.append(np.arange(1280, 1792))
    idx.append(np.arange(1792, 2304))
    return np.concatenate(idx)


def _rope_tables(S, L):
    f32 = np.float32
    pos = np.arange(S)
    row = (pos // 64).astype(f32)
    col = (pos % 64).astype(f32)
    inv16 = (f32(10000.0) ** (-np.arange(16, dtype=f32) / f32(16))).astype(f32)
    ang_row = row[:, None] * inv16[None, :]
    ang_col = col[:, None] * inv16[None, :]
    invR = (f32(10000.0) ** (-np.linspace(0.0, 1.0, 32, dtype=f32))).astype(f32)
    ang_ret = pos.astype(f32)[:, None] * invR[None, :]
    d = np.arange(64)
    angA = np.where((d < 32)[None, :], ang_row[:, d % 16], ang_col[:, d % 16]).astype(f32)
    sgnA = np.where((d % 32) < 16, -1.0, 1.0).astype(f32)
    angR = ang_ret[:, d % 32].astype(f32)
    sgnR = np.where(d < 32, -1.0, 1.0).astype(f32)
    tab = np.zeros((6, 128, S + L), f32)
    for hh in range(2):
        sl = slice(hh * 64, (hh + 1) * 64)
        tab[0, sl, :S] = np.cos(angA).T
        tab[1, sl, :S] = (np.sin(angA) * sgnA[None, :]).T
        tab[2, sl, :S] = np.cos(angR).T
        tab[3, sl, :S] = (np.sin(angR) * sgnR[None, :]).T
    tab[0, :, S:] = 1.0
    tab[2, :, S:] = 1.0
    tab[4] = tab[2] * f32(0.125)
    tab[5] = tab[3] * f32(0.125)
    return tab


def _s3_consts():
    import ml_dtypes
    f32 = np.float32
    cf = np.zeros((128, 5 * 128 + 512 + 2 + 4), f32)
    s = np.arange(128)[:, None]
    q = np.arange(128)[None, :]
    cf[:, 0:128] = np.maximum(q - s, 0)
    cf[:, 128:256] = np.maximum(s - q, 0)
    cf[:, 256:384] = (q >= s)
    cf[:, 384:512] = (q < s)
    i = np.arange(128)
    xi = np.zeros((128, 128), f32)
    xi[0:64, :] = (i + 1)[None, :]
    xi[64:128, :] = (128 - i)[None, :]
    cf[:, 640:1152] = np.tile(xi, (1, 4))
    cf[:, 1152] = 127 - i
    cf[:, 1153] = i
    for c in range(2):
        cf[:, 1154 + 2 * c] = 255 - (c * 128 + i)
        cf[:, 1154 + 2 * c + 1] = c * 128 + i
    tri = np.zeros((128, 2, 512), f32)
    tri[:, 0, :] = np.tile((s >= q).astype(f32), (1, 4))
    tri[:, 1, :] = np.tile((s <= q).astype(f32), (1, 4))
    return cf, tri.astype(ml_dtypes.bfloat16)


def _pp(v):
    return np.ascontiguousarray(v.reshape(8, 128).T)


def kernel(x, c, ctx, c_ctx, ada_w, ada_b, norm_pre, norm_post, ffn1_wi, ffn1_wo, ffn2_wi, ffn2_wo,
           w_in, w_out, attn_sink, ret_decay_fwd, ret_decay_bwd, ret_gn):
    x = np.asarray(x, np.float32)
    B, S, _ = x.shape
    L = ctx.shape[1]
    depth = ada_w.shape[0]
    NT = NCORES
    Tl = B * S // NT
    assert Tl % 128 == 0 and S % Tl == 0
    cores = [((i * Tl) // S, (i * Tl) % S) for i in range(NT)]
    tile_sets = [0] * (Tl // 128) + [1] * (L // 128)
    h = x.copy()
    hc = np.asarray(ctx, np.float32).copy()
    widx = _wext_index()
    tab_full = _rope_tables(S, L)
    cf, tri = _s3_consts()

    def gather_tok():
        return [np.concatenate([h[b, o:o + Tl], hc[b]], 0) for (b, o) in cores]

    def scatter_tok(ys):
        for i, (b, o) in enumerate(cores):
            h[b, o:o + Tl] = ys[i][:Tl]
            if o == 0:
                hc[b] = ys[i][Tl:]

    for l in range(depth):
        ncm = _prog("mod", build_mod)
        maps = []
        for b in range(B):
            cv = np.stack([np.asarray(c[b], np.float32), np.asarray(c_ctx, np.float32)], 0)
            maps.append(dict(cT=np.ascontiguousarray(cv.reshape(2, 8, 128).transpose(2, 1, 0)), ada_w=np.asarray(ada_w[l], np.float32),
                             ada_b=np.asarray(ada_b[l], np.float32).reshape(1, -1), gpre=np.asarray(norm_pre[l], np.float32).reshape(1, -1),
                             gpost=np.asarray(norm_post[l], np.float32).reshape(1, -1)))
        modv = [r["modv"] for r in _run(ncm, maps)]

        def abT_of(b, s):
            return np.stack([np.stack([_pp(modv[b][st_, 3 * s]), _pp(modv[b][st_, 3 * s + 1])], -1) for st_ in range(2)], 0)

        def G_of(b, s):
            return np.ascontiguousarray(modv[b][:, 3 * s + 2, :])

        def run_ffn(s, wi_, wo_):
            ncf = _prog(("ffn", tuple(tile_sets)), lambda: build_ffn(tile_sets))
            xs = gather_tok()
            maps = [dict(x=xs[i], wi=np.asarray(wi_, np.float32), wo=np.asarray(wo_, np.float32), abT=abT_of(b, s), G=G_of(b, s))
                    for i, (b, o) in enumerate(cores)]
            scatter_tok([r["y"] for r in _run(ncf, maps)])

        run_ffn(0, ffn1_wi[l], ffn1_wo[l])

        ncs2 = _prog(("s2", tuple(tile_sets)), lambda: build_s2(tile_sets))
        wext = np.ascontiguousarray(np.asarray(w_in[l], np.float32)[:, widx])
        xs = gather_tok()
        maps = []
        for i, (b, o) in enumerate(cores):
            tab = np.ascontiguousarray(np.concatenate([tab_full[:, :, o:o + Tl], tab_full[:, :, S:]], 2))
            maps.append(dict(x=xs[i], wext=wext, abT=abT_of(b, 1), tab=tab, gn=np.asarray(ret_gn[l], np.float32).reshape(1, -1)))
        r2 = _run(ncs2, maps)

        def asm(name, axis):
            out = []
            for b in range(B):
                parts = [np.take(r2[i][name], np.arange(Tl), axis=axis) for i, (bb, o) in enumerate(cores) if bb == b]
                first = [i for i, (bb, o) in enumerate(cores) if bb == b][0]
                parts.append(np.take(r2[first][name], np.arange(Tl, Tl + L), axis=axis))
                out.append(np.concatenate(parts, axis))
            return out
        FMb, VAb, VRb, GGb = asm("FM", 2), asm("VA", 0), asm("VR", 0), asm("GG", 0)

        ncs3 = _prog(("s3", S), lambda: build_s3(S))
        maps = []
        for b in range(B):
            for j in range(2):
                fm = FMb[b]
                kaj = fm[4, j * 64:(j + 1) * 64]
                kt = np.stack([fm[7 + j, r * 64:(r + 1) * 64] for r in range(2)], 0)
                qd = np.stack([np.concatenate([fm[5 + j, r * 64:(r + 1) * 64]] * 2, 0) for r in range(2)], 0)
                df, db = np.asarray(ret_decay_fwd[l], np.float32), np.asarray(ret_decay_bwd[l], np.float32)
                dec3 = np.zeros((128, 3, 2), np.float32)
                for r in range(2):
                    hr = 2 * j + r
                    dec3[0:64, 0, r] = df[hr]
                    dec3[64:128, 0, r] = db[hr]
                    dec3[:, 1, r] = df[hr]
                    dec3[:, 2, r] = db[hr]
                maps.append(dict(
                    qa=np.ascontiguousarray(fm[2 * j:2 * j + 2]), ka=np.ascontiguousarray(np.stack([np.concatenate([kaj, np.zeros_like(kaj)], 0), np.concatenate([np.zeros_like(kaj), kaj], 0)], 0)),
                    va=np.ascontiguousarray(VAb[b][:, j * 64:(j + 1) * 64]), qd=np.ascontiguousarray(qd), kt=np.ascontiguousarray(kt),
                    ktok=np.ascontiguousarray(kt.transpose(2, 0, 1)),
                    vr=np.ascontiguousarray(VRb[b][:, j * 256:(j + 1) * 256].reshape(-1, 2, 128)),
                    gg=np.ascontiguousarray(GGb[b][:, j * 256:(j + 1) * 256].reshape(-1, 2, 128)),
                    dec3=dec3, sinkb=np.ascontiguousarray(np.broadcast_to(np.asarray(attn_sink[l], np.float32)[4 * j:4 * j + 4], (128, 4))),
                    cf=cf, tri=tri))
        r3 = _run(ncs3, maps)
        MIX = []
        for b in range(B):
            m0, m1 = r3[2 * b]["mix"], r3[2 * b + 1]["mix"]
            MIX.append(np.concatenate([m0[:, 0:256], m1[:, 0:256], m0[:, 256:512], m1[:, 256:512]], 1))

        nco = _prog(("out", tuple(tile_sets)), lambda: build_out(tile_sets))
        xs = gather_tok()
        maps = []
        for i, (b, o) in enumerate(cores):
            mrows = np.concatenate([MIX[b][o:o + Tl], MIX[b][S:]], 0)
            maps.append(dict(x=xs[i], mixT=np.ascontiguousarray(mrows.T), wo=np.asarray(w_out[l], np.float32), G=G_of(b, 1)))
        scatter_tok([r["y"] for r in _run(nco, maps)])

        run_ffn(2, ffn2_wi[l], ffn2_wo[l])
    return h
```

```python
import numpy as np
import concourse.bass as bass
import concourse.mybir as mybir
from concourse.bass_utils import run_bass_kernel_spmd

F32 = mybir.dt.float32
BF16 = mybir.dt.bfloat16
AF = mybir.ActivationFunctionType
ALU = mybir.AluOpType

D = 1024
DFF = 2816
NFM = 9
NCX = 18 * 128 + 1152
PAIR_TAB = [0, 0, 0, 0, 0, 1, 1, 2, 2]
ENGS = ("pe", "act", "dve", "pool", "sp")


class Buf:
    __slots__ = ("name", "w", "rs", "dsem", "dcount", "keep", "nofence")

    def __init__(self, name):
        self.name = name
        self.w = None
        self.rs = []
        self.dsem = None
        self.dcount = 0
        self.keep = False
        self.nofence = False


class Op:
    __slots__ = ("eng", "fn", "waits", "sig", "val", "dma", "snap", "key", "pos")


class Prog:
    def __init__(self, nc):
        self.nc = nc
        self.ops = {e: [] for e in ENGS}
        self.known = {e: {} for e in ENGS}
        self.snapver = {e: None for e in ENGS}
        self.epoch = 0
        self.ecount = {e: 0 for e in ENGS}
        self.free_dsems = []
        self.ndsem = 0
        self.stage_bufs = []

    def buf(self, name=None, keep=False):
        b = Buf(name)
        b.keep = keep
        self.stage_bufs.append(b)
        return b

    def bufs(self, n, name="b"):
        return [self.buf(name) for _ in range(n)]

    def _record(self, eng, fn, reads, writes, dma_buf=None, inc=16):
        op = Op()
        op.eng = eng
        op.fn = fn
        op.sig = False
        op.val = None
        op.dma = dma_buf
        op.waits = []
        known = self.known[eng]
        deps = []
        for b in reads:
            if b.w is not None:
                deps.append((b.w, "raw"))
        for b in writes:
            if b.w is not None:
                deps.append((b.w, "waw"))
            for r in b.rs:
                deps.append((r, "war"))
        changed = False
        for d, kind in deps:
            if d.dma is None and d.eng == eng:
                if eng in ("pe", "sp") or kind == "war":
                    continue
            if known.get(d.key, -1) >= d.pos:
                continue
            op.waits.append(d)
            d.sig = True
            known[d.key] = d.pos
            for k, v in d.snap.items():
                if known.get(k, -1) < v:
                    known[k] = v
            changed = True
        if changed or self.snapver[eng] is None:
            self.snapver[eng] = dict(known)
        op.snap = self.snapver[eng]
        self.ops[eng].append(op)
        if dma_buf is not None:
            if dma_buf.dsem is None:
                if self.free_dsems:
                    idx, base = self.free_dsems.pop()
                else:
                    idx, base = self.ndsem, 0
                    self.ndsem += 1
                dma_buf.dsem = ("dma", idx)
                dma_buf.dcount = base
            dma_buf.dcount += inc
            op.key = dma_buf.dsem
            op.pos = dma_buf.dcount
            op.val = inc
        else:
            op.key = (eng, self.epoch)
            op.pos = self.ecount[eng]
            self.ecount[eng] += 1
        if fn is not None:
            for b in reads:
                b.rs.append(op)
            for b in writes:
                b.w = op
                b.rs = []
        return op

    def op(self, eng, fn, reads=(), writes=()):
        return self._record(eng, fn, list(reads), list(writes))

    def dma(self, queue, fn, sb, reads=(), writes=(), inc=16):
        return self._record(queue, fn, list(reads), list(writes), dma_buf=sb, inc=inc)

    def barrier(self, fence_fn, fence_buf):
        allb = [b for b in self.stage_bufs if not b.nofence]
        self._record("sp", fence_fn, [], allb + [fence_buf], dma_buf=fence_buf)
        for e in ENGS:
            self._record(e, None, [fence_buf], [])
        keep = []
        for b in self.stage_bufs:
            if b.keep:
                keep.append(b)
            elif b.dsem is not None and b is not fence_buf:
                self.free_dsems.append((b.dsem[1], b.dcount))
                b.dsem = None
        self.stage_bufs = keep
        self.epoch += 1
        for e in ENGS:
            self.ecount[e] = 0

    def emit(self):
        nc = self.nc
        cnt = {}
        for e in ENGS:
            for op in self.ops[e]:
                if op.dma is None and op.sig:
                    cnt[op.key] = cnt.get(op.key, 0) + 1
                    op.val = cnt[op.key]
        esem = {k: nc.alloc_semaphore(name=f"s_{k[0]}_{k[1]}") for k in cnt}
        dsem = [nc.alloc_semaphore(name=f"d_{i}") for i in range(self.ndsem)]

        def run(e, eng):
            for op in self.ops[e]:
                for d in op.waits:
                    if d.dma is not None:
                        eng.wait_ge(dsem[d.key[1]], d.pos)
                    else:
                        eng.wait_ge(esem[d.key], d.val)
                if op.fn is None:
                    continue
                ins = op.fn(eng)
                if op.dma is not None:
                    ins.then_inc(dsem[op.key[1]], op.val)
                elif op.sig:
                    ins.then_inc(esem[op.key], 1)

        with nc.Block() as block:
            @block.tensor
            def _(eng):
                run("pe", eng)

            @block.scalar
            def _(eng):
                run("act", eng)

            @block.vector
            def _(eng):
                run("dve", eng)

            @block.gpsimd
            def _(eng):
                run("pool", eng)

            @block.sync
            def _(eng):
                run("sp", eng)
        return len(esem) + len(dsem)


ARENA_ELEMS = 100000


class Ctx:
    def __init__(self):
        self.nc = bass.Bass("TRN2", target_bir_lowering=False)
        self.P = Prog(self.nc)
        nc, P = self.nc, self.P
        self.arena = nc.alloc_sbuf_tensor("arena", [128, ARENA_ELEMS], BF16).ap()
        self.off = 0
        self.pbank = [None] + [nc.alloc_psum_tensor(f"pb{i}", [128, 512], F32).ap() for i in range(1, 7)] + [None]
        self.pbf = {0: nc.alloc_psum_tensor("pbf0", [128, 1024], BF16).ap(), 7: nc.alloc_psum_tensor("pbf7", [128, 1024], BF16).ap()}
        self.nd = 0
        self.identf = nc.alloc_sbuf_tensor("identf", [128, 128], F32).ap()
        self.ident = nc.alloc_sbuf_tensor("ident", [128, 128], BF16).ap()
        self.cst = nc.alloc_sbuf_tensor("cst", [128, 2], F32).ap()
        self.fsb = nc.alloc_sbuf_tensor("fsb", [1, 16], F32).ap()
        self.fdr = nc.dram_tensor("fence_d", [1, 16], F32, kind="Internal").ap()
        self.Bid = P.buf("ident", keep=True)
        self.Bc = P.buf("cst", keep=True)
        self.Bf = P.buf("fence", keep=True)
        identf, ident, cst, fsb = self.identf, self.ident, self.cst, self.fsb
        P.op("pool", lambda e: e.memset(identf[:, :], 0.0), writes=[self.Bid])
        P.op("pool", lambda e: e.affine_select(out=identf[:, :], in_=identf[:, :], pattern=[[-1, 128]], compare_op=ALU.not_equal,
                                                fill=1.0, base=0, channel_multiplier=1), reads=[self.Bid], writes=[self.Bid])
        P.op("dve", lambda e: e.tensor_copy(out=ident[:, :], in_=identf[:, :]), reads=[self.Bid], writes=[self.Bid])
        P.op("pool", lambda e: e.memset(cst[:, 0:1], -0.5), writes=[self.Bc])
        P.op("pool", lambda e: e.memset(fsb[:, :], 0.0), writes=[self.Bf])

    def alloc(self, shape, dt):
        n = 1
        for s in shape[1:]:
            n *= s
        nb = n * (4 if dt == F32 else 2)
        ne = (nb + 31) // 32 * 16
        assert self.off + ne <= ARENA_ELEMS, ("arena overflow", self.off, ne)
        ap = self.arena[0:shape[0], self.off:self.off + nb // 2]
        self.off += ne
        if dt == F32:
            ap = ap.bitcast(F32)
        if len(shape) == 3:
            ap = ap.rearrange("p (a b) -> p a b", b=shape[2])
        elif len(shape) == 4:
            ap = ap.rearrange("p (a b c) -> p a b c", b=shape[2], c=shape[3])
        return ap

    def mark(self):
        return self.off

    def reset(self, to=0):
        self.off = to

    def psum_bf16(self, i):
        return self.pbf[i]

    def din(self, name, shape, dt=F32):
        return self.nc.dram_tensor(name, list(shape), dt, kind="ExternalInput").ap()

    def dout(self, name, shape, dt=F32):
        return self.nc.dram_tensor(name, list(shape), dt, kind="ExternalOutput").ap()

    def dscr(self, name, shape, dt=F32):
        return self.nc.dram_tensor(name, list(shape), dt, kind="Internal").ap()

    def fence(self):
        fsb, fdr = self.fsb, self.fdr
        self.P.barrier(lambda e: e.dma_start(out=fdr[:, :], in_=fsb[:, :]), self.Bf)


def rstd_from_ss(cx, ss_ap, out_ap, bss, bout, eps, tmp_ap):
    P = cx.P
    cst, bcst = cx.cst, cx.Bc
    P.op("dve", lambda e: e.tensor_scalar_add(out=tmp_ap, in0=ss_ap, scalar1=eps), reads=[bss], writes=[bout])
    n = ss_ap.shape[1]
    P.op("pool", lambda e: e.tensor_tensor(out=out_ap, in0=tmp_ap, in1=cst[:, 0:1].to_broadcast([128, n]), op=ALU.pow),
         reads=[bout, bcst], writes=[bout])


def load_mod(cx, modv, s, nset, ab, G, Bab, BG):
    P = cx.P
    ld = cx.alloc([8, nset * 2, 128], F32)
    Bld, Bpp = P.buf(), P.buf()
    pp = cx.pbank[1]
    for st_ in range(nset):
        for w in range(2):
            P.dma("sp", lambda e, st_=st_, w=w: e.dma_start(out=ld[:, st_ * 2 + w, :], in_=modv[st_, 3 * s + w, :].rearrange("(k p) -> k p", p=128)),
                  Bld, writes=[Bld])
    for i in range(nset * 2):
        P.op("pe", lambda e, i=i: e.transpose(out=pp[:, i * 8:(i + 1) * 8], in_=ld[:, i, :], identity=cx.identf[0:8, 0:8]),
             reads=[Bld, cx.Bid], writes=[Bpp])
    P.op("dve", lambda e: e.tensor_copy(out=ab[:, :, :, :], in_=pp[:, 0:nset * 16].rearrange("p (a b c) -> p a b c", b=2, c=8)),
         reads=[Bpp], writes=[Bab])
    for st_ in range(nset):
        if G is not None:
            P.dma("sp", lambda e, st_=st_: e.dma_start(out=G[:, st_, :], in_=modv[st_, 3 * s + 2:3 * s + 3, :].partition_broadcast(128)),
                  BG, writes=[BG])


def st_mod(cx, cT, aw, abias, gpre, gpost, modv, Bcv):
    P = cx.P
    cx.reset()
    GW = 1536
    cs = cx.alloc([128, 8, 2], F32)
    sc = cx.alloc([128, 8, 2], BF16)
    wch = [cx.alloc([128, 8, GW], BF16) for _ in range(2)]
    mod = cx.alloc([2, 9 * D], F32)
    bia = cx.alloc([2, 9 * D], F32)
    gp = cx.alloc([2, 3 * D], F32)
    gq = cx.alloc([2, 3 * D], F32)
    outv = cx.alloc([2, 9, D], F32)
    pm = [cx.pbank[1], cx.pbank[2]]
    Bcs, Bsc, Bmod, Bbia, Bgp, Bgq, Bout = [P.buf() for _ in range(7)]
    Bw, Bpm = P.bufs(2), P.bufs(2)
    P.dma("sp", lambda e: e.dma_start(out=cs[:, :, :], in_=cT[:, :, :]), Bcs, writes=[Bcs])
    P.dma("sp", lambda e: e.dma_start(out=bia[:, :], in_=abias.partition_broadcast(2)), Bbia, writes=[Bbia])
    P.dma("sp", lambda e: e.dma_start(out=gp[:, :], in_=gpre.partition_broadcast(2)), Bgp, writes=[Bgp])
    P.dma("sp", lambda e: e.dma_start(out=gq[:, :], in_=gpost.partition_broadcast(2)), Bgq, writes=[Bgq])
    P.op("act", lambda e: e.activation(out=sc[:, :, :], in_=cs[:, :, :], func=AF.Silu), reads=[Bcs], writes=[Bsc])
    ci = 0
    for g in range(9 * D // GW):
        w, Bwg = wch[g % 2], Bw[g % 2]
        for k in range(8):
            P.dma("sp", lambda e, w=w, k=k, g=g: e.dma_start(out=w[:, k, :], in_=aw[k * 128:(k + 1) * 128, g * GW:(g + 1) * GW]),
                  Bwg, reads=[Bcv], writes=[Bwg])
        for n in range(GW // 512):
            p, Bp = pm[ci % 2], Bpm[ci % 2]
            ci += 1
            for k in range(8):
                P.op("pe", lambda e, w=w, k=k, n=n, p=p: e.matmul(p[0:2, :], lhsT=sc[:, k, :], rhs=w[:, k, n * 512:(n + 1) * 512],
                                                                start=(k == 0), stop=(k == 7)), reads=[Bsc, Bwg], writes=[Bp])
            c0 = g * GW + n * 512
            P.op("dve", lambda e, p=p, c0=c0: e.tensor_tensor(out=mod[:, c0:c0 + 512], in0=p[0:2, :], in1=bia[:, c0:c0 + 512], op=ALU.add),
                 reads=[Bp, Bbia], writes=[Bmod])
    for s in range(3):
        coef = 1.0 if s == 1 else 0.5
        sh, scl, gt = mod[:, (3 * s) * D:(3 * s + 1) * D], mod[:, (3 * s + 1) * D:(3 * s + 2) * D], mod[:, (3 * s + 2) * D:(3 * s + 3) * D]
        P.op("dve", lambda e, s=s, scl=scl: e.scalar_tensor_tensor(out=outv[:, 3 * s, :], in0=scl, scalar=1.0, in1=gp[:, s * D:(s + 1) * D],
                                                                  op0=ALU.add, op1=ALU.mult), reads=[Bmod, Bgp], writes=[Bout])
        P.op("dve", lambda e, s=s, sh=sh: e.tensor_copy(out=outv[:, 3 * s + 1, :], in_=sh), reads=[Bmod], writes=[Bout])
        P.op("dve", lambda e, s=s, gt=gt, coef=coef: e.scalar_tensor_tensor(out=outv[:, 3 * s + 2, :], in0=gt, scalar=coef,
                                                                          in1=gq[:, s * D:(s + 1) * D], op0=ALU.mult, op1=ALU.mult),
             reads=[Bmod, Bgq], writes=[Bout])
    Bo = P.buf()
    P.dma("sp", lambda e: e.dma_start(out=modv[:, :, :], in_=outv[:, :, :]), Bout, reads=[Bout], writes=[Bo])
    cx.fence()


def prenorm_tile(cx, x, t, s, h, Bh, junk, Bjunk, st, Bst, xn, Bxn, pT, BpT, ab, Bab, uTp, BuTp, j):
    P = cx.P
    ident, Bid = cx.ident, cx.Bid
    P.dma("sp", lambda e: e.dma_start(out=h[:, :], in_=x[t * 128:(t + 1) * 128, :]), Bh, writes=[Bh])
    P.op("act", lambda e: e.activation(out=junk[:, :], in_=h[:, :], func=AF.Square, scale=1.0 / 32, accum_out=st[:, 0:1]),
         reads=[Bh], writes=[Bjunk, Bst])
    rstd_from_ss(cx, st[:, 0:1], st[:, 2:3], Bst, Bst, 1e-6, st[:, 1:2])
    P.op("dve", lambda e: e.tensor_scalar(out=xn[:, :], in0=h[:, :], scalar1=st[:, 2:3], scalar2=None, op0=ALU.mult),
         reads=[Bh, Bst], writes=[Bxn])
    for k in range(8):
        P.op("pe", lambda e, k=k: e.transpose(out=pT[:, k * 128:(k + 1) * 128], in_=xn[:, k * 128:(k + 1) * 128], identity=ident[:, :]),
             reads=[Bxn, Bid], writes=[BpT])
    for k in range(8):
        if k % 2 == 0:
            P.op("act", lambda e, k=k: e.activation(out=uTp[:, k, j * 128:(j + 1) * 128], in_=pT[:, k * 128:(k + 1) * 128], func=AF.Identity,
                                                    scale=ab[:, s, 0, k:k + 1], bias=ab[:, s, 1, k:k + 1]), reads=[BpT, Bab], writes=[BuTp])
        else:
            P.op("dve", lambda e, k=k: e.tensor_scalar(out=uTp[:, k, j * 128:(j + 1) * 128], in0=pT[:, k * 128:(k + 1) * 128],
                                                       scalar1=ab[:, s, 0, k:k + 1], scalar2=ab[:, s, 1, k:k + 1], op0=ALU.mult, op1=ALU.add),
                 reads=[BpT, Bab], writes=[BuTp])


def postnorm_residual(cx, pys, s3, Bs3, G, BG, s, t1, Bt1, h, Bh, junk, Bjunk):
    P = cx.P
    for half in range(2):
        py, Bp = pys[half]
        P.op("act", lambda e, py=py, half=half: e.activation(out=junk[:, 0:512], in_=py[:, :], func=AF.Square, scale=1.0 / 32,
                                                            accum_out=s3[:, 4 + half:5 + half]), reads=[Bp], writes=[Bjunk, Bs3])
    P.op("dve", lambda e: e.tensor_tensor(out=s3[:, 6:7], in0=s3[:, 4:5], in1=s3[:, 5:6], op=ALU.add), reads=[Bs3], writes=[Bs3])
    rstd_from_ss(cx, s3[:, 6:7], s3[:, 7:8], Bs3, Bs3, 1e-6, s3[:, 3:4])
    for half in range(2):
        py, Bp = pys[half]
        P.op("dve", lambda e, py=py, half=half: e.scalar_tensor_tensor(out=t1[:, half * 512:(half + 1) * 512], in0=py[:, :], scalar=s3[:, 7:8],
                                                                      in1=G[:, s, half * 512:(half + 1) * 512], op0=ALU.mult, op1=ALU.mult),
             reads=[Bp, Bs3, BG], writes=[Bt1])
    P.op("pool", lambda e: e.tensor_tensor(out=h[:, :], in0=h[:, :], in1=t1[:, :], op=ALU.add), reads=[Bh, Bt1], writes=[Bh])


def st_ffn(cx, x, y, wi, wo, modv, s, tile_sets, Bcvi, Bcvo):
    P = cx.P
    cx.reset()
    nt = len(tile_sets)
    nset = 2
    wib = cx.alloc([128, 8, 2 * DFF], BF16)
    wob = cx.alloc([128, 22, D], BF16)
    ab = cx.alloc([128, nset, 2, 8], F32)
    G = cx.alloc([128, nset, D], F32)
    hb = [cx.alloc([128, D], F32) for _ in range(4)]
    xn = [cx.alloc([128, D], BF16) for _ in range(2)]
    junk = cx.alloc([128, D], BF16)
    t1 = cx.alloc([128, D], F32)
    uT = [cx.alloc([128, 8, 256], BF16) for _ in range(2)]
    gT = cx.alloc([128, 22, 256], BF16)
    sil = [cx.alloc([128, 256], F32) for _ in range(2)]
    st = [cx.alloc([128, 8], F32) for _ in range(4)]
    pT = cx.psum_bf16(0)
    pab = [cx.pbank[1 + i] for i in range(4)]
    pY = [cx.pbank[5], cx.pbank[6], cx.pbank[5]]
    Bwib, Bwob, Bab, BG = P.buf(), P.buf(), P.buf(), P.buf()
    Bhb, Bxn, Bt1, Bsil, Bst = P.bufs(4), P.bufs(2), P.buf(), P.bufs(2), P.bufs(4)
    Bjunk, BuT, BgT, BpT = P.buf(), P.bufs(2), P.buf(), P.buf()
    Bpab, BpY = P.bufs(4), P.bufs(2)
    BpY = [BpY[0], BpY[1], BpY[0]]
    import os
    kq = int(os.environ.get("KSUB", "9"))
    if kq != -1 and kq != 0 and kq != -3:
        load_mod(cx, modv, s, nset, ab, G, Bab, BG)
    if kq != -2 and kq != 0:
        for k in range(8):
            P.dma("sp", lambda e, k=k: e.dma_start(out=wib[:, k, :], in_=wi[k * 128:(k + 1) * 128, :]), Bwib, reads=[Bcvi], writes=[Bwib])
    if kq != -2 and kq != 0 and kq != -3:
        wo_v = wo.rearrange("(k p) n -> p k n", p=128)
        for k0 in range(0, 22, 6):
            k1 = min(22, k0 + 6)
            P.dma("sp", lambda e, k0=k0, k1=k1: e.dma_start(out=wob[:, k0:k1, :], in_=wo_v[:, k0:k1, :]), Bwob, reads=[Bcvo], writes=[Bwob])
    blocks = [list(range(b0, min(nt, b0 + 2))) for b0 in range(0, nt, 2)]
    yctr = [0]

    def phase1(bi):
        par = bi % 2
        for j, t in enumerate(blocks[bi]):
            prenorm_tile(cx, x, t, tile_sets[t], hb[par * 2 + j], Bhb[par * 2 + j], junk, Bjunk, st[j], Bst[j], xn[j], Bxn[j],
                         pT, BpT, ab, Bab, uT[par], BuT[par], j)

    def phase2(bi):
        par = bi % 2
        N = len(blocks[bi]) * 128
        u = uT[par]
        for m in range(22):
            pa, pb = pab[(m % 2) * 2], pab[(m % 2) * 2 + 1]
            Ba, Bb = Bpab[(m % 2) * 2], Bpab[(m % 2) * 2 + 1]
            for k in range(8):
                P.op("pe", lambda e, k=k, m=m, pa=pa: e.matmul(pa[:, 0:N], lhsT=wib[:, k, m * 128:(m + 1) * 128], rhs=u[:, k, 0:N],
                                                             start=(k == 0), stop=(k == 7)), reads=[Bwib, BuT[par]], writes=[Ba])
            for k in range(8):
                P.op("pe", lambda e, k=k, m=m, pb=pb: e.matmul(pb[:, 0:N], lhsT=wib[:, k, DFF + m * 128:DFF + (m + 1) * 128], rhs=u[:, k, 0:N],
                                                             start=(k == 0), stop=(k == 7)), reads=[Bwib, BuT[par]], writes=[Bb])
            sl, Bs = sil[m % 2], Bsil[m % 2]
            P.op("act", lambda e, pa=pa, sl=sl: e.activation(out=sl[:, 0:N], in_=pa[:, 0:N], func=AF.Silu), reads=[Ba], writes=[Bs])
            P.op("dve", lambda e, pb=pb, sl=sl, m=m: e.tensor_tensor(out=gT[:, m, 0:N], in0=sl[:, 0:N], in1=pb[:, 0:N], op=ALU.mult),
                 reads=[Bs, Bb], writes=[BgT])

    def phase3(bi):
        par = bi % 2
        for j, t in enumerate(blocks[bi]):
            h, Bh = hb[par * 2 + j], Bhb[par * 2 + j]
            pys = []
            for half in range(2):
                py, Bp = pY[yctr[0] % 2], BpY[yctr[0] % 2]
                yctr[0] += 1
                pys.append((py, Bp))
                for k in range(22):
                    P.op("pe", lambda e, k=k, j=j, half=half, py=py: e.matmul(py[:, :], lhsT=gT[:, k, j * 128:(j + 1) * 128],
                                                                            rhs=wob[:, k, half * 512:(half + 1) * 512],
                                                                            start=(k == 0), stop=(k == 21)), reads=[BgT, Bwob], writes=[Bp])
            postnorm_residual(cx, pys, st[2 + j], Bst[2 + j], G, BG, tile_sets[t], t1, Bt1, h, Bh, junk, Bjunk)
            Bo = P.buf()
            P.dma("sp", lambda e, h=h, t=t: e.dma_start(out=y[t * 128:(t + 1) * 128, :], in_=h[:, :]), Bh, reads=[Bh], writes=[Bo])

    nb = len(blocks)
    import os
    ksub = int(os.environ.get("KSUB", "9"))
    if ksub >= 2:
        phase1(0)
    if ksub >= 3:
        phase2(0)
    if ksub >= 4:
        for bi in range(nb):
            if bi > 0:
                phase2(bi)
            if bi + 1 < nb:
                phase1(bi + 1)
            phase3(bi)
    cx.fence()


def st_out(cx, hd, MIX, wo, modv, tile_sets, Bcv):
    P = cx.P
    cx.reset()
    nt = len(tile_sets)
    nset = 2
    ident, Bid = cx.ident, cx.Bid
    wob = cx.alloc([128, 8, D], BF16)
    ab = cx.alloc([128, nset, 2, 8], F32)
    G = cx.alloc([128, nset, D], F32)
    hb = [cx.alloc([128, D], F32) for _ in range(3)]
    mx = [cx.alloc([128, D], BF16) for _ in range(3)]
    mt = [cx.alloc([128, 8, 128], BF16) for _ in range(2)]
    t1 = cx.alloc([128, D], F32)
    junk = cx.alloc([128, 512], BF16)
    st = [cx.alloc([128, 8], F32) for _ in range(2)]
    pT = [cx.psum_bf16(0), cx.psum_bf16(7)]
    pY = [cx.pbank[1 + i] for i in range(4)]
    Bwob, Bab, BG, Bjunk, Bt1 = P.buf(), P.buf(), P.buf(), P.buf(), P.buf()
    Bhb, Bmx, Bmt, Bst, BpT, BpY = P.bufs(3), P.bufs(3), P.bufs(2), P.bufs(2), P.bufs(2), P.bufs(4)
    load_mod(cx, modv, 1, nset, ab, G, Bab, BG)
    P.dma("sp", lambda e: e.dma_start(out=wob[:, :, :], in_=wo.rearrange("(k p) n -> p k n", p=128)), Bwob, reads=[Bcv], writes=[Bwob])

    def load(t):
        P.dma("sp", lambda e: e.dma_start(out=hb[t % 3][:, :], in_=hd[t * 128:(t + 1) * 128, :]), Bhb[t % 3], writes=[Bhb[t % 3]])
        P.dma("sp", lambda e: e.dma_start(out=mx[t % 3][:, :], in_=MIX[t * 128:(t + 1) * 128, :]), Bmx[t % 3], writes=[Bmx[t % 3]])

    load(0)
    if nt > 1:
        load(1)
    for t in range(nt):
        if t + 2 < nt:
            load(t + 2)
        s = tile_sets[t]
        h, Bh = hb[t % 3], Bhb[t % 3]
        m, Bm = mt[t % 2], Bmt[t % 2]
        p_t, Bp_t = pT[t % 2], BpT[t % 2]
        for k in range(8):
            P.op("pe", lambda e, k=k, t=t, p_t=p_t: e.transpose(out=p_t[:, k * 128:(k + 1) * 128], in_=mx[t % 3][:, k * 128:(k + 1) * 128],
                                                              identity=ident[:, :]), reads=[Bmx[t % 3], Bid], writes=[Bp_t])
        P.op("act", lambda e, m=m, p_t=p_t: e.activation(out=m[:, :, :], in_=p_t[:, 0:1024].rearrange("p (k n) -> p k n", n=128), func=AF.Copy),
             reads=[Bp_t], writes=[Bm])
        pys = []
        for half in range(2):
            py, Bp = pY[(2 * t + half) % 4], BpY[(2 * t + half) % 4]
            pys.append((py, Bp))
            for k in range(8):
                P.op("pe", lambda e, k=k, half=half, py=py, m=m: e.matmul(py[:, :], lhsT=m[:, k, :], rhs=wob[:, k, half * 512:(half + 1) * 512],
                                                                        start=(k == 0), stop=(k == 7)), reads=[Bm, Bwob], writes=[Bp])
        postnorm_residual(cx, pys, st[t % 2], Bst[t % 2], G, BG, s, t1, Bt1, h, Bh, junk, Bjunk)
        Bo = P.buf()
        P.dma("sp", lambda e, h=h, t=t: e.dma_start(out=hd[t * 128:(t + 1) * 128, :], in_=h[:, :]), Bh, reads=[Bh], writes=[Bo])
    cx.fence()


def st_s2(cx, hd, wext, modv, tab, gn, FM, VA, VR, GG, KTOK, tile_sets, Bcv):
    P = cx.P
    cx.reset()
    nt = len(tile_sets)
    nset = 2
    ident, Bid = cx.ident, cx.Bid
    wb = cx.alloc([128, 8, NCX], BF16)
    ab = cx.alloc([128, nset, 2, 8], F32)
    gnb = cx.alloc([128, 512], F32)
    hb = [cx.alloc([128, D], F32) for _ in range(4)]
    xn = [cx.alloc([128, D], BF16) for _ in range(2)]
    junk = cx.alloc([128, D], BF16)
    uT = [cx.alloc([128, 8, 256], BF16) for _ in range(2)]
    tb = [cx.alloc([128, 6, 256], F32) for _ in range(2)]
    r1 = [cx.alloc([128, 256], F32) for _ in range(2)]
    r2 = [cx.alloc([128, 256], F32) for _ in range(2)]
    fmo = [cx.alloc([128, 256], BF16) for _ in range(3)]
    kto = [cx.alloc([128, 256], BF16) for _ in range(2)]
    vao = [cx.alloc([128, 128], BF16) for _ in range(2)]
    vro = [cx.alloc([128, 512], BF16) for _ in range(2)]
    gs = [cx.alloc([128, 512], F32) for _ in range(2)]
    ggo = [cx.alloc([128, 512], F32) for _ in range(2)]
    st = [cx.alloc([128, 8], F32) for _ in range(2)]
    pT = cx.psum_bf16(0)
    pxp = [cx.pbank[1 + i] for i in range(4)]
    ptm = [cx.pbank[5 + i] for i in range(2)]
    pkt = cx.psum_bf16(7)
    Bwb, Bab, Bgn = P.buf(), P.buf(), P.buf()
    Bhb, Bxn, BuT, Btb = P.bufs(4), P.bufs(2), P.bufs(2), P.bufs(2)
    Br1, Br2, Bfmo, Bkto, Bvao, Bvro, Bgs, Bggo, Bst = P.bufs(2), P.bufs(2), P.bufs(3), P.bufs(2), P.bufs(2), P.bufs(2), P.bufs(2), P.bufs(2), P.bufs(2)
    Bjunk, BpT, Bpkt = P.buf(), P.buf(), P.buf()
    Bpxp, Bptm = P.bufs(4), P.bufs(2)
    load_mod(cx, modv, 1, nset, ab, None, Bab, None)
    P.dma("sp", lambda e: e.dma_start(out=gnb[:, :], in_=gn.partition_broadcast(128)), Bgn, writes=[Bgn])
    for k in range(8):
        P.dma("sp", lambda e, k=k: e.dma_start(out=wb[:, k, :], in_=wext[k * 128:(k + 1) * 128, :]), Bwb, reads=[Bcv], writes=[Bwb])
    blocks = [list(range(b0, min(nt, b0 + 2))) for b0 in range(0, nt, 2)]
    ctr = {"fm": 0, "tm": 0, "o": 0, "k": 0}

    def phase1(bi):
        par = bi % 2
        tiles = blocks[bi]
        for j, t in enumerate(tiles):
            prenorm_tile(cx, hd, t, tile_sets[t], hb[par * 2 + j], Bhb[par * 2 + j], junk, Bjunk, st[j], Bst[j], xn[j], Bxn[j],
                         pT, BpT, ab, Bab, uT[par], BuT[par], j)
        t0 = tiles[0] * 128
        N = len(tiles) * 128
        P.dma("sp", lambda e: e.dma_start(out=tb[par][:, :, 0:N], in_=tab[:, :, t0:t0 + N].rearrange("s p t -> p s t")),
              Btb[par], writes=[Btb[par]])

    def phase2(bi):
        tiles = blocks[bi]
        par = bi % 2
        N = len(tiles) * 128
        t0 = tiles[0] * 128
        u = uT[par]
        for i in range(NFM):
            q = ctr["fm"] % 2
            ctr["fm"] += 1
            px, pp = pxp[q * 2], pxp[q * 2 + 1]
            Bx, Bp = Bpxp[q * 2], Bpxp[q * 2 + 1]
            for k in range(8):
                P.op("pe", lambda e, k=k, i=i, px=px: e.matmul(px[:, 0:N], lhsT=wb[:, k, (2 * i) * 128:(2 * i + 1) * 128], rhs=u[:, k, 0:N],
                                                             start=(k == 0), stop=(k == 7)), reads=[Bwb, BuT[par]], writes=[Bx])
            for k in range(8):
                P.op("pe", lambda e, k=k, i=i, pp=pp: e.matmul(pp[:, 0:N], lhsT=wb[:, k, (2 * i + 1) * 128:(2 * i + 2) * 128], rhs=u[:, k, 0:N],
                                                             start=(k == 0), stop=(k == 7)), reads=[Bwb, BuT[par]], writes=[Bp])
            tp = PAIR_TAB[i]
            P.op("dve", lambda e, px=px, q=q, tp=tp: e.tensor_tensor(out=r1[q][:, 0:N], in0=px[:, 0:N], in1=tb[par][:, 2 * tp, 0:N], op=ALU.mult),
                 reads=[Bx, Btb[par]], writes=[Br1[q]])
            P.op("dve", lambda e, pp=pp, q=q, tp=tp: e.tensor_tensor(out=r2[q][:, 0:N], in0=pp[:, 0:N], in1=tb[par][:, 2 * tp + 1, 0:N], op=ALU.mult),
                 reads=[Bp, Btb[par]], writes=[Br2[q]])
            o = ctr["o"] % 3
            ctr["o"] += 1
            P.op("pool", lambda e, q=q, o=o: e.tensor_tensor(out=fmo[o][:, 0:N], in0=r1[q][:, 0:N], in1=r2[q][:, 0:N], op=ALU.add),
                 reads=[Br1[q], Br2[q]], writes=[Bfmo[o]])
            Bo = P.buf()
            P.dma("sp", lambda e, o=o, i=i: e.dma_start(out=FM[i, :, t0:t0 + N], in_=fmo[o][:, 0:N]), Bfmo[o], reads=[Bfmo[o]], writes=[Bo])
            if i >= 7:
                kq = ctr["k"] % 2
                ctr["k"] += 1
                for j in range(len(tiles)):
                    P.op("pe", lambda e, o=o, j=j: e.transpose(out=pkt[:, j * 128:(j + 1) * 128], in_=fmo[o][:, j * 128:(j + 1) * 128],
                                                              identity=ident[:, :]), reads=[Bfmo[o], Bid], writes=[Bpkt])
                P.op("act", lambda e, kq=kq: e.activation(out=kto[kq][:, 0:N], in_=pkt[:, 0:N], func=AF.Copy), reads=[Bpkt], writes=[Bkto[kq]])
                Bo = P.buf()
                c0k = (i - 7) * 128
                P.dma("sp", lambda e, kq=kq, c0k=c0k: e.dma_start(out=KTOK[t0:t0 + N, c0k:c0k + 128].rearrange("(j p) d -> p j d", p=128),
                                                                 in_=kto[kq][:, 0:N].rearrange("p (j d) -> p j d", d=128)),
                      Bkto[kq], reads=[Bkto[kq]], writes=[Bo])
        c0 = 18 * 128
        for j, t in enumerate(tiles):
            def tm(cols, ncol, j=j):
                q = ctr["tm"] % 2
                ctr["tm"] += 1
                p, Bp_ = ptm[q], Bptm[q]
                for k in range(8):
                    P.op("pe", lambda e, k=k, p=p, j=j: e.matmul(p[:, 0:ncol], lhsT=u[:, k, j * 128:(j + 1) * 128], rhs=wb[:, k, cols:cols + ncol],
                                                               start=(k == 0), stop=(k == 7)), reads=[Bwb, BuT[par]], writes=[Bp_])
                return p, Bp_
            r0 = t * 128
            p, Bp_ = tm(c0, 128)
            P.op("act", lambda e, p=p, j=j: e.activation(out=vao[j][:, :], in_=p[:, 0:128], func=AF.Copy), reads=[Bp_], writes=[Bvao[j]])
            Bo = P.buf()
            P.dma("sp", lambda e, j=j, r0=r0: e.dma_start(out=VA[r0:r0 + 128, :], in_=vao[j][:, :]), Bvao[j], reads=[Bvao[j]], writes=[Bo])
            p, Bp_ = tm(c0 + 128, 512)
            P.op("act", lambda e, p=p, j=j: e.activation(out=vro[j][:, :], in_=p[:, :], func=AF.Copy), reads=[Bp_], writes=[Bvro[j]])
            Bo = P.buf()
            P.dma("sp", lambda e, j=j, r0=r0: e.dma_start(out=VR[r0:r0 + 128, :], in_=vro[j][:, :]), Bvro[j], reads=[Bvro[j]], writes=[Bo])
            p, Bp_ = tm(c0 + 640, 512)
            P.op("act", lambda e, p=p, j=j: e.activation(out=gs[j][:, :], in_=p[:, :], func=AF.Silu), reads=[Bp_], writes=[Bgs[j]])
            P.op("dve", lambda e, j=j: e.tensor_tensor(out=ggo[j][:, :], in0=gs[j][:, :], in1=gnb[:, :], op=ALU.mult),
                 reads=[Bgs[j], Bgn], writes=[Bggo[j]])
            Bo = P.buf()
            P.dma("sp", lambda e, j=j, r0=r0: e.dma_start(out=GG[r0:r0 + 128, :], in_=ggo[j][:, :]), Bggo[j], reads=[Bggo[j]], writes=[Bo])

    nb = len(blocks)
    phase1(0)
    for bi in range(nb):
        if bi + 1 < nb:
            phase1(bi + 1)
        phase2(bi)
    cx.fence()


O_DP, O_DN, O_MF, O_MB, O_XI, O_ZE, O_WC, CF_W = 0, 128, 256, 384, 640, 1152, 1154, 1158


def st_s3(cx, l_idx, Tl, FM, VA, VR, GG, KTOK, MIX, dec3_d, sink_d, cf_d, tri_d, flags_d, EXPS, EXPB, GATS, GATB, groups_cc, sub=9):
    P = cx.P
    nq = Tl // 128
    NCH = nq + 2
    T = NCH * 128
    TH = T + 256
    f, b = slice(0, 64), slice(64, 128)
    cgroups = [list(range(g0, min(g0 + 4, nq))) for g0 in range(0, nq, 4)] + [[nq, nq + 1]]

    cx.reset()
    KVall = cx.alloc([128, 4, NCH * 128], F32)
    S0all = cx.alloc([128, 4, 128], F32)
    L3 = cx.alloc([128, 12], F32)
    mark = cx.mark()
    BKV = [P.buf(keep=True) for _ in range(4)]
    BS0 = [P.buf(keep=True) for _ in range(4)]
    BL3 = P.buf(keep=True)
    cf = cx.alloc([128, CF_W], F32)
    dec3 = cx.alloc([128, 12], F32)
    ltmp = cx.alloc([128, 12], F32)
    ktok = [cx.alloc([128, NCH, 64], BF16) for _ in range(2)]
    vr = [cx.alloc([128, NCH, 128], BF16) for _ in range(2)]
    ZETA = [cx.alloc([128, 2], F32) for _ in range(2)]
    DEC = [cx.alloc([128, 1], F32) for _ in range(2)]
    WC = [cx.alloc([128, 2, 2], F32) for _ in range(2)]
    KZg = [cx.alloc([128, 4, 128], BF16) for _ in range(2)]
    KZc = [cx.alloc([128, 2, 128], BF16) for _ in range(2)]
    R = [cx.alloc([128, 128], F32) for _ in range(2)]
    Bcf, Bdec = P.buf(), P.buf()
    Bktok, Bvr, BZ, BDEC, BWC, BKZg, BKZc, BR = [P.bufs(2) for _ in range(8)]
    pk = [cx.pbank[1], cx.pbank[2]]
    p0 = [cx.pbank[3], cx.pbank[4]]
    Bpk, Bp0 = P.bufs(2), P.bufs(2)
    Bexs, Bexb, Bgs, Bgb = P.buf(), P.buf(), P.buf(), P.buf()

    P.dma("sp", lambda e: e.dma_start(out=cf[:, :], in_=cf_d[:, :]), Bcf, writes=[Bcf])
    P.dma("sp", lambda e: e.dma_start(out=dec3[:, :], in_=dec3_d.rearrange("p a b -> p (a b)")), Bdec, writes=[Bdec])
    P.op("act", lambda e: e.activation(out=ltmp[:, :], in_=dec3[:, :], func=AF.Exp, scale=-1.0), reads=[Bdec], writes=[BL3])
    P.op("dve", lambda e: e.tensor_scalar_add(out=ltmp[:, :], in0=ltmp[:, :], scalar1=1.0), reads=[BL3], writes=[BL3])
    P.op("act", lambda e: e.activation(out=L3[:, :], in_=ltmp[:, :], func=AF.Ln), reads=[BL3], writes=[BL3])
    P.op("dve", lambda e: e.tensor_scalar_mul(out=L3[:, :], in0=L3[:, :], scalar1=-1.0), reads=[BL3], writes=[BL3])
    import os
    ksa = int(os.environ.get("KSA", "0"))
    if ksa < 2:
        for i_, (src) in enumerate([FM[4, :, 0:128], FM[4, :, Tl - 128:Tl], VA[0:128, :], VA[Tl - 128:Tl, :]]):
            P.dma("sp", lambda e, i_=i_, src=src: e.dma_start(out=EXPB[i_ * 128:(i_ + 1) * 128, :], in_=src), Bexb, writes=[Bexb])
    kvq = 0
    wcv = cf[:, O_WC:O_WC + 4].rearrange("p (c d) -> p c d", d=2)
    for hr in (range(4) if ksa != 3 else []):
        q = hr % 2
        kt_, vr_ = ktok[q], vr[q]
        P.dma("sp", lambda e, hr=hr, kt_=kt_: e.dma_start(out=kt_[:, :, :], in_=KTOK[:, hr * 64:(hr + 1) * 64].rearrange("(n p) d -> p n d", p=128)),
              Bktok[q], writes=[Bktok[q]])
        P.dma("sp", lambda e, hr=hr, vr_=vr_: e.dma_start(out=vr_[:, :, :], in_=VR[:, hr * 128:(hr + 1) * 128].rearrange("(n p) d -> p n d", p=128)),
              Bvr[q], writes=[Bvr[q]])
        lf, lb, ls = L3[:, 4 + hr:5 + hr], L3[:, 8 + hr:9 + hr], L3[:, hr:hr + 1]
        P.op("act", lambda e, q=q, lf=lf: e.activation(out=ZETA[q][:, 0:1], in_=cf[:, O_ZE:O_ZE + 1], func=AF.Exp, scale=lf), reads=[Bcf, BL3], writes=[BZ[q]])
        P.op("act", lambda e, q=q, lb=lb: e.activation(out=ZETA[q][:, 1:2], in_=cf[:, O_ZE + 1:O_ZE + 2], func=AF.Exp, scale=lb), reads=[Bcf, BL3], writes=[BZ[q]])
        P.op("act", lambda e, q=q, ls=ls: e.activation(out=DEC[q][:, :], in_=ls, func=AF.Exp, scale=128.0), reads=[BL3], writes=[BDEC[q]])
        P.op("act", lambda e, q=q, lf=lf: e.activation(out=WC[q][:, :, 0], in_=wcv[:, :, 0], func=AF.Exp, scale=lf), reads=[Bcf, BL3], writes=[BWC[q]])
        P.op("act", lambda e, q=q, lb=lb: e.activation(out=WC[q][:, :, 1], in_=wcv[:, :, 1], func=AF.Exp, scale=lb), reads=[Bcf, BL3], writes=[BWC[q]])
        if ksa == 6:
            continue
        for c in range(2):
            for d_ in range(2):
                P.op("dve", lambda e, c=c, d_=d_, q=q, kt_=kt_: e.tensor_scalar(out=KZc[q][:, c, d_ * 64:(d_ + 1) * 64], in0=kt_[:, nq + c, :],
                                                                             scalar1=WC[q][:, c, d_:d_ + 1], scalar2=None, op0=ALU.mult),
                     reads=[Bktok[q], BWC[q]], writes=[BKZc[q]])
        for c in range(2):
            P.op("pe", lambda e, c=c, q=q, vr_=vr_: e.matmul(p0[q][:, 0:128], lhsT=KZc[q][:, c, :], rhs=vr_[:, nq + c, :], start=(c == 0), stop=(c == 1)),
                 reads=[BKZc[q], Bvr[q]], writes=[Bp0[q]])
        if ksa == 7:
            continue
        P.op("act", lambda e, q=q, hr=hr: e.activation(out=S0all[:, hr, :], in_=p0[q][:, 0:128], func=AF.Copy), reads=[Bp0[q]], writes=[BS0[hr]])
        P.op("dve", lambda e, q=q, hr=hr: e.tensor_copy(out=R[q][:, :], in_=S0all[:, hr, :]), reads=[BS0[hr]], writes=[BR[q]])
        if ksa == 8:
            continue
        for grp in (cgroups if ksa not in (5, 6, 7, 8) else []):
            g0, ng = grp[0], len(grp)
            kz, Bkz = KZg[kvq % 2], BKZg[kvq % 2]
            pkk, Bpkk = pk[kvq % 2], Bpk[kvq % 2]
            kvq += 1
            for d_ in range(2):
                P.op("dve", lambda e, kz=kz, g0=g0, ng=ng, d_=d_, q=q, kt_=kt_: e.tensor_scalar(out=kz[:, 0:ng, d_ * 64:(d_ + 1) * 64], in0=kt_[:, g0:g0 + ng, :],
                                                                                            scalar1=ZETA[q][:, d_:d_ + 1], scalar2=None, op0=ALU.mult),
                     reads=[Bktok[q], BZ[q]], writes=[Bkz])
            for i, n in enumerate(grp):
                P.op("pe", lambda e, kz=kz, i=i, n=n, pkk=pkk, vr_=vr_: e.matmul(pkk[:, i * 128:(i + 1) * 128], lhsT=kz[:, i, :], rhs=vr_[:, n, :],
                                                                              start=True, stop=True), reads=[Bkz, Bvr[q]], writes=[Bpkk])
            P.op("act", lambda e, pkk=pkk, g0=g0, ng=ng, hr=hr: e.activation(out=KVall[:, hr, g0 * 128:(g0 + ng) * 128], in_=pkk[:, 0:ng * 128], func=AF.Copy),
                 reads=[Bpkk], writes=[BKV[hr]])
        for n in (range(nq) if ksa not in (4, 5, 6) else []):
            P.op("dve", lambda e, n=n, q=q, hr=hr: e.scalar_tensor_tensor(out=R[q][f, :], in0=R[q][f, :], scalar=DEC[q][f, 0:1], in1=KVall[f, hr, n * 128:(n + 1) * 128],
                                                                        op0=ALU.mult, op1=ALU.add), reads=[BR[q], BDEC[q], BKV[hr]], writes=[BR[q]])
            m = nq - 1 - n
            P.op("dve", lambda e, m=m, q=q, hr=hr: e.scalar_tensor_tensor(out=R[q][b, :], in0=R[q][b, :], scalar=DEC[q][b, 0:1], in1=KVall[b, hr, m * 128:(m + 1) * 128],
                                                                        op0=ALU.mult, op1=ALU.add), reads=[BR[q], BDEC[q], BKV[hr]], writes=[BR[q]])
        P.dma("sp", lambda e, q=q, hr=hr: e.dma_start(out=EXPS[hr * 128:(hr + 1) * 128, :], in_=R[q][:, :]), BR[q], reads=[BR[q]], writes=[Bexs])
    if ksa < 1:
        P.dma("pool", lambda e: e.collective_compute("AllGather", ALU.bypass, replica_groups=groups_cc, ins=[EXPS[:, :]], outs=[GATS[:, :]]),
              Bgs, reads=[Bexs], writes=[Bgs], inc=1)
        P.dma("pool", lambda e: e.collective_compute("AllGather", ALU.bypass, replica_groups=groups_cc, ins=[EXPB[:, :]], outs=[GATB[:, :]]),
              Bgb, reads=[Bexb], writes=[Bgb], inc=1)
    cx.fence()
    if sub <= 1:
        return

    cx.reset(mark)
    qa = cx.alloc([128, 4, T], BF16)
    kaP = cx.alloc([128, 4, TH], BF16)
    vaug = cx.alloc([128, NCH + 2, 2, 65], BF16)
    Pt = [cx.alloc([128, 5, 512], BF16) for _ in range(2)]
    tri = cx.alloc([128, 4, 512], BF16)
    sinkb = cx.alloc([128, 8], F32)
    ES = cx.alloc([128, 8], F32)
    den = [cx.alloc([128, 8], F32) for _ in range(2)]
    mixa = [cx.alloc([128, 256], BF16) for _ in range(2)]
    Bqa, Bka, Bva, Btri, Bsink, BES, BpO = [P.buf() for _ in range(7)]
    BPt, Bden, Bmixa = P.bufs(2), P.bufs(2), P.bufs(2)
    pS = [cx.pbank[1 + i] for i in range(5)]
    pOs = [cx.pbank[6], cx.pbank[6]]
    BpS, BpOs = P.bufs(5), P.bufs(1) * 2
    P.dma("sp", lambda e: e.dma_start(out=tri[:, :, :], in_=tri_d[:, :, :]), Btri, writes=[Btri])
    P.dma("sp", lambda e: e.dma_start(out=sinkb[:, :], in_=sink_d[:, :]), Bsink, writes=[Bsink])
    P.op("act", lambda e: e.activation(out=ES[:, :], in_=sinkb[:, :], func=AF.Exp), reads=[Bsink], writes=[BES])
    for c in range(4):
        P.dma("sp", lambda e, c=c: e.dma_start(out=qa[:, c, :], in_=FM[c]), Bqa, writes=[Bqa])
    P.op("pool", lambda e: e.memset(kaP[:, :, :], 0.0), writes=[Bka])
    for kv in range(2):
        for r in range(2):
            v = kv * 2 + r
            ps_ = slice(r * 64, (r + 1) * 64)
            P.dma("sp", lambda e, v=v, ps_=ps_, kv=kv: e.dma_start(out=kaP[ps_, v, 0:T], in_=FM[4, kv * 64:(kv + 1) * 64, :]), Bka, writes=[Bka])
            P.dma("sp", lambda e, v=v, ps_=ps_, kv=kv: e.dma_start(out=kaP[ps_, v, T:T + 128], in_=GATB[128 + kv * 64:128 + (kv + 1) * 64, :]), Bka, writes=[Bka])
            P.dma("sp", lambda e, v=v, ps_=ps_, kv=kv: e.dma_start(out=kaP[ps_, v, T + 128:T + 256], in_=GATB[512 + kv * 64:512 + (kv + 1) * 64, :]), Bka, writes=[Bka])
    P.op("pool", lambda e: e.memset(vaug[:, :, :, 64:65], 1.0), writes=[Bva])
    for kv in range(2):
        P.dma("sp", lambda e, kv=kv: e.dma_start(out=vaug[:, 0:NCH, kv, 0:64], in_=VA[:, kv * 64:(kv + 1) * 64].rearrange("(n p) d -> p n d", p=128)), Bva, writes=[Bva])
        P.dma("sp", lambda e, kv=kv: e.dma_start(out=vaug[:, NCH, kv, 0:64], in_=GATB[384:512, kv * 64:(kv + 1) * 64]), Bva, writes=[Bva])
        P.dma("sp", lambda e, kv=kv: e.dma_start(out=vaug[:, NCH + 1, kv, 0:64], in_=GATB[512 + 256:512 + 384, kv * 64:(kv + 1) * 64]), Bva, writes=[Bva])
    actr = [0]

    def attn_tile(qo, g, keys, out_row):
        i0 = actr[0]
        actr[0] += 1
        pt, Bpt = Pt[i0 % 2], BPt[i0 % 2]
        pO, BpO_ = pOs[i0 % 2], BpOs[i0 % 2]
        nk = len(keys)
        for i, (ko, ci, mk) in enumerate(keys):
            ps, Bps = pS[i], BpS[i]
            for a in range(4):
                c, r = 2 * g + a // 2, a % 2
                P.op("pe", lambda e, ps=ps, a=a, c=c, r=r, ko=ko: e.matmul(ps[:, a * 128:(a + 1) * 128], lhsT=kaP[:, g * 2 + r, ko:ko + 128],
                                                                          rhs=qa[:, c, qo:qo + 128], start=True, stop=True), reads=[Bka, Bqa], writes=[Bps])
            P.op("act", lambda e, ps=ps, i=i: e.activation(out=pt[:, i, :], in_=ps[:, :], func=AF.Exp, scale=0.125), reads=[Bps], writes=[Bpt])
            if mk is not None:
                P.op("dve", lambda e, i=i, mk=mk: e.tensor_tensor(out=pt[:, i, :], in0=pt[:, i, :], in1=tri[:, mk, :], op=ALU.mult),
                     reads=[Bpt, Btri], writes=[Bpt])
        for a in range(4):
            for i, (ko, ci, mk) in enumerate(keys):
                P.op("pe", lambda e, a=a, i=i, ci=ci: e.matmul(pO[:, a * 128:a * 128 + 65], lhsT=pt[:, i, a * 128:(a + 1) * 128], rhs=vaug[:, ci, g, :],
                                                             start=(i == 0), stop=(i == nk - 1)), reads=[Bpt, Bva], writes=[BpO_])
        dn, Bdn = den[i0 % 2], Bden[i0 % 2]
        mo, Bmo = mixa[i0 % 2], Bmixa[i0 % 2]
        pov = pO[:, :].rearrange("p (a e) -> p a e", e=128)
        P.op("dve", lambda e: e.tensor_tensor(out=dn[:, 0:4], in0=pov[:, :, 64], in1=ES[:, g * 4:(g + 1) * 4], op=ALU.add), reads=[BpO_, BES], writes=[Bdn])
        P.op("dve", lambda e: e.reciprocal(out=dn[:, 4:8], in_=dn[:, 0:4]), reads=[Bdn], writes=[Bdn])
        for a in range(4):
            P.op("act", lambda e, a=a: e.activation(out=mo[:, a * 64:(a + 1) * 64], in_=pO[:, a * 128:a * 128 + 64], func=AF.Identity, scale=dn[:, 4 + a:5 + a]),
                 reads=[BpO_, Bdn], writes=[Bmo])
        Bo = P.buf()
        P.dma("sp", lambda e: e.dma_start(out=MIX[out_row:out_row + 128, g * 256:(g + 1) * 256], in_=mo[:, :]), Bmo, reads=[Bmo], writes=[Bo])

    ctxk = [(Tl + c * 128, nq + c, None) for c in range(2)]
    for n in range(nq):
        prev = ((n - 1) * 128, n - 1, 0) if n > 0 else (T, NCH, 2)
        nxt = ((n + 1) * 128, n + 1, 1) if n < nq - 1 else (T + 128, NCH + 1, 3)
        for g in range(2):
            attn_tile(n * 128, g, [prev, (n * 128, n, None), nxt] + ctxk, n * 128)
    for c in range(2):
        for g in range(2):
            attn_tile(Tl + c * 128, g, ctxk, Tl + c * 128)
    cx.fence()
    if sub <= 2:
        return

    cx.reset(mark)
    cf2 = cx.alloc([128, CF_W], F32)
    flg = cx.alloc([128, 2], F32)
    qd = [cx.alloc([128, T], BF16) for _ in range(2)]
    kt2 = [cx.alloc([64, T], BF16) for _ in range(2)]
    vr2 = [cx.alloc([128, NCH, 128], BF16) for _ in range(2)]
    STb = cx.alloc([128, NCH, 128], BF16)
    XI4 = cx.alloc([128, 512], F32)
    MT = cx.alloc([128, 128], F32)
    E2 = cx.alloc([128, 128], F32)
    DEC2 = cx.alloc([128, 1], F32)
    X = cx.alloc([128, 128], F32)
    Rr = [cx.alloc([128, 128], F32) for _ in range(2)]
    QXg = [cx.alloc([128, 512], BF16) for _ in range(2)]
    Wt = [cx.alloc([128, 128], BF16) for _ in range(2)]
    gst = [cx.alloc([128, 24], F32) for _ in range(2)]
    ggt = [cx.alloc([128, 4, 128], F32) for _ in range(2)]
    tn = [cx.alloc([128, 4, 128], F32) for _ in range(2)]
    mixr = [cx.alloc([128, 4, 128], BF16) for _ in range(2)]
    junk = cx.alloc([128, 128], BF16)
    Bcf2, Bflg, BSTb, BXI, BMT, BE2, BDEC2, BX, Bjunk = [P.buf() for _ in range(9)]
    Bqd, Bkt2, Bvr2, BRr, BQXg, BWt, Bgst, Bggt, Btn, Bmixr = [P.bufs(2) for _ in range(10)]
    psc = [cx.pbank[1], cx.pbank[2]]
    pout = [cx.pbank[3], cx.pbank[4]]
    Bpsc, Bpout = P.bufs(2), P.bufs(2)
    P.dma("sp", lambda e: e.dma_start(out=cf2[:, :], in_=cf_d[:, :]), Bcf2, writes=[Bcf2])
    P.dma("sp", lambda e: e.dma_start(out=flg[:, :], in_=flags_d[:, :]), Bflg, writes=[Bflg])
    scq, oq = 0, 0
    for hr in range(4):
        q = hr % 2
        c, r = hr // 2, hr % 2
        for half in range(2):
            P.dma("sp", lambda e, q=q, c=c, r=r, half=half: e.dma_start(out=qd[q][half * 64:(half + 1) * 64, :], in_=FM[5 + c, r * 64:(r + 1) * 64, :]),
                  Bqd[q], writes=[Bqd[q]])
        P.dma("sp", lambda e, q=q, c=c, r=r: e.dma_start(out=kt2[q][:, :], in_=FM[7 + c, r * 64:(r + 1) * 64, :]), Bkt2[q], writes=[Bkt2[q]])
        P.dma("sp", lambda e, q=q, hr=hr: e.dma_start(out=vr2[q][:, :, :], in_=VR[:, hr * 128:(hr + 1) * 128].rearrange("(n p) d -> p n d", p=128)),
              Bvr2[q], writes=[Bvr2[q]])
        P.dma("sp", lambda e, hr=hr: e.dma_start(out=X[f, :], in_=GATS[hr * 128:hr * 128 + 64, :]), BX, writes=[BX])
        P.dma("sp", lambda e, hr=hr: e.dma_start(out=X[b, :], in_=GATS[512 + hr * 128 + 64:512 + (hr + 1) * 128, :]), BX, writes=[BX])
        lf, lb, ls = L3[:, 4 + hr:5 + hr], L3[:, 8 + hr:9 + hr], L3[:, hr:hr + 1]
        P.op("act", lambda e, ls=ls: e.activation(out=XI4[:, :], in_=cf2[:, O_XI:O_XI + 512], func=AF.Exp, scale=ls), reads=[Bcf2, BL3], writes=[BXI])
        P.op("act", lambda e, lf=lf: e.activation(out=MT[:, :], in_=cf2[:, O_DP:O_DP + 128], func=AF.Exp, scale=lf), reads=[Bcf2, BL3], writes=[BMT])
        P.op("act", lambda e, lb=lb: e.activation(out=E2[:, :], in_=cf2[:, O_DN:O_DN + 128], func=AF.Exp, scale=lb), reads=[Bcf2, BL3], writes=[BE2])
        P.op("dve", lambda e: e.tensor_tensor(out=MT[:, :], in0=MT[:, :], in1=cf2[:, O_MF:O_MF + 128], op=ALU.mult), reads=[BMT, Bcf2], writes=[BMT])
        P.op("dve", lambda e: e.tensor_tensor(out=E2[:, :], in0=E2[:, :], in1=cf2[:, O_MB:O_MB + 128], op=ALU.mult), reads=[BE2, Bcf2], writes=[BE2])
        P.op("dve", lambda e: e.tensor_tensor(out=MT[:, :], in0=MT[:, :], in1=E2[:, :], op=ALU.add), reads=[BMT, BE2], writes=[BMT])
        P.op("act", lambda e, ls=ls: e.activation(out=DEC2[:, :], in_=ls, func=AF.Exp, scale=128.0), reads=[BL3], writes=[BDEC2])
        P.op("dve", lambda e: e.tensor_scalar(out=X[:, :], in0=X[:, :], scalar1=flg[:, 1:2], scalar2=None, op0=ALU.mult), reads=[BX, Bflg], writes=[BX])
        for i_ in range(2):
            P.op("dve", lambda e, i_=i_, hr=hr: e.scalar_tensor_tensor(out=Rr[i_][:, :], in0=S0all[:, hr, :], scalar=flg[:, 0:1], in1=X[:, :],
                                                                     op0=ALU.mult, op1=ALU.add), reads=[BS0[hr], Bflg, BX], writes=[BRr[i_]])
        P.op("act", lambda e: e.activation(out=STb[f, 0, :], in_=Rr[0][f, :], func=AF.Copy), reads=[BRr[0]], writes=[BSTb])
        P.op("act", lambda e: e.activation(out=STb[b, nq - 1, :], in_=Rr[1][b, :], func=AF.Copy), reads=[BRr[1]], writes=[BSTb])
        for n in range(nq - 1):
            P.op("dve", lambda e, n=n, hr=hr: e.scalar_tensor_tensor(out=Rr[0][f, :], in0=Rr[0][f, :], scalar=DEC2[f, 0:1], in1=KVall[f, hr, n * 128:(n + 1) * 128],
                                                                   op0=ALU.mult, op1=ALU.add), reads=[BRr[0], BDEC2, BKV[hr]], writes=[BRr[0]])
            P.op("act", lambda e, n=n: e.activation(out=STb[f, n + 1, :], in_=Rr[0][f, :], func=AF.Copy), reads=[BRr[0]], writes=[BSTb])
            m = nq - 1 - n
            P.op("dve", lambda e, m=m, hr=hr: e.scalar_tensor_tensor(out=Rr[1][b, :], in0=Rr[1][b, :], scalar=DEC2[b, 0:1], in1=KVall[b, hr, m * 128:(m + 1) * 128],
                                                                   op0=ALU.mult, op1=ALU.add), reads=[BRr[1], BDEC2, BKV[hr]], writes=[BRr[1]])
            P.op("act", lambda e, m=m: e.activation(out=STb[b, m - 1, :], in_=Rr[1][b, :], func=AF.Copy), reads=[BRr[1]], writes=[BSTb])
        P.op("pool", lambda e: e.memset(STb[f, nq, :], 0.0), writes=[BSTb])
        P.op("pool", lambda e: e.memset(STb[b, nq + 1, :], 0.0), writes=[BSTb])
        P.op("act", lambda e, hr=hr: e.activation(out=STb[f, nq + 1, :], in_=KVall[f, hr, nq * 128:(nq + 1) * 128], func=AF.Copy), reads=[BKV[hr]], writes=[BSTb])
        P.op("act", lambda e, hr=hr: e.activation(out=STb[b, nq, :], in_=KVall[b, hr, (nq + 1) * 128:(nq + 2) * 128], func=AF.Copy), reads=[BKV[hr]], writes=[BSTb])
        for grp in cgroups:
            g0, ng = grp[0], len(grp)
            qq = oq % 2
            oq += 1
            po, Bpo = pout[qq], Bpout[qq]
            qx, Bqx = QXg[qq], BQXg[qq]
            P.op("dve", lambda e, qx=qx, g0=g0, ng=ng, q=q: e.tensor_tensor(out=qx[:, 0:ng * 128], in0=qd[q][:, g0 * 128:(g0 + ng) * 128],
                                                                           in1=XI4[:, 0:ng * 128], op=ALU.mult), reads=[Bqd[q], BXI], writes=[Bqx])
            for i, n in enumerate(grp):
                s_ = scq % 2
                scq += 1
                P.op("pe", lambda e, n=n, s_=s_, q=q: e.matmul(psc[s_][:, 0:128], lhsT=kt2[q][0:64, n * 128:(n + 1) * 128], rhs=qd[q][0:64, n * 128:(n + 1) * 128],
                                                             start=True, stop=True), reads=[Bkt2[q], Bqd[q]], writes=[Bpsc[s_]])
                P.op("dve", lambda e, s_=s_: e.tensor_tensor(out=Wt[s_][:, :], in0=psc[s_][:, 0:128], in1=MT[:, :], op=ALU.mult),
                     reads=[Bpsc[s_], BMT], writes=[BWt[s_]])
                P.op("pe", lambda e, i=i, n=n, s_=s_, po=po, q=q: e.matmul(po[:, i * 128:(i + 1) * 128], lhsT=Wt[s_][:, :], rhs=vr2[q][:, n, :],
                                                                         start=True, stop=False), reads=[BWt[s_], Bvr2[q]], writes=[Bpo])
                P.op("pe", lambda e, i=i, n=n, qx=qx, po=po: e.matmul(po[:, i * 128:(i + 1) * 128], lhsT=qx[:, i * 128:(i + 1) * 128], rhs=STb[:, n, :],
                                                                    start=False, stop=True), reads=[Bqx, BSTb], writes=[Bpo])
            gs_, Bgs_ = gst[qq], Bgst[qq]
            for i in range(ng):
                P.op("act", lambda e, i=i, po=po, gs_=gs_: e.activation(out=junk[:, :], in_=po[:, i * 128:(i + 1) * 128], func=AF.Identity,
                                                                      accum_out=gs_[:, i:i + 1]), reads=[Bpo], writes=[Bjunk, Bgs_])
                P.op("act", lambda e, i=i, po=po, gs_=gs_: e.activation(out=junk[:, :], in_=po[:, i * 128:(i + 1) * 128], func=AF.Square,
                                                                      accum_out=gs_[:, 4 + i:5 + i]), reads=[Bpo], writes=[Bjunk, Bgs_])
            P.op("dve", lambda e, gs_=gs_, ng=ng: e.tensor_scalar_mul(out=gs_[:, 8:8 + ng], in0=gs_[:, 0:ng], scalar1=1.0 / 128), reads=[Bgs_], writes=[Bgs_])
            P.op("dve", lambda e, gs_=gs_, ng=ng: e.tensor_tensor(out=gs_[:, 12:12 + ng], in0=gs_[:, 8:8 + ng], in1=gs_[:, 8:8 + ng], op=ALU.mult),
                 reads=[Bgs_], writes=[Bgs_])
            P.op("dve", lambda e, gs_=gs_, ng=ng: e.scalar_tensor_tensor(out=gs_[:, 16:16 + ng], in0=gs_[:, 4:4 + ng], scalar=1.0 / 128,
                                                                        in1=gs_[:, 12:12 + ng], op0=ALU.mult, op1=ALU.subtract), reads=[Bgs_], writes=[Bgs_])
            rstd_from_ss(cx, gs_[:, 16:16 + ng], gs_[:, 20:20 + ng], Bgs_, Bgs_, 1e-5, gs_[:, 12:12 + ng])
            gt, Bgt = ggt[qq], Bggt[qq]
            P.dma("sp", lambda e, gt=gt, g0=g0, ng=ng, hr=hr: e.dma_start(out=gt[:, 0:ng, :], in_=GG[g0 * 128:(g0 + ng) * 128, hr * 128:(hr + 1) * 128].rearrange("(n p) d -> p n d", p=128)),
                  Bgt, writes=[Bgt])
            for i in range(ng):
                P.op("dve", lambda e, i=i, po=po, gs_=gs_, qq=qq: e.tensor_scalar(out=tn[qq][:, i, :], in0=po[:, i * 128:(i + 1) * 128],
                                                                                scalar1=gs_[:, 8 + i:9 + i], scalar2=gs_[:, 20 + i:21 + i],
                                                                                op0=ALU.subtract, op1=ALU.mult), reads=[Bpo, Bgs_], writes=[Btn[qq]])
            P.op("pool", lambda e, qq=qq, gt=gt, ng=ng: e.tensor_tensor(out=mixr[qq][:, 0:ng, :], in0=tn[qq][:, 0:ng, :], in1=gt[:, 0:ng, :], op=ALU.mult),
                 reads=[Btn[qq], Bgt], writes=[Bmixr[qq]])
            Bo = P.buf()
            P.dma("sp", lambda e, qq=qq, g0=g0, ng=ng, hr=hr: e.dma_start(
                out=MIX[g0 * 128:(g0 + ng) * 128, 512 + hr * 128:512 + (hr + 1) * 128].rearrange("(n p) d -> p n d", p=128),
                in_=mixr[qq][:, 0:ng, :]), Bmixr[qq], reads=[Bmixr[qq]], writes=[Bo])
    for bb in BKV + BS0 + [BL3]:
        bb.keep = False
    cx.fence()


def build_model(depth, Tl, L, ncores, stop=99):
    cx = Ctx()
    nc, P = cx.nc, cx.P
    T = Tl + L
    tile_sets = [0] * (Tl // 128) + [1] * (L // 128)
    x_in = cx.din("x_in", [T, D])
    cT = cx.din("cT", [128, 8, 2])
    ada_w = cx.din("ada_w", [depth, D, 9 * D])
    ada_b = cx.din("ada_b", [depth, 1, 9 * D])
    gpre = cx.din("gpre", [depth, 1, 3 * D])
    gpost = cx.din("gpost", [depth, 1, 3 * D])
    f1wi = cx.din("f1wi", [depth, D, 2 * DFF])
    f1wo = cx.din("f1wo", [depth, DFF, D])
    f2wi = cx.din("f2wi", [depth, D, 2 * DFF])
    f2wo = cx.din("f2wo", [depth, DFF, D])
    wext = cx.din("wext", [depth, D, NCX])
    w_out = cx.din("w_out", [depth, D, D])
    tab = cx.din("tab", [6, 128, T])
    gn = cx.din("gn", [depth, 1, 512])
    dec3 = cx.din("dec3", [depth, 128, 3, 4])
    sinkb = cx.din("sinkb", [depth, 128, 8])
    cf = cx.din("cf", [128, CF_W])
    tri = cx.din("tri", [128, 4, 512], BF16)
    flags = cx.din("flags", [128, 2])
    y = cx.dout("y", [T, D])
    hd = cx.dscr("hd", [T, D])
    modv = cx.dscr("modv", [2, 9, D])
    FM = cx.dscr("FM", [NFM, 128, T], BF16)
    VA = cx.dscr("VA", [T, 128], BF16)
    VR = cx.dscr("VR", [T, 512], BF16)
    GG = cx.dscr("GG", [T, 512], F32)
    KTOK = cx.dscr("KTOK", [T, 256], BF16)
    MIX = cx.dscr("MIX", [T, D], BF16)
    EXPS = cx.dscr("EXPS", [512, 128], F32)
    EXPB = cx.dscr("EXPB", [512, 128], BF16)
    GATS = cx.dscr("GATS", [1024, 128], F32)
    GATB = cx.dscr("GATB", [1024, 128], BF16)
    groups_cc = [[2 * i, 2 * i + 1] for i in range(ncores // 2)]
    def conv(name, src, rows_per):
        R_, C_ = src.shape
        dst = cx.dscr(name, [R_, C_], BF16)
        bb = P.buf(name, keep=True)
        bb.nofence = True
        for r0 in range(0, R_, rows_per):
            r1 = min(R_, r0 + rows_per)
            P.dma("pool", lambda e, r0=r0, r1=r1: e.dma_start(out=dst[r0:r1, :], in_=src[r0:r1, :]), bb, writes=[bb])
        return dst, bb

    def conv_layer(l):
        return dict(aw=conv(f"c_aw{l}", ada_w[l], 256), f1wi=conv(f"c_f1wi{l}", f1wi[l], 256), f1wo=conv(f"c_f1wo{l}", f1wo[l], 704),
                    wext=conv(f"c_wext{l}", wext[l], 512), wout=conv(f"c_wout{l}", w_out[l], 1024),
                    f2wi=conv(f"c_f2wi{l}", f2wi[l], 256), f2wo=conv(f"c_f2wo{l}", f2wo[l], 704))

    cw = {0: conv_layer(0)}
    for l in range(depth):
        w = cw[l]
        st_mod(cx, cT, w["aw"][0], ada_b[l], gpre[l], gpost[l], modv, w["aw"][1])
        st_ffn(cx, x_in if l == 0 else hd, hd, w["f1wi"][0], w["f1wo"][0], modv, 0, tile_sets, w["f1wi"][1], w["f1wo"][1])
        if l + 1 < depth:
            cw[l + 1] = conv_layer(l + 1)
        st_s2(cx, hd, w["wext"][0], modv, tab, gn[l], FM, VA, VR, GG, KTOK, tile_sets, w["wext"][1])
        st_s3(cx, l, Tl, FM, VA, VR, GG, KTOK, MIX, dec3[l], sinkb[l], cf, tri, flags, EXPS, EXPB, GATS, GATB, groups_cc)
        st_out(cx, hd, MIX, w["wout"][0], modv, tile_sets, w["wout"][1])
        st_ffn(cx, hd, y if l == depth - 1 else hd, w["f2wi"][0], w["f2wo"][0], modv, 2, tile_sets, w["f2wi"][1], w["f2wo"][1])
    cx.nsem = P.emit()
    return cx


_PROGS = {}


def _wext_index():
    permA = np.array([d + 16 if (d % 32) < 16 else d - 16 for d in range(64)])
    permR = np.array([d + 32 if d < 32 else d - 32 for d in range(64)])
    idx = []

    def pair(base, perm):
        idx.append(base + np.arange(128))
        idx.append(base + np.concatenate([perm, 64 + perm]))
    for c in range(4):
        pair(c * 128, permA)
    pair(512, permA)
    for c in range(2):
        pair(768 + c * 128, permR)
    for c in range(2):
        pair(1024 + c * 128, permR)
    idx.append(np.arange(640, 768))
    idx.append(np.arange(1280, 1792))
    idx.append(np.arange(1792, 2304))
    return np.concatenate(idx)


def _rope_tables(S, L):
    f32 = np.float32
    pos = np.arange(S)
    row = (pos // 64).astype(f32)
    col = (pos % 64).astype(f32)
    inv16 = (f32(10000.0) ** (-np.arange(16, dtype=f32) / f32(16))).astype(f32)
    ang_row = row[:, None] * inv16[None, :]
    ang_col = col[:, None] * inv16[None, :]
    invR = (f32(10000.0) ** (-np.linspace(0.0, 1.0, 32, dtype=f32))).astype(f32)
    ang_ret = pos.astype(f32)[:, None] * invR[None, :]
    d = np.arange(64)
    angA = np.where((d < 32)[None, :], ang_row[:, d % 16], ang_col[:, d % 16]).astype(f32)
    sgnA = np.where((d % 32) < 16, -1.0, 1.0).astype(f32)
    angR = ang_ret[:, d % 32].astype(f32)
    sgnR = np.where(d < 32, -1.0, 1.0).astype(f32)
    tab = np.zeros((6, 128, S + L), f32)
    for hh in range(2):
        sl = slice(hh * 64, (hh + 1) * 64)
        tab[0, sl, :S] = np.cos(angA).T
        tab[1, sl, :S] = (np.sin(angA) * sgnA[None, :]).T
        tab[2, sl, :S] = np.cos(angR).T
        tab[3, sl, :S] = (np.sin(angR) * sgnR[None, :]).T
    tab[0, :, S:] = 1.0
    tab[2, :, S:] = 1.0
    tab[4] = tab[2] * f32(0.125)
    tab[5] = tab[3] * f32(0.125)
    return tab


def _s3_consts():
    f32 = np.float32
    cf = np.zeros((128, CF_W), f32)
    s = np.arange(128)[:, None]
    q = np.arange(128)[None, :]
    cf[:, 0:128] = np.maximum(q - s, 0)
    cf[:, 128:256] = np.maximum(s - q, 0)
    cf[:, 256:384] = (q >= s)
    cf[:, 384:512] = (q < s)
    i = np.arange(128)
    xi = np.zeros((128, 128), f32)
    xi[0:64, :] = (i + 1)[None, :]
    xi[64:128, :] = (128 - i)[None, :]
    cf[:, 640:1152] = np.tile(xi, (1, 4))
    cf[:, 1152] = 127 - i
    cf[:, 1153] = i
    for c in range(2):
        cf[:, 1154 + 2 * c] = 255 - (c * 128 + i)
        cf[:, 1154 + 2 * c + 1] = c * 128 + i
    t1 = np.tile((s >= q).astype(f32), (1, 4))
    t2 = np.tile((s <= q).astype(f32), (1, 4))
    return cf, t1, t2


def kernel(x, c, ctx, c_ctx, ada_w, ada_b, norm_pre, norm_post, ffn1_wi, ffn1_wo, ffn2_wi, ffn2_wo,
           w_in, w_out, attn_sink, ret_decay_fwd, ret_decay_bwd, ret_gn):
    import ml_dtypes
    f32 = np.float32
    x = np.asarray(x, f32)
    B, S, _ = x.shape
    L = ctx.shape[1]
    depth = ada_w.shape[0]
    ncores = 2 * B
    Tl = S // 2
    import os
    key = (depth, Tl, L, ncores)
    if key not in _PROGS:
        _PROGS[key] = build_model(*key, stop=int(os.environ.get("KSTOP", "99")))
    cx = _PROGS[key]
    widx = _wext_index()
    tab_full = _rope_tables(S, L)
    cf, t1, t2 = _s3_consts()
    A = lambda a: np.ascontiguousarray(np.asarray(a, f32))
    shared = dict(
        ada_w=A(ada_w), ada_b=A(ada_b).reshape(depth, 1, -1), gpre=A(norm_pre).reshape(depth, 1, -1), gpost=A(norm_post).reshape(depth, 1, -1),
        f1wi=A(ffn1_wi), f1wo=A(ffn1_wo), f2wi=A(ffn2_wi), f2wo=A(ffn2_wo),
        wext=np.ascontiguousarray(A(w_in)[:, :, widx]), w_out=A(w_out), gn=A(ret_gn).reshape(depth, 1, -1), cf=cf)
    df, db = A(ret_decay_fwd), A(ret_decay_bwd)
    dec3 = np.zeros((depth, 128, 3, 4), f32)
    dec3[:, 0:64, 0, :] = df[:, None, :]
    dec3[:, 64:128, 0, :] = db[:, None, :]
    dec3[:, :, 1, :] = df[:, None, :]
    dec3[:, :, 2, :] = db[:, None, :]
    shared["dec3"] = dec3
    shared["sinkb"] = np.ascontiguousarray(np.broadcast_to(A(attn_sink)[:, None, :], (depth, 128, 8)))
    maps = []
    for i in range(ncores):
        b, half = i // 2, i % 2
        o = half * Tl
        m = dict(shared)
        m["x_in"] = np.concatenate([x[b, o:o + Tl], A(ctx[b])], 0)
        cv = np.stack([A(c[b]), A(c_ctx)], 0)
        m["cT"] = np.ascontiguousarray(cv.reshape(2, 8, 128).transpose(2, 1, 0))
        m["tab"] = np.ascontiguousarray(np.concatenate([tab_full[:, :, o:o + Tl], tab_full[:, :, S:]], 2))
        has_left, has_right = float(half == 1), float(half == 0)
        m["tri"] = np.stack([t1, t2, t1 * has_left, t2 * has_right], 1).astype(ml_dtypes.bfloat16)
        fl = np.zeros((128, 2), f32)
        fl[0:64, 0], fl[0:64, 1] = (1.0, 0.0) if half == 0 else (0.0, 1.0)
        fl[64:128, 0], fl[64:128, 1] = (0.0, 1.0) if half == 0 else (1.0, 0.0)
        m["flags"] = fl
        maps.append(m)
    res = run_bass_kernel_spmd(cx.nc, maps, core_ids=list(range(ncores)))
    out = np.empty((B, S, D), f32)
    for i in range(ncores):
        b, half = i // 2, i % 2
        out[b, half * Tl:(half + 1) * Tl] = res.results[i]["y"][:Tl]
    return out
```

```python
import numpy as np
import concourse.bass as bass
import concourse.mybir as mybir
from concourse.bass_utils import run_bass_kernel_spmd

F32 = mybir.dt.float32
BF16 = mybir.dt.bfloat16
AF = mybir.ActivationFunctionType
ALU = mybir.AluOpType

D = 1024
DFF = 2816
NFM = 9
NCX = 18 * 128 + 1152
PAIR_TAB = [0, 0, 0, 0, 0, 1, 1, 2, 2]
ENGS = ("pe", "act", "dve", "pool", "sp")


class Buf:
    __slots__ = ("name", "w", "rs", "dsem", "dcount", "keep")

    def __init__(self, name):
        self.name = name
        self.w = None
        self.rs = []
        self.dsem = None
        self.dcount = 0
        self.keep = False


class Op:
    __slots__ = ("eng", "fn", "waits", "sig", "val", "dma", "snap", "key", "pos")


class Prog:
    def __init__(self, nc):
        self.nc = nc
        self.ops = {e: [] for e in ENGS}
        self.known = {e: {} for e in ENGS}
        self.snapver = {e: None for e in ENGS}
        self.epoch = 0
        self.ecount = {e: 0 for e in ENGS}
        self.free_dsems = []
        self.ndsem = 0
        self.stage_bufs = []

    def buf(self, name=None, keep=False):
        b = Buf(name)
        b.keep = keep
        self.stage_bufs.append(b)
        return b

    def bufs(self, n, name="b"):
        return [self.buf(name) for _ in range(n)]

    def _record(self, eng, fn, reads, writes, dma_buf=None, inc=16):
        op = Op()
        op.eng = eng
        op.fn = fn
        op.sig = False
        op.val = None
        op.dma = dma_buf
        op.waits = []
        known = self.known[eng]
        deps = []
        for b in reads:
            if b.w is not None:
                deps.append((b.w, "raw"))
        for b in writes:
            if b.w is not None:
                deps.append((b.w, "waw"))
            for r in b.rs:
                deps.append((r, "war"))
        changed = False
        for d, kind in deps:
            if d.dma is None and d.eng == eng:
                if eng in ("pe", "sp") or kind == "war":
                    continue
            if known.get(d.key, -1) >= d.pos:
                continue
            op.waits.append(d)
            d.sig = True
            known[d.key] = d.pos
            for k, v in d.snap.items():
                if known.get(k, -1) < v:
                    known[k] = v
            changed = True
        if changed or self.snapver[eng] is None:
            self.snapver[eng] = dict(known)
        op.snap = self.snapver[eng]
        self.ops[eng].append(op)
        if dma_buf is not None:
            if dma_buf.dsem is None:
                if self.free_dsems:
                    idx, base = self.free_dsems.pop()
                else:
                    idx, base = self.ndsem, 0
                    self.ndsem += 1
                dma_buf.dsem = ("dma", idx)
                dma_buf.dcount = base
            dma_buf.dcount += inc
            op.key = dma_buf.dsem
            op.pos = dma_buf.dcount
            op.val = inc
        else:
            op.key = (eng, self.epoch)
            op.pos = self.ecount[eng]
            self.ecount[eng] += 1
        if fn is not None:
            for b in reads:
                b.rs.append(op)
            for b in writes:
                b.w = op
                b.rs = []
        return op

    def op(self, eng, fn, reads=(), writes=()):
        return self._record(eng, fn, list(reads), list(writes))

    def dma(self, queue, fn, sb, reads=(), writes=(), inc=16):
        return self._record(queue, fn, list(reads), list(writes), dma_buf=sb, inc=inc)

    def barrier(self, fence_fn, fence_buf):
        allb = [b for b in self.stage_bufs]
        self._record("sp", fence_fn, [], allb + [fence_buf], dma_buf=fence_buf)
        for e in ENGS:
            self._record(e, None, [fence_buf], [])
        keep = []
        for b in self.stage_bufs:
            if b.keep:
                keep.append(b)
            elif b.dsem is not None and b is not fence_buf:
                self.free_dsems.append((b.dsem[1], b.dcount))
                b.dsem = None
        self.stage_bufs = keep
        self.epoch += 1
        for e in ENGS:
            self.ecount[e] = 0

    def emit(self):
        nc = self.nc
        cnt = {}
        for e in ENGS:
            for op in self.ops[e]:
                if op.dma is None and op.sig:
                    cnt[op.key] = cnt.get(op.key, 0) + 1
                    op.val = cnt[op.key]
        esem = {k: nc.alloc_semaphore(name=f"s_{k[0]}_{k[1]}") for k in cnt}
        dsem = [nc.alloc_semaphore(name=f"d_{i}") for i in range(self.ndsem)]

        def run(e, eng):
            for op in self.ops[e]:
                for d in op.waits:
                    if d.dma is not None:
                        eng.wait_ge(dsem[d.key[1]], d.pos)
                    else:
                        eng.wait_ge(esem[d.key], d.val)
                if op.fn is None:
                    continue
                ins = op.fn(eng)
                if op.dma is not None:
                    ins.then_inc(dsem[op.key[1]], op.val)
                elif op.sig:
                    ins.then_inc(esem[op.key], 1)

        with nc.Block() as block:
            @block.tensor
            def _(eng):
                run("pe", eng)

            @block.scalar
            def _(eng):
                run("act", eng)

            @block.vector
            def _(eng):
                run("dve", eng)

            @block.gpsimd
            def _(eng):
                run("pool", eng)

            @block.sync
            def _(eng):
                run("sp", eng)
        return len(esem) + len(dsem)


ARENA_ELEMS = 100000


class Ctx:
    def __init__(self):
        self.nc = bass.Bass("TRN2", target_bir_lowering=False)
        self.P = Prog(self.nc)
        nc, P = self.nc, self.P
        self.arena = nc.alloc_sbuf_tensor("arena", [128, ARENA_ELEMS], BF16).ap()
        self.off = 0
        self.pbank = [None] + [nc.alloc_psum_tensor(f"pb{i}", [128, 512], F32).ap() for i in range(1, 7)] + [None]
        self.pbf = {0: nc.alloc_psum_tensor("pbf0", [128, 1024], BF16).ap(), 7: nc.alloc_psum_tensor("pbf7", [128, 1024], BF16).ap()}
        self.nd = 0
        self.identf = nc.alloc_sbuf_tensor("identf", [128, 128], F32).ap()
        self.ident = nc.alloc_sbuf_tensor("ident", [128, 128], BF16).ap()
        self.cst = nc.alloc_sbuf_tensor("cst", [128, 2], F32).ap()
        self.fsb = nc.alloc_sbuf_tensor("fsb", [1, 16], F32).ap()
        self.fdr = nc.dram_tensor("fence_d", [1, 16], F32, kind="Internal").ap()
        self.Bid = P.buf("ident", keep=True)
        self.Bc = P.buf("cst", keep=True)
        self.Bf = P.buf("fence", keep=True)
        identf, ident, cst, fsb = self.identf, self.ident, self.cst, self.fsb
        P.op("pool", lambda e: e.memset(identf[:, :], 0.0), writes=[self.Bid])
        P.op("pool", lambda e: e.affine_select(out=identf[:, :], in_=identf[:, :], pattern=[[-1, 128]], compare_op=ALU.not_equal,
                                                fill=1.0, base=0, channel_multiplier=1), reads=[self.Bid], writes=[self.Bid])
        P.op("dve", lambda e: e.tensor_copy(out=ident[:, :], in_=identf[:, :]), reads=[self.Bid], writes=[self.Bid])
        P.op("pool", lambda e: e.memset(cst[:, 0:1], -0.5), writes=[self.Bc])
        P.op("pool", lambda e: e.memset(fsb[:, :], 0.0), writes=[self.Bf])

    def alloc(self, shape, dt):
        n = 1
        for s in shape[1:]:
            n *= s
        nb = n * (4 if dt == F32 else 2)
        ne = (nb + 31) // 32 * 16
        assert self.off + ne <= ARENA_ELEMS, ("arena overflow", self.off, ne)
        ap = self.arena[0:shape[0], self.off:self.off + nb // 2]
        self.off += ne
        if dt == F32:
            ap = ap.bitcast(F32)
        if len(shape) == 3:
            ap = ap.rearrange("p (a b) -> p a b", b=shape[2])
        elif len(shape) == 4:
            ap = ap.rearrange("p (a b c) -> p a b c", b=shape[2], c=shape[3])
        return ap

    def mark(self):
        return self.off

    def reset(self, to=0):
        self.off = to

    def psum_bf16(self, i):
        return self.pbf[i]

    def din(self, name, shape, dt=F32):
        return self.nc.dram_tensor(name, list(shape), dt, kind="ExternalInput").ap()

    def dout(self, name, shape, dt=F32):
        return self.nc.dram_tensor(name, list(shape), dt, kind="ExternalOutput").ap()

    def dscr(self, name, shape, dt=F32):
        return self.nc.dram_tensor(name, list(shape), dt, kind="Internal").ap()

    def fence(self):
        fsb, fdr = self.fsb, self.fdr
        self.P.barrier(lambda e: e.dma_start(out=fdr[:, :], in_=fsb[:, :]), self.Bf)


def rstd_from_ss(cx, ss_ap, out_ap, bss, bout, eps, tmp_ap):
    P = cx.P
    cst, bcst = cx.cst, cx.Bc
    P.op("dve", lambda e: e.tensor_scalar_add(out=tmp_ap, in0=ss_ap, scalar1=eps), reads=[bss], writes=[bout])
    n = ss_ap.shape[1]
    P.op("pool", lambda e: e.tensor_tensor(out=out_ap, in0=tmp_ap, in1=cst[:, 0:1].to_broadcast([128, n]), op=ALU.pow),
         reads=[bout, bcst], writes=[bout])


def load_mod(cx, modv, s, nset, ab, G, Bab, BG):
    P = cx.P
    ld = cx.alloc([8, nset * 2, 128], F32)
    Bld, Bpp = P.buf(), P.buf()
    pp = cx.pbank[1]
    for st_ in range(nset):
        for w in range(2):
            P.dma("sp", lambda e, st_=st_, w=w: e.dma_start(out=ld[:, st_ * 2 + w, :], in_=modv[st_, 3 * s + w, :].rearrange("(k p) -> k p", p=128)),
                  Bld, writes=[Bld])
    for i in range(nset * 2):
        P.op("pe", lambda e, i=i: e.transpose(out=pp[:, i * 8:(i + 1) * 8], in_=ld[:, i, :], identity=cx.identf[0:8, 0:8]),
             reads=[Bld, cx.Bid], writes=[Bpp])
    P.op("dve", lambda e: e.tensor_copy(out=ab[:, :, :, :], in_=pp[:, 0:nset * 16].rearrange("p (a b c) -> p a b c", b=2, c=8)),
         reads=[Bpp], writes=[Bab])
    for st_ in range(nset):
        if G is not None:
            P.dma("sp", lambda e, st_=st_: e.dma_start(out=G[:, st_, :], in_=modv[st_, 3 * s + 2:3 * s + 3, :].partition_broadcast(128)),
                  BG, writes=[BG])


def st_mod(cx, cT, aw, abias, gpre, gpost, modv):
    P = cx.P
    cx.reset()
    GW = 1536
    cs = cx.alloc([128, 8, 2], F32)
    sc = cx.alloc([128, 8, 2], BF16)
    wch = [cx.alloc([128, 8, GW], BF16) for _ in range(2)]
    mod = cx.alloc([2, 9 * D], F32)
    bia = cx.alloc([2, 9 * D], F32)
    gp = cx.alloc([2, 3 * D], F32)
    gq = cx.alloc([2, 3 * D], F32)
    outv = cx.alloc([2, 9, D], F32)
    pm = [cx.pbank[1], cx.pbank[2]]
    Bcs, Bsc, Bmod, Bbia, Bgp, Bgq, Bout = [P.buf() for _ in range(7)]
    Bw, Bpm = P.bufs(2), P.bufs(2)
    P.dma("sp", lambda e: e.dma_start(out=cs[:, :, :], in_=cT[:, :, :]), Bcs, writes=[Bcs])
    P.dma("sp", lambda e: e.dma_start(out=bia[:, :], in_=abias.partition_broadcast(2)), Bbia, writes=[Bbia])
    P.dma("sp", lambda e: e.dma_start(out=gp[:, :], in_=gpre.partition_broadcast(2)), Bgp, writes=[Bgp])
    P.dma("sp", lambda e: e.dma_start(out=gq[:, :], in_=gpost.partition_broadcast(2)), Bgq, writes=[Bgq])
    P.op("act", lambda e: e.activation(out=sc[:, :, :], in_=cs[:, :, :], func=AF.Silu), reads=[Bcs], writes=[Bsc])
    ci = 0
    for g in range(9 * D // GW):
        w, Bwg = wch[g % 2], Bw[g % 2]
        for k in range(8):
            P.dma("pool", lambda e, w=w, k=k, g=g: e.dma_start(out=w[:, k, :], in_=aw[k * 128:(k + 1) * 128, g * GW:(g + 1) * GW]),
                  Bwg, writes=[Bwg])
        for n in range(GW // 512):
            p, Bp = pm[ci % 2], Bpm[ci % 2]
            ci += 1
            for k in range(8):
                P.op("pe", lambda e, w=w, k=k, n=n, p=p: e.matmul(p[0:2, :], lhsT=sc[:, k, :], rhs=w[:, k, n * 512:(n + 1) * 512],
                                                                start=(k == 0), stop=(k == 7)), reads=[Bsc, Bwg], writes=[Bp])
            c0 = g * GW + n * 512
            P.op("dve", lambda e, p=p, c0=c0: e.tensor_tensor(out=mod[:, c0:c0 + 512], in0=p[0:2, :], in1=bia[:, c0:c0 + 512], op=ALU.add),
                 reads=[Bp, Bbia], writes=[Bmod])
    for s in range(3):
        coef = 1.0 if s == 1 else 0.5
        sh, scl, gt = mod[:, (3 * s) * D:(3 * s + 1) * D], mod[:, (3 * s + 1) * D:(3 * s + 2) * D], mod[:, (3 * s + 2) * D:(3 * s + 3) * D]
        P.op("dve", lambda e, s=s, scl=scl: e.scalar_tensor_tensor(out=outv[:, 3 * s, :], in0=scl, scalar=1.0, in1=gp[:, s * D:(s + 1) * D],
                                                                  op0=ALU.add, op1=ALU.mult), reads=[Bmod, Bgp], writes=[Bout])
        P.op("dve", lambda e, s=s, sh=sh: e.tensor_copy(out=outv[:, 3 * s + 1, :], in_=sh), reads=[Bmod], writes=[Bout])
        P.op("dve", lambda e, s=s, gt=gt, coef=coef: e.scalar_tensor_tensor(out=outv[:, 3 * s + 2, :], in0=gt, scalar=coef,
                                                                          in1=gq[:, s * D:(s + 1) * D], op0=ALU.mult, op1=ALU.mult),
             reads=[Bmod, Bgq], writes=[Bout])
    Bo = P.buf()
    P.dma("sp", lambda e: e.dma_start(out=modv[:, :, :], in_=outv[:, :, :]), Bout, reads=[Bout], writes=[Bo])
    cx.fence()


def prenorm_tile(cx, x, t, s, h, Bh, junk, Bjunk, st, Bst, xn, Bxn, pT, BpT, ab, Bab, uTp, BuTp, j):
    P = cx.P
    ident, Bid = cx.ident, cx.Bid
    P.dma("sp", lambda e: e.dma_start(out=h[:, :], in_=x[t * 128:(t + 1) * 128, :]), Bh, writes=[Bh])
    P.op("act", lambda e: e.activation(out=junk[:, :], in_=h[:, :], func=AF.Square, scale=1.0 / 32, accum_out=st[:, 0:1]),
         reads=[Bh], writes=[Bjunk, Bst])
    rstd_from_ss(cx, st[:, 0:1], st[:, 2:3], Bst, Bst, 1e-6, st[:, 1:2])
    P.op("dve", lambda e: e.tensor_scalar(out=xn[:, :], in0=h[:, :], scalar1=st[:, 2:3], scalar2=None, op0=ALU.mult),
         reads=[Bh, Bst], writes=[Bxn])
    for k in range(8):
        P.op("pe", lambda e, k=k: e.transpose(out=pT[:, k * 128:(k + 1) * 128], in_=xn[:, k * 128:(k + 1) * 128], identity=ident[:, :]),
             reads=[Bxn, Bid], writes=[BpT])
    for k in range(8):
        if k % 2 == 0:
            P.op("act", lambda e, k=k: e.activation(out=uTp[:, k, j * 128:(j + 1) * 128], in_=pT[:, k * 128:(k + 1) * 128], func=AF.Identity,
                                                    scale=ab[:, s, 0, k:k + 1], bias=ab[:, s, 1, k:k + 1]), reads=[BpT, Bab], writes=[BuTp])
        else:
            P.op("dve", lambda e, k=k: e.tensor_scalar(out=uTp[:, k, j * 128:(j + 1) * 128], in0=pT[:, k * 128:(k + 1) * 128],
                                                       scalar1=ab[:, s, 0, k:k + 1], scalar2=ab[:, s, 1, k:k + 1], op0=ALU.mult, op1=ALU.add),
                 reads=[BpT, Bab], writes=[BuTp])


def postnorm_residual(cx, pys, s3, Bs3, G, BG, s, t1, Bt1, h, Bh, junk, Bjunk):
    P = cx.P
    for half in range(2):
        py, Bp = pys[half]
        P.op("act", lambda e, py=py, half=half: e.activation(out=junk[:, 0:512], in_=py[:, :], func=AF.Square, scale=1.0 / 32,
                                                            accum_out=s3[:, 4 + half:5 + half]), reads=[Bp], writes=[Bjunk, Bs3])
    P.op("dve", lambda e: e.tensor_tensor(out=s3[:, 6:7], in0=s3[:, 4:5], in1=s3[:, 5:6], op=ALU.add), reads=[Bs3], writes=[Bs3])
    rstd_from_ss(cx, s3[:, 6:7], s3[:, 7:8], Bs3, Bs3, 1e-6, s3[:, 3:4])
    for half in range(2):
        py, Bp = pys[half]
        P.op("dve", lambda e, py=py, half=half: e.scalar_tensor_tensor(out=t1[:, half * 512:(half + 1) * 512], in0=py[:, :], scalar=s3[:, 7:8],
                                                                      in1=G[:, s, half * 512:(half + 1) * 512], op0=ALU.mult, op1=ALU.mult),
             reads=[Bp, Bs3, BG], writes=[Bt1])
    P.op("pool", lambda e: e.tensor_tensor(out=h[:, :], in0=h[:, :], in1=t1[:, :], op=ALU.add), reads=[Bh, Bt1], writes=[Bh])


def st_ffn(cx, x, y, wi, wo, modv, s, tile_sets):
    P = cx.P
    cx.reset()
    nt = len(tile_sets)
    nset = 2
    wib = cx.alloc([128, 8, 2 * DFF], BF16)
    wob = cx.alloc([128, 22, D], BF16)
    ab = cx.alloc([128, nset, 2, 8], F32)
    G = cx.alloc([128, nset, D], F32)
    hb = [cx.alloc([128, D], F32) for _ in range(4)]
    xn = [cx.alloc([128, D], BF16) for _ in range(2)]
    junk = cx.alloc([128, D], BF16)
    t1 = cx.alloc([128, D], F32)
    uT = [cx.alloc([128, 8, 256], BF16) for _ in range(2)]
    gT = cx.alloc([128, 22, 256], BF16)
    sil = [cx.alloc([128, 256], F32) for _ in range(2)]
    st = [cx.alloc([128, 8], F32) for _ in range(4)]
    pT = cx.psum_bf16(0)
    pab = [cx.pbank[1 + i] for i in range(4)]
    pY = [cx.pbank[5], cx.pbank[6], cx.pbank[5]]
    Bwib, Bwob, Bab, BG = P.buf(), P.buf(), P.buf(), P.buf()
    Bhb, Bxn, Bt1, Bsil, Bst = P.bufs(4), P.bufs(2), P.buf(), P.bufs(2), P.bufs(4)
    Bjunk, BuT, BgT, BpT = P.buf(), P.bufs(2), P.buf(), P.buf()
    Bpab, BpY = P.bufs(4), P.bufs(2)
    BpY = [BpY[0], BpY[1], BpY[0]]
    import os
    kq = int(os.environ.get("KSUB", "9"))
    if kq != -1 and kq != 0 and kq != -3:
        load_mod(cx, modv, s, nset, ab, G, Bab, BG)
    if kq != -2 and kq != 0:
        for k in range(8):
            P.dma("pool", lambda e, k=k: e.dma_start(out=wib[:, k, :], in_=wi[k * 128:(k + 1) * 128, :]), Bwib, writes=[Bwib])
    if kq != -2 and kq != 0 and kq != -3:
        wo_v = wo.rearrange("(k p) n -> p k n", p=128)
        for k0 in range(0, 22, 6):
            k1 = min(22, k0 + 6)
            P.dma("pool", lambda e, k0=k0, k1=k1: e.dma_start(out=wob[:, k0:k1, :], in_=wo_v[:, k0:k1, :]), Bwob, writes=[Bwob])
    blocks = [list(range(b0, min(nt, b0 + 2))) for b0 in range(0, nt, 2)]
    yctr = [0]

    def phase1(bi):
        par = bi % 2
        for j, t in enumerate(blocks[bi]):
            prenorm_tile(cx, x, t, tile_sets[t], hb[par * 2 + j], Bhb[par * 2 + j], junk, Bjunk, st[j], Bst[j], xn[j], Bxn[j],
                         pT, BpT, ab, Bab, uT[par], BuT[par], j)

    def phase2(bi):
        par = bi % 2
        N = len(blocks[bi]) * 128
        u = uT[par]
        for m in range(22):
            pa, pb = pab[(m % 2) * 2], pab[(m % 2) * 2 + 1]
            Ba, Bb = Bpab[(m % 2) * 2], Bpab[(m % 2) * 2 + 1]
            for k in range(8):
                P.op("pe", lambda e, k=k, m=m, pa=pa: e.matmul(pa[:, 0:N], lhsT=wib[:, k, m * 128:(m + 1) * 128], rhs=u[:, k, 0:N],
                                                             start=(k == 0), stop=(k == 7)), reads=[Bwib, BuT[par]], writes=[Ba])
            for k in range(8):
                P.op("pe", lambda e, k=k, m=m, pb=pb: e.matmul(pb[:, 0:N], lhsT=wib[:, k, DFF + m * 128:DFF + (m + 1) * 128], rhs=u[:, k, 0:N],
                                                             start=(k == 0), stop=(k == 7)), reads=[Bwib, BuT[par]], writes=[Bb])
            sl, Bs = sil[m % 2], Bsil[m % 2]
            P.op("act", lambda e, pa=pa, sl=sl: e.activation(out=sl[:, 0:N], in_=pa[:, 0:N], func=AF.Silu), reads=[Ba], writes=[Bs])
            P.op("dve", lambda e, pb=pb, sl=sl, m=m: e.tensor_tensor(out=gT[:, m, 0:N], in0=sl[:, 0:N], in1=pb[:, 0:N], op=ALU.mult),
                 reads=[Bs, Bb], writes=[BgT])

    def phase3(bi):
        par = bi % 2
        for j, t in enumerate(blocks[bi]):
            h, Bh = hb[par * 2 + j], Bhb[par * 2 + j]
            pys = []
            for half in range(2):
                py, Bp = pY[yctr[0] % 2], BpY[yctr[0] % 2]
                yctr[0] += 1
                pys.append((py, Bp))
                for k in range(22):
                    P.op("pe", lambda e, k=k, j=j, half=half, py=py: e.matmul(py[:, :], lhsT=gT[:, k, j * 128:(j + 1) * 128],
                                                                            rhs=wob[:, k, half * 512:(half + 1) * 512],
                                                                            start=(k == 0), stop=(k == 21)), reads=[BgT, Bwob], writes=[Bp])
            postnorm_residual(cx, pys, st[2 + j], Bst[2 + j], G, BG, tile_sets[t], t1, Bt1, h, Bh, junk, Bjunk)
            Bo = P.buf()
            P.dma("sp", lambda e, h=h, t=t: e.dma_start(out=y[t * 128:(t + 1) * 128, :], in_=h[:, :]), Bh, reads=[Bh], writes=[Bo])

    nb = len(blocks)
    import os
    ksub = int(os.environ.get("KSUB", "9"))
    if ksub >= 2:
        phase1(0)
    if ksub >= 3:
        phase2(0)
    if ksub >= 4:
        for bi in range(nb):
            if bi > 0:
                phase2(bi)
            if bi + 1 < nb:
                phase1(bi + 1)
            phase3(bi)
    cx.fence()


def st_out(cx, hd, MIX, wo, modv, tile_sets):
    P = cx.P
    cx.reset()
    nt = len(tile_sets)
    nset = 2
    ident, Bid = cx.ident, cx.Bid
    wob = cx.alloc([128, 8, D], BF16)
    ab = cx.alloc([128, nset, 2, 8], F32)
    G = cx.alloc([128, nset, D], F32)
    hb = [cx.alloc([128, D], F32) for _ in range(3)]
    mx = [cx.alloc([128, D], BF16) for _ in range(3)]
    mt = [cx.alloc([128, 8, 128], BF16) for _ in range(2)]
    t1 = cx.alloc([128, D], F32)
    junk = cx.alloc([128, 512], BF16)
    st = [cx.alloc([128, 8], F32) for _ in range(2)]
    pT = [cx.psum_bf16(0), cx.psum_bf16(7)]
    pY = [cx.pbank[1 + i] for i in range(4)]
    Bwob, Bab, BG, Bjunk, Bt1 = P.buf(), P.buf(), P.buf(), P.buf(), P.buf()
    Bhb, Bmx, Bmt, Bst, BpT, BpY = P.bufs(3), P.bufs(3), P.bufs(2), P.bufs(2), P.bufs(2), P.bufs(4)
    load_mod(cx, modv, 1, nset, ab, G, Bab, BG)
    P.dma("pool", lambda e: e.dma_start(out=wob[:, :, :], in_=wo.rearrange("(k p) n -> p k n", p=128)), Bwob, writes=[Bwob])

    def load(t):
        P.dma("sp", lambda e: e.dma_start(out=hb[t % 3][:, :], in_=hd[t * 128:(t + 1) * 128, :]), Bhb[t % 3], writes=[Bhb[t % 3]])
        P.dma("sp", lambda e: e.dma_start(out=mx[t % 3][:, :], in_=MIX[t * 128:(t + 1) * 128, :]), Bmx[t % 3], writes=[Bmx[t % 3]])

    load(0)
    if nt > 1:
        load(1)
    for t in range(nt):
        if t + 2 < nt:
            load(t + 2)
        s = tile_sets[t]
        h, Bh = hb[t % 3], Bhb[t % 3]
        m, Bm = mt[t % 2], Bmt[t % 2]
        p_t, Bp_t = pT[t % 2], BpT[t % 2]
        for k in range(8):
            P.op("pe", lambda e, k=k, t=t, p_t=p_t: e.transpose(out=p_t[:, k * 128:(k + 1) * 128], in_=mx[t % 3][:, k * 128:(k + 1) * 128],
                                                              identity=ident[:, :]), reads=[Bmx[t % 3], Bid], writes=[Bp_t])
        P.op("act", lambda e, m=m, p_t=p_t: e.activation(out=m[:, :, :], in_=p_t[:, 0:1024].rearrange("p (k n) -> p k n", n=128), func=AF.Copy),
             reads=[Bp_t], writes=[Bm])
        pys = []
        for half in range(2):
            py, Bp = pY[(2 * t + half) % 4], BpY[(2 * t + half) % 4]
            pys.append((py, Bp))
            for k in range(8):
                P.op("pe", lambda e, k=k, half=half, py=py, m=m: e.matmul(py[:, :], lhsT=m[:, k, :], rhs=wob[:, k, half * 512:(half + 1) * 512],
                                                                        start=(k == 0), stop=(k == 7)), reads=[Bm, Bwob], writes=[Bp])
        postnorm_residual(cx, pys, st[t % 2], Bst[t % 2], G, BG, s, t1, Bt1, h, Bh, junk, Bjunk)
        Bo = P.buf()
        P.dma("sp", lambda e, h=h, t=t: e.dma_start(out=hd[t * 128:(t + 1) * 128, :], in_=h[:, :]), Bh, reads=[Bh], writes=[Bo])
    cx.fence()


def st_s2(cx, hd, wext, modv, tab, gn, FM, VA, VR, GG, KTOK, tile_sets):
    P = cx.P
    cx.reset()
    nt = len(tile_sets)
    nset = 2
    ident, Bid = cx.ident, cx.Bid
    wb = cx.alloc([128, 8, NCX], BF16)
    ab = cx.alloc([128, nset, 2, 8], F32)
    gnb = cx.alloc([128, 512], F32)
    hb = [cx.alloc([128, D], F32) for _ in range(4)]
    xn = [cx.alloc([128, D], BF16) for _ in range(2)]
    junk = cx.alloc([128, D], BF16)
    uT = [cx.alloc([128, 8, 256], BF16) for _ in range(2)]
    tb = [cx.alloc([128, 6, 256], F32) for _ in range(2)]
    r1 = [cx.alloc([128, 256], F32) for _ in range(2)]
    r2 = [cx.alloc([128, 256], F32) for _ in range(2)]
    fmo = [cx.alloc([128, 256], BF16) for _ in range(3)]
    kto = [cx.alloc([128, 256], BF16) for _ in range(2)]
    vao = [cx.alloc([128, 128], BF16) for _ in range(2)]
    vro = [cx.alloc([128, 512], BF16) for _ in range(2)]
    gs = [cx.alloc([128, 512], F32) for _ in range(2)]
    ggo = [cx.alloc([128, 512], F32) for _ in range(2)]
    st = [cx.alloc([128, 8], F32) for _ in range(2)]
    pT = cx.psum_bf16(0)
    pxp = [cx.pbank[1 + i] for i in range(4)]
    ptm = [cx.pbank[5 + i] for i in range(2)]
    pkt = cx.psum_bf16(7)
    Bwb, Bab, Bgn = P.buf(), P.buf(), P.buf()
    Bhb, Bxn, BuT, Btb = P.bufs(4), P.bufs(2), P.bufs(2), P.bufs(2)
    Br1, Br2, Bfmo, Bkto, Bvao, Bvro, Bgs, Bggo, Bst = P.bufs(2), P.bufs(2), P.bufs(3), P.bufs(2), P.bufs(2), P.bufs(2), P.bufs(2), P.bufs(2), P.bufs(2)
    Bjunk, BpT, Bpkt = P.buf(), P.buf(), P.buf()
    Bpxp, Bptm = P.bufs(4), P.bufs(2)
    load_mod(cx, modv, 1, nset, ab, None, Bab, None)
    P.dma("sp", lambda e: e.dma_start(out=gnb[:, :], in_=gn.partition_broadcast(128)), Bgn, writes=[Bgn])
    for k in range(8):
        P.dma("pool", lambda e, k=k: e.dma_start(out=wb[:, k, :], in_=wext[k * 128:(k + 1) * 128, :]), Bwb, writes=[Bwb])
    blocks = [list(range(b0, min(nt, b0 + 2))) for b0 in range(0, nt, 2)]
    ctr = {"fm": 0, "tm": 0, "o": 0, "k": 0}

    def phase1(bi):
        par = bi % 2
        tiles = blocks[bi]
        for j, t in enumerate(tiles):
            prenorm_tile(cx, hd, t, tile_sets[t], hb[par * 2 + j], Bhb[par * 2 + j], junk, Bjunk, st[j], Bst[j], xn[j], Bxn[j],
                         pT, BpT, ab, Bab, uT[par], BuT[par], j)
        t0 = tiles[0] * 128
        N = len(tiles) * 128
        P.dma("sp", lambda e: e.dma_start(out=tb[par][:, :, 0:N], in_=tab[:, :, t0:t0 + N].rearrange("s p t -> p s t")),
              Btb[par], writes=[Btb[par]])

    def phase2(bi):
        tiles = blocks[bi]
        par = bi % 2
        N = len(tiles) * 128
        t0 = tiles[0] * 128
        u = uT[par]
        for i in range(NFM):
            q = ctr["fm"] % 2
            ctr["fm"] += 1
            px, pp = pxp[q * 2], pxp[q * 2 + 1]
            Bx, Bp = Bpxp[q * 2], Bpxp[q * 2 + 1]
            for k in range(8):
                P.op("pe", lambda e, k=k, i=i, px=px: e.matmul(px[:, 0:N], lhsT=wb[:, k, (2 * i) * 128:(2 * i + 1) * 128], rhs=u[:, k, 0:N],
                                                             start=(k == 0), stop=(k == 7)), reads=[Bwb, BuT[par]], writes=[Bx])
            for k in range(8):
                P.op("pe", lambda e, k=k, i=i, pp=pp: e.matmul(pp[:, 0:N], lhsT=wb[:, k, (2 * i + 1) * 128:(2 * i + 2) * 128], rhs=u[:, k, 0:N],
                                                             start=(k == 0), stop=(k == 7)), reads=[Bwb, BuT[par]], writes=[Bp])
            tp = PAIR_TAB[i]
            P.op("dve", lambda e, px=px, q=q, tp=tp: e.tensor_tensor(out=r1[q][:, 0:N], in0=px[:, 0:N], in1=tb[par][:, 2 * tp, 0:N], op=ALU.mult),
                 reads=[Bx, Btb[par]], writes=[Br1[q]])
            P.op("dve", lambda e, pp=pp, q=q, tp=tp: e.tensor_tensor(out=r2[q][:, 0:N], in0=pp[:, 0:N], in1=tb[par][:, 2 * tp + 1, 0:N], op=ALU.mult),
                 reads=[Bp, Btb[par]], writes=[Br2[q]])
            o = ctr["o"] % 3
            ctr["o"] += 1
            P.op("pool", lambda e, q=q, o=o: e.tensor_tensor(out=fmo[o][:, 0:N], in0=r1[q][:, 0:N], in1=r2[q][:, 0:N], op=ALU.add),
                 reads=[Br1[q], Br2[q]], writes=[Bfmo[o]])
            Bo = P.buf()
            P.dma("sp", lambda e, o=o, i=i: e.dma_start(out=FM[i, :, t0:t0 + N], in_=fmo[o][:, 0:N]), Bfmo[o], reads=[Bfmo[o]], writes=[Bo])
            if i >= 7:
                kq = ctr["k"] % 2
                ctr["k"] += 1
                for j in range(len(tiles)):
                    P.op("pe", lambda e, o=o, j=j: e.transpose(out=pkt[:, j * 128:(j + 1) * 128], in_=fmo[o][:, j * 128:(j + 1) * 128],
                                                              identity=ident[:, :]), reads=[Bfmo[o], Bid], writes=[Bpkt])
                P.op("act", lambda e, kq=kq: e.activation(out=kto[kq][:, 0:N], in_=pkt[:, 0:N], func=AF.Copy), reads=[Bpkt], writes=[Bkto[kq]])
                Bo = P.buf()
                c0k = (i - 7) * 128
                P.dma("sp", lambda e, kq=kq, c0k=c0k: e.dma_start(out=KTOK[t0:t0 + N, c0k:c0k + 128].rearrange("(j p) d -> p j d", p=128),
                                                                 in_=kto[kq][:, 0:N].rearrange("p (j d) -> p j d", d=128)),
                      Bkto[kq], reads=[Bkto[kq]], writes=[Bo])
        c0 = 18 * 128
        for j, t in enumerate(tiles):
            def tm(cols, ncol, j=j):
                q = ctr["tm"] % 2
                ctr["tm"] += 1
                p, Bp_ = ptm[q], Bptm[q]
                for k in range(8):
                    P.op("pe", lambda e, k=k, p=p, j=j: e.matmul(p[:, 0:ncol], lhsT=u[:, k, j * 128:(j + 1) * 128], rhs=wb[:, k, cols:cols + ncol],
                                                               start=(k == 0), stop=(k == 7)), reads=[Bwb, BuT[par]], writes=[Bp_])
                return p, Bp_
            r0 = t * 128
            p, Bp_ = tm(c0, 128)
            P.op("act", lambda e, p=p, j=j: e.activation(out=vao[j][:, :], in_=p[:, 0:128], func=AF.Copy), reads=[Bp_], writes=[Bvao[j]])
            Bo = P.buf()
            P.dma("sp", lambda e, j=j, r0=r0: e.dma_start(out=VA[r0:r0 + 128, :], in_=vao[j][:, :]), Bvao[j], reads=[Bvao[j]], writes=[Bo])
            p, Bp_ = tm(c0 + 128, 512)
            P.op("act", lambda e, p=p, j=j: e.activation(out=vro[j][:, :], in_=p[:, :], func=AF.Copy), reads=[Bp_], writes=[Bvro[j]])
            Bo = P.buf()
            P.dma("sp", lambda e, j=j, r0=r0: e.dma_start(out=VR[r0:r0 + 128, :], in_=vro[j][:, :]), Bvro[j], reads=[Bvro[j]], writes=[Bo])
            p, Bp_ = tm(c0 + 640, 512)
            P.op("act", lambda e, p=p, j=j: e.activation(out=gs[j][:, :], in_=p[:, :], func=AF.Silu), reads=[Bp_], writes=[Bgs[j]])
            P.op("dve", lambda e, j=j: e.tensor_tensor(out=ggo[j][:, :], in0=gs[j][:, :], in1=gnb[:, :], op=ALU.mult),
                 reads=[Bgs[j], Bgn], writes=[Bggo[j]])
            Bo = P.buf()
            P.dma("sp", lambda e, j=j, r0=r0: e.dma_start(out=GG[r0:r0 + 128, :], in_=ggo[j][:, :]), Bggo[j], reads=[Bggo[j]], writes=[Bo])

    nb = len(blocks)
    phase1(0)
    for bi in range(nb):
        if bi + 1 < nb:
            phase1(bi + 1)
        phase2(bi)
    cx.fence()


O_DP, O_DN, O_MF, O_MB, O_XI, O_ZE, O_WC, CF_W = 0, 128, 256, 384, 640, 1152, 1154, 1158


def st_s3(cx, l_idx, Tl, FM, VA, VR, GG, KTOK, MIX, dec3_d, sink_d, cf_d, tri_d, flags_d, EXPS, EXPB, GATS, GATB, groups_cc, sub=9):
    P = cx.P
    nq = Tl // 128
    NCH = nq + 2
    T = NCH * 128
    TH = T + 256
    f, b = slice(0, 64), slice(64, 128)
    cgroups = [list(range(g0, min(g0 + 4, nq))) for g0 in range(0, nq, 4)] + [[nq, nq + 1]]

    cx.reset()
    KVall = cx.alloc([128, 4, NCH * 128], F32)
    S0all = cx.alloc([128, 4, 128], F32)
    L3 = cx.alloc([128, 12], F32)
    mark = cx.mark()
    BKV = [P.buf(keep=True) for _ in range(4)]
    BS0 = [P.buf(keep=True) for _ in range(4)]
    BL3 = P.buf(keep=True)
    cf = cx.alloc([128, CF_W], F32)
    dec3 = cx.alloc([128, 12], F32)
    ltmp = cx.alloc([128, 12], F32)
    ktok = [cx.alloc([128, NCH, 64], BF16) for _ in range(2)]
    vr = [cx.alloc([128, NCH, 128], BF16) for _ in range(2)]
    ZETA = [cx.alloc([128, 2], F32) for _ in range(2)]
    DEC = [cx.alloc([128, 1], F32) for _ in range(2)]
    WC = [cx.alloc([128, 2, 2], F32) for _ in range(2)]
    KZg = [cx.alloc([128, 4, 128], BF16) for _ in range(2)]
    KZc = [cx.alloc([128, 2, 128], BF16) for _ in range(2)]
    R = [cx.alloc([128, 128], F32) for _ in range(2)]
    Bcf, Bdec = P.buf(), P.buf()
    Bktok, Bvr, BZ, BDEC, BWC, BKZg, BKZc, BR = [P.bufs(2) for _ in range(8)]
    pk = [cx.pbank[1], cx.pbank[2]]
    p0 = [cx.pbank[3], cx.pbank[4]]
    Bpk, Bp0 = P.bufs(2), P.bufs(2)
    Bexs, Bexb, Bgs, Bgb = P.buf(), P.buf(), P.buf(), P.buf()

    P.dma("sp", lambda e: e.dma_start(out=cf[:, :], in_=cf_d[:, :]), Bcf, writes=[Bcf])
    P.dma("sp", lambda e: e.dma_start(out=dec3[:, :], in_=dec3_d.rearrange("p a b -> p (a b)")), Bdec, writes=[Bdec])
    P.op("act", lambda e: e.activation(out=ltmp[:, :], in_=dec3[:, :], func=AF.Exp, scale=-1.0), reads=[Bdec], writes=[BL3])
    P.op("dve", lambda e: e.tensor_scalar_add(out=ltmp[:, :], in0=ltmp[:, :], scalar1=1.0), reads=[BL3], writes=[BL3])
    P.op("act", lambda e: e.activation(out=L3[:, :], in_=ltmp[:, :], func=AF.Ln), reads=[BL3], writes=[BL3])
    P.op("dve", lambda e: e.tensor_scalar_mul(out=L3[:, :], in0=L3[:, :], scalar1=-1.0), reads=[BL3], writes=[BL3])
    import os
    ksa = int(os.environ.get("KSA", "0"))
    if ksa < 2:
        for i_, (src) in enumerate([FM[4, :, 0:128], FM[4, :, Tl - 128:Tl], VA[0:128, :], VA[Tl - 128:Tl, :]]):
            P.dma("sp", lambda e, i_=i_, src=src: e.dma_start(out=EXPB[i_ * 128:(i_ + 1) * 128, :], in_=src), Bexb, writes=[Bexb])
    kvq = 0
    wcv = cf[:, O_WC:O_WC + 4].rearrange("p (c d) -> p c d", d=2)
    for hr in (range(4) if ksa != 3 else []):
        q = hr % 2
        kt_, vr_ = ktok[q], vr[q]
        P.dma("sp", lambda e, hr=hr, kt_=kt_: e.dma_start(out=kt_[:, :, :], in_=KTOK[:, hr * 64:(hr + 1) * 64].rearrange("(n p) d -> p n d", p=128)),
              Bktok[q], writes=[Bktok[q]])
        P.dma("sp", lambda e, hr=hr, vr_=vr_: e.dma_start(out=vr_[:, :, :], in_=VR[:, hr * 128:(hr + 1) * 128].rearrange("(n p) d -> p n d", p=128)),
              Bvr[q], writes=[Bvr[q]])
        lf, lb, ls = L3[:, 4 + hr:5 + hr], L3[:, 8 + hr:9 + hr], L3[:, hr:hr + 1]
        P.op("act", lambda e, q=q, lf=lf: e.activation(out=ZETA[q][:, 0:1], in_=cf[:, O_ZE:O_ZE + 1], func=AF.Exp, scale=lf), reads=[Bcf, BL3], writes=[BZ[q]])
        P.op("act", lambda e, q=q, lb=lb: e.activation(out=ZETA[q][:, 1:2], in_=cf[:, O_ZE + 1:O_ZE + 2], func=AF.Exp, scale=lb), reads=[Bcf, BL3], writes=[BZ[q]])
        P.op("act", lambda e, q=q, ls=ls: e.activation(out=DEC[q][:, :], in_=ls, func=AF.Exp, scale=128.0), reads=[BL3], writes=[BDEC[q]])
        P.op("act", lambda e, q=q, lf=lf: e.activation(out=WC[q][:, :, 0], in_=wcv[:, :, 0], func=AF.Exp, scale=lf), reads=[Bcf, BL3], writes=[BWC[q]])
        P.op("act", lambda e, q=q, lb=lb: e.activation(out=WC[q][:, :, 1], in_=wcv[:, :, 1], func=AF.Exp, scale=lb), reads=[Bcf, BL3], writes=[BWC[q]])
        if ksa == 6:
            continue
        for c in range(2):
            for d_ in range(2):
                P.op("dve", lambda e, c=c, d_=d_, q=q, kt_=kt_: e.tensor_scalar(out=KZc[q][:, c, d_ * 64:(d_ + 1) * 64], in0=kt_[:, nq + c, :],
                                                                             scalar1=WC[q][:, c, d_:d_ + 1], scalar2=None, op0=ALU.mult),
                     reads=[Bktok[q], BWC[q]], writes=[BKZc[q]])
        for c in range(2):
            P.op("pe", lambda e, c=c, q=q, vr_=vr_: e.matmul(p0[q][:, 0:128], lhsT=KZc[q][:, c, :], rhs=vr_[:, nq + c, :], start=(c == 0), stop=(c == 1)),
                 reads=[BKZc[q], Bvr[q]], writes=[Bp0[q]])
        if ksa == 7:
            continue
        P.op("act", lambda e, q=q, hr=hr: e.activation(out=S0all[:, hr, :], in_=p0[q][:, 0:128], func=AF.Copy), reads=[Bp0[q]], writes=[BS0[hr]])
        P.op("dve", lambda e, q=q, hr=hr: e.tensor_copy(out=R[q][:, :], in_=S0all[:, hr, :]), reads=[BS0[hr]], writes=[BR[q]])
        if ksa == 8:
            continue
        for grp in (cgroups if ksa not in (5, 6, 7, 8) else []):
            g0, ng = grp[0], len(grp)
            kz, Bkz = KZg[kvq % 2], BKZg[kvq % 2]
            pkk, Bpkk = pk[kvq % 2], Bpk[kvq % 2]
            kvq += 1
            for d_ in range(2):
                P.op("dve", lambda e, kz=kz, g0=g0, ng=ng, d_=d_, q=q, kt_=kt_: e.tensor_scalar(out=kz[:, 0:ng, d_ * 64:(d_ + 1) * 64], in0=kt_[:, g0:g0 + ng, :],
                                                                                            scalar1=ZETA[q][:, d_:d_ + 1], scalar2=None, op0=ALU.mult),
                     reads=[Bktok[q], BZ[q]], writes=[Bkz])
            for i, n in enumerate(grp):
                P.op("pe", lambda e, kz=kz, i=i, n=n, pkk=pkk, vr_=vr_: e.matmul(pkk[:, i * 128:(i + 1) * 128], lhsT=kz[:, i, :], rhs=vr_[:, n, :],
                                                                              start=True, stop=True), reads=[Bkz, Bvr[q]], writes=[Bpkk])
            P.op("act", lambda e, pkk=pkk, g0=g0, ng=ng, hr=hr: e.activation(out=KVall[:, hr, g0 * 128:(g0 + ng) * 128], in_=pkk[:, 0:ng * 128], func=AF.Copy),
                 reads=[Bpkk], writes=[BKV[hr]])
        for n in (range(nq) if ksa not in (4, 5, 6) else []):
            P.op("dve", lambda e, n=n, q=q, hr=hr: e.scalar_tensor_tensor(out=R[q][f, :], in0=R[q][f, :], scalar=DEC[q][f, 0:1], in1=KVall[f, hr, n * 128:(n + 1) * 128],
                                                                        op0=ALU.mult, op1=ALU.add), reads=[BR[q], BDEC[q], BKV[hr]], writes=[BR[q]])
            m = nq - 1 - n
            P.op("dve", lambda e, m=m, q=q, hr=hr: e.scalar_tensor_tensor(out=R[q][b, :], in0=R[q][b, :], scalar=DEC[q][b, 0:1], in1=KVall[b, hr, m * 128:(m + 1) * 128],
                                                                        op0=ALU.mult, op1=ALU.add), reads=[BR[q], BDEC[q], BKV[hr]], writes=[BR[q]])
        P.dma("sp", lambda e, q=q, hr=hr: e.dma_start(out=EXPS[hr * 128:(hr + 1) * 128, :], in_=R[q][:, :]), BR[q], reads=[BR[q]], writes=[Bexs])
    if ksa < 1:
        P.dma("pool", lambda e: e.collective_compute("AllGather", ALU.bypass, replica_groups=groups_cc, ins=[EXPS[:, :]], outs=[GATS[:, :]]),
              Bgs, reads=[Bexs], writes=[Bgs], inc=1)
        P.dma("pool", lambda e: e.collective_compute("AllGather", ALU.bypass, replica_groups=groups_cc, ins=[EXPB[:, :]], outs=[GATB[:, :]]),
              Bgb, reads=[Bexb], writes=[Bgb], inc=1)
    cx.fence()
    if sub <= 1:
        return

    cx.reset(mark)
    qa = cx.alloc([128, 4, T], BF16)
    kaP = cx.alloc([128, 4, TH], BF16)
    vaug = cx.alloc([128, NCH + 2, 2, 65], BF16)
    Pt = [cx.alloc([128, 5, 512], BF16) for _ in range(2)]
    tri = cx.alloc([128, 4, 512], BF16)
    sinkb = cx.alloc([128, 8], F32)
    ES = cx.alloc([128, 8], F32)
    den = [cx.alloc([128, 8], F32) for _ in range(2)]
    mixa = [cx.alloc([128, 256], BF16) for _ in range(2)]
    Bqa, Bka, Bva, Btri, Bsink, BES, BpO = [P.buf() for _ in range(7)]
    BPt, Bden, Bmixa = P.bufs(2), P.bufs(2), P.bufs(2)
    pS = [cx.pbank[1 + i] for i in range(5)]
    pOs = [cx.pbank[6], cx.pbank[6]]
    BpS, BpOs = P.bufs(5), P.bufs(1) * 2
    P.dma("sp", lambda e: e.dma_start(out=tri[:, :, :], in_=tri_d[:, :, :]), Btri, writes=[Btri])
    P.dma("sp", lambda e: e.dma_start(out=sinkb[:, :], in_=sink_d[:, :]), Bsink, writes=[Bsink])
    P.op("act", lambda e: e.activation(out=ES[:, :], in_=sinkb[:, :], func=AF.Exp), reads=[Bsink], writes=[BES])
    for c in range(4):
        P.dma("sp", lambda e, c=c: e.dma_start(out=qa[:, c, :], in_=FM[c]), Bqa, writes=[Bqa])
    P.op("pool", lambda e: e.memset(kaP[:, :, :], 0.0), writes=[Bka])
    for kv in range(2):
        for r in range(2):
            v = kv * 2 + r
            ps_ = slice(r * 64, (r + 1) * 64)
            P.dma("sp", lambda e, v=v, ps_=ps_, kv=kv: e.dma_start(out=kaP[ps_, v, 0:T], in_=FM[4, kv * 64:(kv + 1) * 64, :]), Bka, writes=[Bka])
            P.dma("sp", lambda e, v=v, ps_=ps_, kv=kv: e.dma_start(out=kaP[ps_, v, T:T + 128], in_=GATB[128 + kv * 64:128 + (kv + 1) * 64, :]), Bka, writes=[Bka])
            P.dma("sp", lambda e, v=v, ps_=ps_, kv=kv: e.dma_start(out=kaP[ps_, v, T + 128:T + 256], in_=GATB[512 + kv * 64:512 + (kv + 1) * 64, :]), Bka, writes=[Bka])
    P.op("pool", lambda e: e.memset(vaug[:, :, :, 64:65], 1.0), writes=[Bva])
    for kv in range(2):
        P.dma("sp", lambda e, kv=kv: e.dma_start(out=vaug[:, 0:NCH, kv, 0:64], in_=VA[:, kv * 64:(kv + 1) * 64].rearrange("(n p) d -> p n d", p=128)), Bva, writes=[Bva])
        P.dma("sp", lambda e, kv=kv: e.dma_start(out=vaug[:, NCH, kv, 0:64], in_=GATB[384:512, kv * 64:(kv + 1) * 64]), Bva, writes=[Bva])
        P.dma("sp", lambda e, kv=kv: e.dma_start(out=vaug[:, NCH + 1, kv, 0:64], in_=GATB[512 + 256:512 + 384, kv * 64:(kv + 1) * 64]), Bva, writes=[Bva])
    actr = [0]

    def attn_tile(qo, g, keys, out_row):
        i0 = actr[0]
        actr[0] += 1
        pt, Bpt = Pt[i0 % 2], BPt[i0 % 2]
        pO, BpO_ = pOs[i0 % 2], BpOs[i0 % 2]
        nk = len(keys)
        for i, (ko, ci, mk) in enumerate(keys):
            ps, Bps = pS[i], BpS[i]
            for a in range(4):
                c, r = 2 * g + a // 2, a % 2
                P.op("pe", lambda e, ps=ps, a=a, c=c, r=r, ko=ko: e.matmul(ps[:, a * 128:(a + 1) * 128], lhsT=kaP[:, g * 2 + r, ko:ko + 128],
                                                                          rhs=qa[:, c, qo:qo + 128], start=True, stop=True), reads=[Bka, Bqa], writes=[Bps])
            P.op("act", lambda e, ps=ps, i=i: e.activation(out=pt[:, i, :], in_=ps[:, :], func=AF.Exp, scale=0.125), reads=[Bps], writes=[Bpt])
            if mk is not None:
                P.op("dve", lambda e, i=i, mk=mk: e.tensor_tensor(out=pt[:, i, :], in0=pt[:, i, :], in1=tri[:, mk, :], op=ALU.mult),
                     reads=[Bpt, Btri], writes=[Bpt])
        for a in range(4):
            for i, (ko, ci, mk) in enumerate(keys):
                P.op("pe", lambda e, a=a, i=i, ci=ci: e.matmul(pO[:, a * 128:a * 128 + 65], lhsT=pt[:, i, a * 128:(a + 1) * 128], rhs=vaug[:, ci, g, :],
                                                             start=(i == 0), stop=(i == nk - 1)), reads=[Bpt, Bva], writes=[BpO_])
        dn, Bdn = den[i0 % 2], Bden[i0 % 2]
        mo, Bmo = mixa[i0 % 2], Bmixa[i0 % 2]
        pov = pO[:, :].rearrange("p (a e) -> p a e", e=128)
        P.op("dve", lambda e: e.tensor_tensor(out=dn[:, 0:4], in0=pov[:, :, 64], in1=ES[:, g * 4:(g + 1) * 4], op=ALU.add), reads=[BpO_, BES], writes=[Bdn])
        P.op("dve", lambda e: e.reciprocal(out=dn[:, 4:8], in_=dn[:, 0:4]), reads=[Bdn], writes=[Bdn])
        for a in range(4):
            P.op("act", lambda e, a=a: e.activation(out=mo[:, a * 64:(a + 1) * 64], in_=pO[:, a * 128:a * 128 + 64], func=AF.Identity, scale=dn[:, 4 + a:5 + a]),
                 reads=[BpO_, Bdn], writes=[Bmo])
        Bo = P.buf()
        P.dma("sp", lambda e: e.dma_start(out=MIX[out_row:out_row + 128, g * 256:(g + 1) * 256], in_=mo[:, :]), Bmo, reads=[Bmo], writes=[Bo])

    ctxk = [(Tl + c * 128, nq + c, None) for c in range(2)]
    for n in range(nq):
        prev = ((n - 1) * 128, n - 1, 0) if n > 0 else (T, NCH, 2)
        nxt = ((n + 1) * 128, n + 1, 1) if n < nq - 1 else (T + 128, NCH + 1, 3)
        for g in range(2):
            attn_tile(n * 128, g, [prev, (n * 128, n, None), nxt] + ctxk, n * 128)
    for c in range(2):
        for g in range(2):
            attn_tile(Tl + c * 128, g, ctxk, Tl + c * 128)
    cx.fence()
    if sub <= 2:
        return

    cx.reset(mark)
    cf2 = cx.alloc([128, CF_W], F32)
    flg = cx.alloc([128, 2], F32)
    qd = [cx.alloc([128, T], BF16) for _ in range(2)]
    kt2 = [cx.alloc([64, T], BF16) for _ in range(2)]
    vr2 = [cx.alloc([128, NCH, 128], BF16) for _ in range(2)]
    STb = cx.alloc([128, NCH, 128], BF16)
    XI4 = cx.alloc([128, 512], F32)
    MT = cx.alloc([128, 128], F32)
    E2 = cx.alloc([128, 128], F32)
    DEC2 = cx.alloc([128, 1], F32)
    X = cx.alloc([128, 128], F32)
    Rr = [cx.alloc([128, 128], F32) for _ in range(2)]
    Rr2 = [cx.alloc([128, 128], F32) for _ in range(2)]
    QXg = [cx.alloc([128, 512], BF16) for _ in range(2)]
    Wt = [cx.alloc([128, 128], BF16) for _ in range(2)]
    gst = [cx.alloc([128, 24], F32) for _ in range(2)]
    ggt = [cx.alloc([128, 4, 128], F32) for _ in range(2)]
    tn = [cx.alloc([128, 4, 128], F32) for _ in range(2)]
    mixr = [cx.alloc([128, 4, 128], BF16) for _ in range(2)]
    junk = cx.alloc([128, 128], BF16)
    Bcf2, Bflg, BSTb, BXI, BMT, BE2, BDEC2, BX, Bjunk = [P.buf() for _ in range(9)]
    Bqd, Bkt2, Bvr2, BRr, BQXg, BWt, Bgst, Bggt, Btn, Bmixr = [P.bufs(2) for _ in range(10)]
    BRr2 = P.bufs(2)
    psc = [cx.pbank[1], cx.pbank[2]]
    pout = [cx.pbank[3], cx.pbank[4]]
    Bpsc, Bpout = P.bufs(2), P.bufs(2)
    P.dma("sp", lambda e: e.dma_start(out=cf2[:, :], in_=cf_d[:, :]), Bcf2, writes=[Bcf2])
    P.dma("sp", lambda e: e.dma_start(out=flg[:, :], in_=flags_d[:, :]), Bflg, writes=[Bflg])
    scq, oq = 0, 0
    for hr in range(4):
        q = hr % 2
        c, r = hr // 2, hr % 2
        for half in range(2):
            P.dma("sp", lambda e, q=q, c=c, r=r, half=half: e.dma_start(out=qd[q][half * 64:(half + 1) * 64, :], in_=FM[5 + c, r * 64:(r + 1) * 64, :]),
                  Bqd[q], writes=[Bqd[q]])
        P.dma("sp", lambda e, q=q, c=c, r=r: e.dma_start(out=kt2[q][:, :], in_=FM[7 + c, r * 64:(r + 1) * 64, :]), Bkt2[q], writes=[Bkt2[q]])
        P.dma("sp", lambda e, q=q, hr=hr: e.dma_start(out=vr2[q][:, :, :], in_=VR[:, hr * 128:(hr + 1) * 128].rearrange("(n p) d -> p n d", p=128)),
              Bvr2[q], writes=[Bvr2[q]])
        P.dma("sp", lambda e, hr=hr: e.dma_start(out=X[f, :], in_=GATS[hr * 128:hr * 128 + 64, :]), BX, writes=[BX])
        P.dma("sp", lambda e, hr=hr: e.dma_start(out=X[b, :], in_=GATS[512 + hr * 128 + 64:512 + (hr + 1) * 128, :]), BX, writes=[BX])
        lf, lb, ls = L3[:, 4 + hr:5 + hr], L3[:, 8 + hr:9 + hr], L3[:, hr:hr + 1]
        P.op("act", lambda e, ls=ls: e.activation(out=XI4[:, :], in_=cf2[:, O_XI:O_XI + 512], func=AF.Exp, scale=ls), reads=[Bcf2, BL3], writes=[BXI])
        P.op("act", lambda e, lf=lf: e.activation(out=MT[:, :], in_=cf2[:, O_DP:O_DP + 128], func=AF.Exp, scale=lf), reads=[Bcf2, BL3], writes=[BMT])
        P.op("act", lambda e, lb=lb: e.activation(out=E2[:, :], in_=cf2[:, O_DN:O_DN + 128], func=AF.Exp, scale=lb), reads=[Bcf2, BL3], writes=[BE2])
        P.op("dve", lambda e: e.tensor_tensor(out=MT[:, :], in0=MT[:, :], in1=cf2[:, O_MF:O_MF + 128], op=ALU.mult), reads=[BMT, Bcf2], writes=[BMT])
        P.op("dve", lambda e: e.tensor_tensor(out=E2[:, :], in0=E2[:, :], in1=cf2[:, O_MB:O_MB + 128], op=ALU.mult), reads=[BE2, Bcf2], writes=[BE2])
        P.op("dve", lambda e: e.tensor_tensor(out=MT[:, :], in0=MT[:, :], in1=E2[:, :], op=ALU.add), reads=[BMT, BE2], writes=[BMT])
        P.op("act", lambda e, ls=ls: e.activation(out=DEC2[:, :], in_=ls, func=AF.Exp, scale=128.0), reads=[BL3], writes=[BDEC2])
        P.op("dve", lambda e: e.tensor_scalar(out=X[:, :], in0=X[:, :], scalar1=flg[:, 1:2], scalar2=None, op0=ALU.mult), reads=[BX, Bflg], writes=[BX])
        for i_ in range(2):
            P.op("dve", lambda e, i_=i_, hr=hr: e.scalar_tensor_tensor(out=Rr[i_][:, :], in0=S0all[:, hr, :], scalar=flg[:, 0:1], in1=X[:, :],
                                                                     op0=ALU.mult, op1=ALU.add), reads=[BS0[hr], Bflg, BX], writes=[BRr[i_]])
        P.op("act", lambda e: e.activation(out=STb[f, 0, :], in_=Rr[0][f, :], func=AF.Copy), reads=[BRr[0]], writes=[BSTb])
        P.op("act", lambda e: e.activation(out=STb[b, nq - 1, :], in_=Rr[1][b, :], func=AF.Copy), reads=[BRr[1]], writes=[BSTb])
        RF, BRF = [Rr[0], Rr2[0]], [BRr[0], BRr2[0]]
        RB, BRB = [Rr[1], Rr2[1]], [BRr[1], BRr2[1]]
        for n in range(nq - 1):
            sf, df_, Bsf, Bdf = RF[n % 2], RF[(n + 1) % 2], BRF[n % 2], BRF[(n + 1) % 2]
            P.op("dve", lambda e, n=n, hr=hr, sf=sf, df_=df_: e.scalar_tensor_tensor(out=df_[f, :], in0=sf[f, :], scalar=DEC2[f, 0:1], in1=KVall[f, hr, n * 128:(n + 1) * 128],
                                                                                  op0=ALU.mult, op1=ALU.add), reads=[Bsf, BDEC2, BKV[hr]], writes=[Bdf])
            P.op("act", lambda e, n=n, df_=df_: e.activation(out=STb[f, n + 1, :], in_=df_[f, :], func=AF.Copy), reads=[Bdf], writes=[BSTb])
            m = nq - 1 - n
            sb_, db_, Bsb, Bdb = RB[n % 2], RB[(n + 1) % 2], BRB[n % 2], BRB[(n + 1) % 2]
            P.op("dve", lambda e, m=m, hr=hr, sb_=sb_, db_=db_: e.scalar_tensor_tensor(out=db_[b, :], in0=sb_[b, :], scalar=DEC2[b, 0:1], in1=KVall[b, hr, m * 128:(m + 1) * 128],
                                                                                    op0=ALU.mult, op1=ALU.add), reads=[Bsb, BDEC2, BKV[hr]], writes=[Bdb])
            P.op("act", lambda e, m=m, db_=db_: e.activation(out=STb[b, m - 1, :], in_=db_[b, :], func=AF.Copy), reads=[Bdb], writes=[BSTb])
        P.op("pool", lambda e: e.memset(STb[f, nq, :], 0.0), writes=[BSTb])
        P.op("pool", lambda e: e.memset(STb[b, nq + 1, :], 0.0), writes=[BSTb])
        P.op("act", lambda e, hr=hr: e.activation(out=STb[f, nq + 1, :], in_=KVall[f, hr, nq * 128:(nq + 1) * 128], func=AF.Copy), reads=[BKV[hr]], writes=[BSTb])
        P.op("act", lambda e, hr=hr: e.activation(out=STb[b, nq, :], in_=KVall[b, hr, (nq + 1) * 128:(nq + 2) * 128], func=AF.Copy), reads=[BKV[hr]], writes=[BSTb])
        for grp in cgroups:
            g0, ng = grp[0], len(grp)
            qq = oq % 2
            oq += 1
            po, Bpo = pout[qq], Bpout[qq]
            qx, Bqx = QXg[qq], BQXg[qq]
            P.op("dve", lambda e, qx=qx, g0=g0, ng=ng, q=q: e.tensor_tensor(out=qx[:, 0:ng * 128], in0=qd[q][:, g0 * 128:(g0 + ng) * 128],
                                                                           in1=XI4[:, 0:ng * 128], op=ALU.mult), reads=[Bqd[q], BXI], writes=[Bqx])
            for i, n in enumerate(grp):
                s_ = scq % 2
                scq += 1
                P.op("pe", lambda e, n=n, s_=s_, q=q: e.matmul(psc[s_][:, 0:128], lhsT=kt2[q][0:64, n * 128:(n + 1) * 128], rhs=qd[q][0:64, n * 128:(n + 1) * 128],
                                                             start=True, stop=True), reads=[Bkt2[q], Bqd[q]], writes=[Bpsc[s_]])
                P.op("dve", lambda e, s_=s_: e.tensor_tensor(out=Wt[s_][:, :], in0=psc[s_][:, 0:128], in1=MT[:, :], op=ALU.mult),
                     reads=[Bpsc[s_], BMT], writes=[BWt[s_]])
                P.op("pe", lambda e, i=i, n=n, s_=s_, po=po, q=q: e.matmul(po[:, i * 128:(i + 1) * 128], lhsT=Wt[s_][:, :], rhs=vr2[q][:, n, :],
                                                                         start=True, stop=False), reads=[BWt[s_], Bvr2[q]], writes=[Bpo])
                P.op("pe", lambda e, i=i, n=n, qx=qx, po=po: e.matmul(po[:, i * 128:(i + 1) * 128], lhsT=qx[:, i * 128:(i + 1) * 128], rhs=STb[:, n, :],
                                                                    start=False, stop=True), reads=[Bqx, BSTb], writes=[Bpo])
            gs_, Bgs_ = gst[qq], Bgst[qq]
            for i in range(ng):
                P.op("act", lambda e, i=i, po=po, gs_=gs_: e.activation(out=junk[:, :], in_=po[:, i * 128:(i + 1) * 128], func=AF.Identity,
                                                                      accum_out=gs_[:, i:i + 1]), reads=[Bpo], writes=[Bjunk, Bgs_])
                P.op("act", lambda e, i=i, po=po, gs_=gs_: e.activation(out=junk[:, :], in_=po[:, i * 128:(i + 1) * 128], func=AF.Square,
                                                                      accum_out=gs_[:, 4 + i:5 + i]), reads=[Bpo], writes=[Bjunk, Bgs_])
            P.op("dve", lambda e, gs_=gs_, ng=ng: e.tensor_scalar_mul(out=gs_[:, 8:8 + ng], in0=gs_[:, 0:ng], scalar1=1.0 / 128), reads=[Bgs_], writes=[Bgs_])
            P.op("dve", lambda e, gs_=gs_, ng=ng: e.tensor_tensor(out=gs_[:, 12:12 + ng], in0=gs_[:, 8:8 + ng], in1=gs_[:, 8:8 + ng], op=ALU.mult),
                 reads=[Bgs_], writes=[Bgs_])
            P.op("dve", lambda e, gs_=gs_, ng=ng: e.scalar_tensor_tensor(out=gs_[:, 16:16 + ng], in0=gs_[:, 4:4 + ng], scalar=1.0 / 128,
                                                                        in1=gs_[:, 12:12 + ng], op0=ALU.mult, op1=ALU.subtract), reads=[Bgs_], writes=[Bgs_])
            rstd_from_ss(cx, gs_[:, 16:16 + ng], gs_[:, 20:20 + ng], Bgs_, Bgs_, 1e-5, gs_[:, 12:12 + ng])
            gt, Bgt = ggt[qq], Bggt[qq]
            P.dma("sp", lambda e, gt=gt, g0=g0, ng=ng, hr=hr: e.dma_start(out=gt[:, 0:ng, :], in_=GG[g0 * 128:(g0 + ng) * 128, hr * 128:(hr + 1) * 128].rearrange("(n p) d -> p n d", p=128)),
                  Bgt, writes=[Bgt])
            for i in range(ng):
                P.op("dve", lambda e, i=i, po=po, gs_=gs_, qq=qq: e.tensor_scalar(out=tn[qq][:, i, :], in0=po[:, i * 128:(i + 1) * 128],
                                                                                scalar1=gs_[:, 8 + i:9 + i], scalar2=gs_[:, 20 + i:21 + i],
                                                                                op0=ALU.subtract, op1=ALU.mult), reads=[Bpo, Bgs_], writes=[Btn[qq]])
            P.op("pool", lambda e, qq=qq, gt=gt, ng=ng: e.tensor_tensor(out=mixr[qq][:, 0:ng, :], in0=tn[qq][:, 0:ng, :], in1=gt[:, 0:ng, :], op=ALU.mult),
                 reads=[Btn[qq], Bgt], writes=[Bmixr[qq]])
            Bo = P.buf()
            P.dma("sp", lambda e, qq=qq, g0=g0, ng=ng, hr=hr: e.dma_start(
                out=MIX[g0 * 128:(g0 + ng) * 128, 512 + hr * 128:512 + (hr + 1) * 128].rearrange("(n p) d -> p n d", p=128),
                in_=mixr[qq][:, 0:ng, :]), Bmixr[qq], reads=[Bmixr[qq]], writes=[Bo])
    for bb in BKV + BS0 + [BL3]:
        bb.keep = False
    cx.fence()


def build_model(depth, Tl, L, ncores, stop=99):
    cx = Ctx()
    nc, P = cx.nc, cx.P
    T = Tl + L
    tile_sets = [0] * (Tl // 128) + [1] * (L // 128)
    x_in = cx.din("x_in", [T, D])
    cT = cx.din("cT", [128, 8, 2])
    ada_w = cx.din("ada_w", [depth, D, 9 * D])
    ada_b = cx.din("ada_b", [depth, 1, 9 * D])
    gpre = cx.din("gpre", [depth, 1, 3 * D])
    gpost = cx.din("gpost", [depth, 1, 3 * D])
    f1wi = cx.din("f1wi", [depth, D, 2 * DFF])
    f1wo = cx.din("f1wo", [depth, DFF, D])
    f2wi = cx.din("f2wi", [depth, D, 2 * DFF])
    f2wo = cx.din("f2wo", [depth, DFF, D])
    wext = cx.din("wext", [depth, D, NCX])
    w_out = cx.din("w_out", [depth, D, D])
    tab = cx.din("tab", [6, 128, T])
    gn = cx.din("gn", [depth, 1, 512])
    dec3 = cx.din("dec3", [depth, 128, 3, 4])
    sinkb = cx.din("sinkb", [depth, 128, 8])
    cf = cx.din("cf", [128, CF_W])
    tri = cx.din("tri", [128, 4, 512], BF16)
    flags = cx.din("flags", [128, 2])
    y = cx.dout("y", [T, D])
    hd = cx.dscr("hd", [T, D])
    modv = cx.dscr("modv", [2, 9, D])
    FM = cx.dscr("FM", [NFM, 128, T], BF16)
    VA = cx.dscr("VA", [T, 128], BF16)
    VR = cx.dscr("VR", [T, 512], BF16)
    GG = cx.dscr("GG", [T, 512], F32)
    KTOK = cx.dscr("KTOK", [T, 256], BF16)
    MIX = cx.dscr("MIX", [T, D], BF16)
    EXPS = cx.dscr("EXPS", [512, 128], F32)
    EXPB = cx.dscr("EXPB", [512, 128], BF16)
    GATS = cx.dscr("GATS", [1024, 128], F32)
    GATB = cx.dscr("GATB", [1024, 128], BF16)
    groups_cc = [[2 * i, 2 * i + 1] for i in range(ncores // 2)]
    for l in range(depth):
        if stop >= 1:
            st_mod(cx, cT, ada_w[l], ada_b[l], gpre[l], gpost[l], modv)
        if stop >= 2:
            st_ffn(cx, x_in if l == 0 else hd, hd, f1wi[l], f1wo[l], modv, 0, tile_sets)
        if stop >= 3:
            st_s2(cx, hd, wext[l], modv, tab, gn[l], FM, VA, VR, GG, KTOK, tile_sets)
        if stop >= 4:
            st_s3(cx, l, Tl, FM, VA, VR, GG, KTOK, MIX, dec3[l], sinkb[l], cf, tri, flags, EXPS, EXPB, GATS, GATB, groups_cc, sub=stop - 3)
        if stop >= 7:
            st_out(cx, hd, MIX, w_out[l], modv, tile_sets)
        if stop >= 8:
            st_ffn(cx, hd, y if l == depth - 1 else hd, f2wi[l], f2wo[l], modv, 2, tile_sets)
    cx.nsem = P.emit()
    return cx


_PROGS = {}


def _wext_index():
    permA = np.array([d + 16 if (d % 32) < 16 else d - 16 for d in range(64)])
    permR = np.array([d + 32 if d < 32 else d - 32 for d in range(64)])
    idx = []

    def pair(base, perm):
        idx.append(base + np.arange(128))
        idx.append(base + np.concatenate([perm, 64 + perm]))
    for c in range(4):
        pair(c * 128, permA)
    pair(512, permA)
    for c in range(2):
        pair(768 + c * 128, permR)
    for c in range(2):
        pair(1024 + c * 128, permR)
    idx.append(np.arange(640, 768))
    idx.append(np.arange(1280, 1792))
    idx.append(np.arange(1792, 2304))
    return np.concatenate(idx)


def _rope_tables(S, L):
    f32 = np.float32
    pos = np.arange(S)
    row = (pos // 64).astype(f32)
    col = (pos % 64).astype(f32)
    inv16 = (f32(10000.0) ** (-np.arange(16, dtype=f32) / f32(16))).astype(f32)
    ang_row = row[:, None] * inv16[None, :]
    ang_col = col[:, None] * inv16[None, :]
    invR = (f32(10000.0) ** (-np.linspace(0.0, 1.0, 32, dtype=f32))).astype(f32)
    ang_ret = pos.astype(f32)[:, None] * invR[None, :]
    d = np.arange(64)
    angA = np.where((d < 32)[None, :], ang_row[:, d % 16], ang_col[:, d % 16]).astype(f32)
    sgnA = np.where((d % 32) < 16, -1.0, 1.0).astype(f32)
    angR = ang_ret[:, d % 32].astype(f32)
    sgnR = np.where(d < 32, -1.0, 1.0).astype(f32)
    tab = np.zeros((6, 128, S + L), f32)
    for hh in range(2):
        sl = slice(hh * 64, (hh + 1) * 64)
        tab[0, sl, :S] = np.cos(angA).T
        tab[1, sl, :S] = (np.sin(angA) * sgnA[None, :]).T
        tab[2, sl, :S] = np.cos(angR).T
        tab[3, sl, :S] = (np.sin(angR) * sgnR[None, :]).T
    tab[0, :, S:] = 1.0
    tab[2, :, S:] = 1.0
    tab[4] = tab[2] * f32(0.125)
    tab[5] = tab[3] * f32(0.125)
    return tab


def _s3_consts():
    f32 = np.float32
    cf = np.zeros((128, CF_W), f32)
    s = np.arange(128)[:, None]
    q = np.arange(128)[None, :]
    cf[:, 0:128] = np.maximum(q - s, 0)
    cf[:, 128:256] = np.maximum(s - q, 0)
    cf[:, 256:384] = (q >= s)
    cf[:, 384:512] = (q < s)
    i = np.arange(128)
    xi = np.zeros((128, 128), f32)
    xi[0:64, :] = (i + 1)[None, :]
    xi[64:128, :] = (128 - i)[None, :]
    cf[:, 640:1152] = np.tile(xi, (1, 4))
    cf[:, 1152] = 127 - i
    cf[:, 1153] = i
    for c in range(2):
        cf[:, 1154 + 2 * c] = 255 - (c * 128 + i)
        cf[:, 1154 + 2 * c + 1] = c * 128 + i
    t1 = np.tile((s >= q).astype(f32), (1, 4))
    t2 = np.tile((s <= q).astype(f32), (1, 4))
    return cf, t1, t2


def kernel(x, c, ctx, c_ctx, ada_w, ada_b, norm_pre, norm_post, ffn1_wi, ffn1_wo, ffn2_wi, ffn2_wo,
           w_in, w_out, attn_sink, ret_decay_fwd, ret_decay_bwd, ret_gn):
    import ml_dtypes
    f32 = np.float32
    x = np.asarray(x, f32)
    B, S, _ = x.shape
    L = ctx.shape[1]
    depth = ada_w.shape[0]
    ncores = 2 * B
    Tl = S // 2
    import os
    key = (depth, Tl, L, ncores)
    if key not in _PROGS:
        _PROGS[key] = build_model(*key, stop=int(os.environ.get("KSTOP", "99")))
    cx = _PROGS[key]
    widx = _wext_index()
    tab_full = _rope_tables(S, L)
    cf, t1, t2 = _s3_consts()
    A = lambda a: np.ascontiguousarray(np.asarray(a, f32))
    shared = dict(
        ada_w=A(ada_w), ada_b=A(ada_b).reshape(depth, 1, -1), gpre=A(norm_pre).reshape(depth, 1, -1), gpost=A(norm_post).reshape(depth, 1, -1),
        f1wi=A(ffn1_wi), f1wo=A(ffn1_wo), f2wi=A(ffn2_wi), f2wo=A(ffn2_wo),
        wext=np.ascontiguousarray(A(w_in)[:, :, widx]), w_out=A(w_out), gn=A(ret_gn).reshape(depth, 1, -1), cf=cf)
    df, db = A(ret_decay_fwd), A(ret_decay_bwd)
    dec3 = np.zeros((depth, 128, 3, 4), f32)
    dec3[:, 0:64, 0, :] = df[:, None, :]
    dec3[:, 64:128, 0, :] = db[:, None, :]
    dec3[:, :, 1, :] = df[:, None, :]
    dec3[:, :, 2, :] = db[:, None, :]
    shared["dec3"] = dec3
    shared["sinkb"] = np.ascontiguousarray(np.broadcast_to(A(attn_sink)[:, None, :], (depth, 128, 8)))
    maps = []
    for i in range(ncores):
        b, half = i // 2, i % 2
        o = half * Tl
        m = dict(shared)
        m["x_in"] = np.concatenate([x[b, o:o + Tl], A(ctx[b])], 0)
        cv = np.stack([A(c[b]), A(c_ctx)], 0)
        m["cT"] = np.ascontiguousarray(cv.reshape(2, 8, 128).transpose(2, 1, 0))
        m["tab"] = np.ascontiguousarray(np.concatenate([tab_full[:, :, o:o + Tl], tab_full[:, :, S:]], 2))
        has_left, has_right = float(half == 1), float(half == 0)
        m["tri"] = np.stack([t1, t2, t1 * has_left, t2 * has_right], 1).astype(ml_dtypes.bfloat16)
        fl = np.zeros((128, 2), f32)
        fl[0:64, 0], fl[0:64, 1] = (1.0, 0.0) if half == 0 else (0.0, 1.0)
        fl[64:128, 0], fl[64:128, 1] = (0.0, 1.0) if half == 0 else (1.0, 0.0)
        m["flags"] = fl
        maps.append(m)
    res = run_bass_kernel_spmd(cx.nc, maps, core_ids=list(range(ncores)))
    out = np.empty((B, S, D), f32)
    for i in range(ncores):
        b, half = i // 2, i % 2
        out[b, half * Tl:(half + 1) * Tl] = res.results[i]["y"][:Tl]
    return out
```
